# Optimizing a Trainium2 kernel written in Bass

```python
import math
import jax, jax.numpy as jnp
from jax import lax
import numpy as np

D_MODEL = 2048
BATCH = 4
SEQ = 2048
DEPTH = 2
DEC_BATCH = 128
DEC_SEQ = 4
PAST_LEN = 16384
PAGE_SIZE = 128

D_MIX = 2 * D_MODEL
W_A = D_MIX // 2
S5_CH = 16
S5_GROUPS = W_A // S5_CH
S5_P = 64
W_S = D_MIX - W_A
SSD_HEADDIM = 64
SSD_HEADS = W_S // SSD_HEADDIM
SSD_GROUPS = 8
SSD_N = 128
CONV_K = 4
CONV_DIM = W_S + 2 * SSD_GROUPS * SSD_N
IN_COLS = 2 * W_A + CONV_DIM + W_S + SSD_HEADS
SSD_CHUNK = 128
EPS = 1e-5
DT_MIN = 1e-3
DT_MAX = 1e-1
LAMBDA_RE_MAX = -1e-4

kernel_name = "hymba_s5_ssd_hybrid_step"


def rmsnorm(x, w):
    xf = x.astype(jnp.float32)
    y = xf * lax.rsqrt(jnp.mean(xf * xf, axis=-1, keepdims=True) + EPS)
    return (y * w.astype(jnp.float32)).astype(x.dtype)


def s5_branch(u, h0_re, h0_im, lam_re, lam_im, log_step, b_re, b_im, c_re, c_im, d):
    f32 = jnp.float32
    bsz, L, _ = u.shape
    lr = jnp.minimum(lam_re.astype(f32), LAMBDA_RE_MAX)
    li = lam_im.astype(f32)
    step = jnp.exp(log_step.astype(f32))[:, None]
    mag = jnp.exp(lr * step)
    ab_re = mag * jnp.cos(li * step)
    ab_im = mag * jnp.sin(li * step)
    den = lr * lr + li * li
    nr = ab_re - 1.0
    g_re = (nr * lr + ab_im * li) / den
    g_im = (ab_im * lr - nr * li) / den
    br = b_re.astype(f32)
    bi = b_im.astype(f32)
    bb_re = g_re[..., None] * br - g_im[..., None] * bi
    bb_im = g_re[..., None] * bi + g_im[..., None] * br
    uf = u.astype(f32)
    ug = uf.reshape(bsz, L, S5_GROUPS, S5_CH)
    bu_re = jnp.einsum("blgc,gpc->lbgp", ug, bb_re)
    bu_im = jnp.einsum("blgc,gpc->lbgp", ug, bb_im)
    h0r = h0_re.astype(f32)
    h0i = h0_im.astype(f32)
    bu_re = bu_re.at[0].add(ab_re * h0r - ab_im * h0i)
    bu_im = bu_im.at[0].add(ab_re * h0i + ab_im * h0r)
    a_re = jnp.broadcast_to(ab_re, (L, 1, S5_GROUPS, S5_P))
    a_im = jnp.broadcast_to(ab_im, (L, 1, S5_GROUPS, S5_P))

    def combine(e1, e2):
        a1r, a1i, b1r, b1i = e1
        a2r, a2i, b2r, b2i = e2
        return (a2r * a1r - a2i * a1i,
                a2r * a1i + a2i * a1r,
                a2r * b1r - a2i * b1i + b2r,
                a2r * b1i + a2i * b1r + b2i)

    _, _, hr, hi = lax.associative_scan(combine, (a_re, a_im, bu_re, bu_im), axis=0)
    y = (jnp.einsum("lbgp,gcp->blgc", hr, c_re.astype(f32))
         - jnp.einsum("lbgp,gcp->blgc", hi, c_im.astype(f32)))
    y = y.reshape(bsz, L, W_A) + d.astype(f32) * uf
    return y.astype(u.dtype), hr[-1], hi[-1]


def causal_conv(xbc, buf, w, b):
    L = xbc.shape[1]
    xp = jnp.concatenate([buf.astype(xbc.dtype), xbc], axis=1)
    out = b
    for k in range(CONV_K):
        out = out + xp[:, k:k + L] * w[k]
    return jax.nn.silu(out), xp[:, L:]


def segsum(a):
    T = a.shape[-1]
    cs = jnp.cumsum(a, axis=-1)
    diff = cs[..., :, None] - cs[..., None, :]
    mask = jnp.tril(jnp.ones((T, T), dtype=bool))
    return jnp.where(mask, diff, -jnp.inf)


def ssd_scan(x, dt, a, bm, cm, h0):
    f32 = jnp.float32
    bsz, L = x.shape[:2]
    T = math.gcd(L, SSD_CHUNK)
    nc = L // T
    R = SSD_HEADS // SSD_GROUPS
    xd = (x.astype(f32) * dt[..., None]).reshape(bsz, nc, T, SSD_GROUPS, R, SSD_HEADDIM)
    da = (dt * a).reshape(bsz, nc, T, SSD_GROUPS, R).transpose(0, 3, 4, 1, 2)
    bc = bm.astype(f32).reshape(bsz, nc, T, SSD_GROUPS, SSD_N)
    cc = cm.astype(f32).reshape(bsz, nc, T, SSD_GROUPS, SSD_N)
    da_cs = jnp.cumsum(da, axis=-1)
    decay = jnp.exp(segsum(da))
    cb = jnp.einsum("bctgn,bcsgn->bgcts", cc, bc)
    y_diag = jnp.einsum("bgcts,bgrcts,bcsgrp->bctgrp", cb, decay, xd)
    decay_states = jnp.exp(da_cs[..., -1:] - da_cs)
    states = jnp.einsum("bctgn,bgrct,bctgrp->bcgrpn", bc, decay_states, xd)
    h0r = h0.astype(f32).reshape(bsz, 1, SSD_GROUPS, R, SSD_HEADDIM, SSD_N)
    states = jnp.concatenate([h0r, states], axis=1)
    chunk_ends = jnp.pad(da_cs[..., -1], ((0, 0), (0, 0), (0, 0), (1, 0)))
    chunk_decay = jnp.exp(segsum(chunk_ends))
    states = jnp.einsum("bgrzc,bcgrpn->bzgrpn", chunk_decay, states)
    prev_states, final = states[:, :-1], states[:, -1]
    y_off = jnp.einsum("bctgn,bcgrpn,bgrct->bctgrp", cc, prev_states, jnp.exp(da_cs))
    y = (y_diag + y_off).reshape(bsz, L, SSD_HEADS, SSD_HEADDIM)
    return y, final.reshape(bsz, SSD_HEADS, SSD_HEADDIM, SSD_N)


def mixer_layer(x, s5_h_re, s5_h_im, ssd_h, conv_buf,
                norm_w, w_in, lam_re, lam_im, log_step, b_re, b_im, c_re, c_im, s5_d,
                glu_w, glu_b, s5_norm_w, conv_w, conv_b, dt_bias, a_log, ssd_d, ssd_norm_w, w_out):
    bsz, L, _ = x.shape
    f32 = jnp.float32
    h = rmsnorm(x, norm_w)
    proj = h @ w_in
    u_a, z_a, xbc, z_s, dt_raw = jnp.split(
        proj, [W_A, 2 * W_A, 2 * W_A + CONV_DIM, 2 * W_A + CONV_DIM + W_S], axis=-1)

    y_a, s5_re, s5_im = s5_branch(u_a, s5_h_re, s5_h_im, lam_re, lam_im, log_step,
                                  b_re, b_im, c_re, c_im, s5_d)
    g = jax.nn.gelu(y_a)
    y_a = g * jax.nn.sigmoid(g @ glu_w + glu_b)
    y_a = rmsnorm(y_a * jax.nn.silu(z_a), s5_norm_w).astype(x.dtype)

    xbc, conv_new = causal_conv(xbc, conv_buf, conv_w, conv_b)
    xs, bm, cm = jnp.split(xbc, [W_S, W_S + SSD_GROUPS * SSD_N], axis=-1)
    dt = jax.nn.softplus(dt_raw.astype(f32) + dt_bias.astype(f32))
    a = -jnp.exp(a_log.astype(f32))
    xs_h = xs.reshape(bsz, L, SSD_HEADS, SSD_HEADDIM)
    y_s, ssd_new = ssd_scan(xs_h, dt, a,
                            bm.reshape(bsz, L, SSD_GROUPS, SSD_N),
                            cm.reshape(bsz, L, SSD_GROUPS, SSD_N), ssd_h)
    y_s = y_s + ssd_d.astype(f32)[:, None] * xs_h.astype(f32)
    y_s = y_s.reshape(bsz, L, W_S) * jax.nn.silu(z_s.astype(f32))
    y_s = rmsnorm(y_s, ssd_norm_w).astype(x.dtype)

    out = jnp.concatenate([y_a, y_s], axis=-1) @ w_out
    sd = s5_h_re.dtype
    return (x + out, s5_re.astype(sd), s5_im.astype(sd),
            ssd_new.astype(ssd_h.dtype), conv_new.astype(conv_buf.dtype))


def setup_inputs(seed: int = 0) -> dict:
    key = jax.random.key(seed)
    ks = iter(jax.random.split(key, 40))
    f32 = jnp.float32

    def nrm(shape, s):
        return jax.random.normal(next(ks), shape, f32) * s

    x_prompt = nrm((BATCH, SEQ, D_MODEL), 1.0)
    x_sample = nrm((DEC_BATCH, DEC_SEQ, D_MODEL), 1.0)
    state_s5_re = nrm((DEPTH, DEC_BATCH, S5_GROUPS, S5_P), 0.3)
    state_s5_im = nrm((DEPTH, DEC_BATCH, S5_GROUPS, S5_P), 0.3)
    state_ssd = nrm((DEPTH, DEC_BATCH, SSD_HEADS, SSD_HEADDIM, SSD_N), 0.3)
    cache_conv = nrm((DEPTH, DEC_BATCH, CONV_K - 1, CONV_DIM), 1.0)

    norm_w = 1.0 + nrm((DEPTH, D_MODEL), 0.02)
    w_in = nrm((DEPTH, D_MODEL, IN_COLS), D_MODEL ** -0.5)
    n_idx = jnp.arange(S5_P, dtype=f32)
    s5_lambda_re = -0.5 + nrm((DEPTH, S5_GROUPS, S5_P), 0.01)
    s5_lambda_im = math.pi * n_idx + nrm((DEPTH, S5_GROUPS, S5_P), 0.01)
    s5_log_step = jax.random.uniform(next(ks), (DEPTH, S5_GROUPS), f32,
                                     math.log(DT_MIN), math.log(DT_MAX))
    s5_b_re = nrm((DEPTH, S5_GROUPS, S5_P, S5_CH), (2 * S5_CH) ** -0.5)
    s5_b_im = nrm((DEPTH, S5_GROUPS, S5_P, S5_CH), (2 * S5_CH) ** -0.5)
    s5_c_re = nrm((DEPTH, S5_GROUPS, S5_CH, S5_P), S5_P ** -0.5)
    s5_c_im = nrm((DEPTH, S5_GROUPS, S5_CH, S5_P), S5_P ** -0.5)
    s5_d = nrm((DEPTH, W_A), 1.0)
    s5_glu_w = nrm((DEPTH, W_A, W_A), W_A ** -0.5)
    s5_glu_b = nrm((DEPTH, W_A), 0.01)
    s5_norm_w = 1.0 + nrm((DEPTH, W_A), 0.02)
    conv_w = nrm((DEPTH, CONV_K, CONV_DIM), CONV_K ** -0.5)
    conv_b = nrm((DEPTH, CONV_DIM), 0.01)
    dt0 = jnp.exp(jax.random.uniform(next(ks), (DEPTH, SSD_HEADS), f32,
                                     math.log(DT_MIN), math.log(DT_MAX)))
    dt_bias = dt0 + jnp.log(-jnp.expm1(-dt0))
    a_log = jnp.log(jax.random.uniform(next(ks), (DEPTH, SSD_HEADS), f32, 1.0, 16.0))
    ssd_d = 1.0 + nrm((DEPTH, SSD_HEADS), 0.02)
    ssd_norm_w = 1.0 + nrm((DEPTH, W_S), 0.02)
    w_out = nrm((DEPTH, D_MIX, D_MODEL), D_MIX ** -0.5)
    final_norm_w = 1.0 + nrm((D_MODEL,), 0.02)
    return {
        "x_prompt": x_prompt, "x_sample": x_sample,
        "state_s5_re": state_s5_re, "state_s5_im": state_s5_im,
        "state_ssd": state_ssd, "cache_conv": cache_conv,
        "norm_w": norm_w, "w_in": w_in,
        "s5_lambda_re": s5_lambda_re, "s5_lambda_im": s5_lambda_im,
        "s5_log_step": s5_log_step, "s5_b_re": s5_b_re, "s5_b_im": s5_b_im,
        "s5_c_re": s5_c_re, "s5_c_im": s5_c_im, "s5_d": s5_d,
        "s5_glu_w": s5_glu_w, "s5_glu_b": s5_glu_b, "s5_norm_w": s5_norm_w,
        "conv_w": conv_w, "conv_b": conv_b, "dt_bias": dt_bias, "a_log": a_log,
        "ssd_d": ssd_d, "ssd_norm_w": ssd_norm_w, "w_out": w_out,
        "final_norm_w": final_norm_w,
    }


def reference(x_prompt, x_sample, state_s5_re, state_s5_im, state_ssd, cache_conv,
              norm_w, w_in, s5_lambda_re, s5_lambda_im, s5_log_step, s5_b_re, s5_b_im,
              s5_c_re, s5_c_im, s5_d, s5_glu_w, s5_glu_b, s5_norm_w,
              conv_w, conv_b, dt_bias, a_log, ssd_d, ssd_norm_w, w_out, final_norm_w):
    bp = x_prompt.shape[0]
    dtp = x_prompt.dtype
    z_s5 = jnp.zeros((bp, S5_GROUPS, S5_P), dtp)
    z_ssd = jnp.zeros((bp, SSD_HEADS, SSD_HEADDIM, SSD_N), dtp)
    z_conv = jnp.zeros((bp, CONV_K - 1, CONV_DIM), dtp)
    hp, hs = x_prompt, x_sample
    p_re, p_im, p_ssd, p_conv = [], [], [], []
    s_re, s_im, s_ssd, s_conv = [], [], [], []
    for l in range(DEPTH):
        lw = (norm_w[l], w_in[l], s5_lambda_re[l], s5_lambda_im[l], s5_log_step[l],
              s5_b_re[l], s5_b_im[l], s5_c_re[l], s5_c_im[l], s5_d[l],
              s5_glu_w[l], s5_glu_b[l], s5_norm_w[l], conv_w[l], conv_b[l],
              dt_bias[l], a_log[l], ssd_d[l], ssd_norm_w[l], w_out[l])
        hp, a1, a2, a3, a4 = mixer_layer(hp, z_s5, z_s5, z_ssd, z_conv, *lw)
        hs, b1, b2, b3, b4 = mixer_layer(hs, state_s5_re[l], state_s5_im[l],
                                         state_ssd[l], cache_conv[l], *lw)
        p_re.append(a1); p_im.append(a2); p_ssd.append(a3); p_conv.append(a4)
        s_re.append(b1); s_im.append(b2); s_ssd.append(b3); s_conv.append(b4)
    y_prompt = rmsnorm(hp, final_norm_w)
    y_sample = rmsnorm(hs, final_norm_w)
    return (y_prompt, y_sample,
            jnp.stack(p_re), jnp.stack(p_im), jnp.stack(p_ssd), jnp.stack(p_conv),
            jnp.stack(s_re), jnp.stack(s_im), jnp.stack(s_ssd), jnp.stack(s_conv))
```

```python
import bisect
import numpy as np
import ml_dtypes
import concourse.bass as bass
import concourse.mybir as mybir
from concourse.bass_utils import run_bass_kernel_spmd
from contextlib import ExitStack

F32 = mybir.dt.float32
BF16 = mybir.dt.bfloat16
DEBUG = False
LAST_RES = None
AF = mybir.ActivationFunctionType
ALU = mybir.AluOpType

D = 2048
KD = 16
SEQ = 2048
NSQ = 16
NCH = 8
NTP = SEQ // NCH
NS = 64
NTMAX = NTP + NS
SB = 64
NWB = 4
W_A = 2048
IN_COLS = 10272
EPS = 1e-5
TWO_PI_HI = 6.28125
TWO_PI_LO = 0.0019353071795864769


class Eng:
    def __init__(self, name, e, sem, compute):
        self.name, self.e, self.sem, self.compute = name, e, sem, compute
        self.idx = 0
        self.inc_idx = []
        self.last = None
        self.waited = {}


class Builder:
    def __init__(self, nc, es):
        self.nc, self.es = nc, es
        self.engs = {}
        for name, e, comp in (("pe", nc.tensor, True), ("act", nc.scalar, True), ("dve", nc.vector, True),
                              ("sp", nc.sync, False), ("pool", nc.gpsimd, False)):
            self.engs[name] = Eng(name, e, es.enter_context(nc.semaphore("sem_" + name)), comp)
        self.res = {}
        self.chans = {}

    def chan(self, name):
        if name not in self.chans:
            self.chans[name] = [self.es.enter_context(self.nc.semaphore("ch_" + name)), 0]
        return self.chans[name]

    def _need(self, waiter, dep):
        if dep is None:
            return
        if dep[0] == "dma":
            _, cname, n = dep
            n = self.chans[cname][1]
            key = "dma:" + cname
            if waiter.waited.get(key, 0) >= n:
                return
            waiter.waited[key] = n
            waiter.e.wait_ge(self.chans[cname][0], 16 * n)
            return
        pname, i = dep
        p = self.engs[pname]
        if waiter.name == "pe" and pname == "pe":
            return
        k = bisect.bisect_left(p.inc_idx, i)
        if k == len(p.inc_idx):
            p.last.then_inc(p.sem, 1)
            p.inc_idx.append(p.idx - 1)
        v = k + 1
        if waiter.waited.get(pname, 0) >= v:
            return
        waiter.waited[pname] = v
        waiter.e.wait_ge(p.sem, v)

    def _deps(self, waiter, r, w):
        for key in r:
            st = self.res.get(key)
            if st:
                self._need(waiter, st["w"])
        for key in w:
            st = self.res.get(key)
            if st:
                self._need(waiter, st["w"])
                for dep in st["r"].values():
                    self._need(waiter, dep)

    def _record(self, me, r, w):
        for key in r:
            st = self.res.setdefault(key, {"w": None, "r": {}})
            st["r"][me[0] if me[0] != "dma" else "dma:" + me[1]] = me
        for key in w:
            self.res[key] = {"w": me, "r": {}}

    def op(self, eng, fn, r=(), w=()):
        E = self.engs[eng]
        self._deps(E, r, w)
        ins = fn()
        E.last = ins
        me = (eng, E.idx)
        E.idx += 1
        self._record(me, r, w)
        return ins

    def dma(self, eng, out, in_, r=(), w=(), chan="d"):
        E = self.engs[eng]
        self._deps(E, r, w)
        c = self.chan(chan)
        E.e.dma_start(out=out, in_=in_).then_inc(c[0], 16)
        c[1] += 1
        self._record(("dma", chan, c[1]), r, w)

    def barrier(self):
        for W in self.engs.values():
            for p in ("pe", "act", "dve"):
                P = self.engs[p]
                if P.last is not None and not (W.name == "pe" and p == "pe"):
                    self._need(W, (p, P.idx - 1))
            for cname, c in self.chans.items():
                if c[1]:
                    self._need(W, ("dma", cname, c[1]))

    def finish(self):
        E = self.engs["sp"]
        for p in ("pe", "act", "dve"):
            P = self.engs[p]
            if P.last is not None:
                self._need(E, (p, P.idx - 1))
        for cname, c in self.chans.items():
            if c[1]:
                self._need(E, ("dma", cname, c[1]))


def build(const_shapes):
    nc = bass.Bass("TRN2", target_bir_lowering=False)
    es = ExitStack()
    with es:
        _build(nc, es, const_shapes)
    return nc


def _build(nc, es, const_shapes):
    def din(name, shape, dt=F32):
        return nc.dram_tensor(name, list(shape), dt, kind="ExternalInput").ap()

    def dout(name, shape):
        return nc.dram_tensor(name, list(shape), F32, kind="ExternalOutput").ap()

    xpT = din("xpT", [128, KD, SEQ])
    xsT = din("xsT", [128, KD, NS])
    s5in = din("s5in", [2, 2, 128, 64, NSQ])
    ssd_in = din("ssd_in", [2, NSQ, 32, 64, 128])
    ssd_inT = din("ssd_inT", [2, NSQ, 128, 2048])
    conv_in = din("conv_in", [2, 128, 32, NSQ, 3])
    w_in = din("w_in", [2, D, IN_COLS])
    glu_w = din("glu_w", [2, W_A, W_A])
    w_out = din("w_out", [2, 4096, D])
    vecs = din("vecs", const_shapes["vecs"])
    bc32 = din("bc32", [128, 2, 2, 32])
    s5par = din("s5par", [2, 3, 128, 64])
    bw = din("bw", [2, 128, 16, 2, 128])
    cw = din("cw", [2, 128, 64, 2, 32])
    cst = din("cst", const_shapes["cst"])
    negrep = din("negrep", [128, 2, 512], BF16)

    yT_out = dout("yT_out", [128, KD, SEQ + NS])
    s5p_out = dout("s5p_out", [2, 2, 128, 64])
    s5s_out = dout("s5s_out", [2, 2, 128, 64, NSQ])
    ssdp_out = dout("ssdp_out", [2, 128, 2048])
    ssds_out = dout("ssds_out", [2, NSQ, 32, 64, 128])
    convp_out = dout("convp_out", [2, 128, 32, 3])
    convs_out = dout("convs_out", [2, 128, 32, NSQ, 3])

    B = Builder(nc, es)
    op, dma = B.op, B.dma
    dbg_n = [0]

    def dbg(name, ap, key, dt=F32):
        if not DEBUG:
            return
        shp = list(ap.shape)
        t = nc.dram_tensor("dbg_" + name, shp, dt, kind="ExternalOutput").ap()
        dbg_n[0] += 1
        dma("sp", t, ap, r=[key], chan="dbg")

    def sb(name, shape, dt=F32):
        return es.enter_context(nc.sbuf_tensor("sb_" + name, list(shape), dt))

    ps = es.enter_context(nc.psum_tensor("ps", [128, 8, 512], F32))

    NVEC = const_shapes["vecs"][1]
    vec = sb("vec", [128, NVEC])
    dma("sp", vec[:], vecs[:, :], w=["vec"], chan="c0")
    NCST = const_shapes["cst"][1]
    cs_t = sb("cst", [128, NCST])
    dma("sp", cs_t[:], cst[:, :], w=["cst"], chan="c0")
    negr_b = sb("negr_b", [128, 2, 512], BF16)
    dma("sp", negr_b[:], negrep[:, :, :], w=["negr_b"], chan="c0")
    bc = sb("bc", [128, 2, 2, 32])
    dma("sp", bc[:], bc32[:, :, :, :], w=["bc"], chan="c0")
    ident_f = cs_t[:, 0:128]
    ones_f = cs_t[:, 128:256]
    maskT_p = cs_t[:, 256:384]
    maskT_s = cs_t[:, 384:448]
    EL_p = cs_t[:, 448:576]
    EL_s = cs_t[:, 576:640]
    selq = cs_t[:, 640:656]
    sellast = cs_t[:, 656:672]
    ident_b = sb("ident_b", [128, 128], BF16)
    op("act", lambda: nc.scalar.activation(out=ident_b[:], in_=ident_f, func=AF.Copy), r=["cst"], w=["ident_b"])

    VO = {}
    off = 0
    for nm, n in (("norm_w", 2 * 16), ("final_w", 16), ("s5_d", 2 * 16), ("glu_b", 2 * 16), ("s5_nw", 2 * 16),
                  ("conv_w", 2 * 4 * 32), ("conv_b", 2 * 32), ("ssd_nw", 2 * 16), ("ssd_d", 2 * 16)):
        VO[nm] = off
        off += n

    def vcol(nm, i):
        c = VO[nm] + i
        return vec[:, c:c + 1]

    a_bc = sb("a_bc", [128, 2, 32])
    op("act", lambda: nc.scalar.activation(out=a_bc[:], in_=bc[:, :, 1, :], func=AF.Exp), r=["bc"], w=["a_bc"])
    op("dve", lambda: nc.vector.tensor_scalar(out=a_bc[:], in0=a_bc[:], scalar1=-1.0, scalar2=None, op0=ALU.mult),
       r=["a_bc"], w=["a_bc"])

    lam = sb("lam", [128, 2, 5, 64])
    lis = sb("lis", [128, 2, 64, 2])
    setup_ph = ExitStack()
    sp_raw = setup_ph.enter_context(nc.sbuf_tensor("su_sp_raw", [128, 2, 3, 64], F32))
    for l in range(2):
        dma("sp", sp_raw[:, l, :, :], s5par[l].rearrange("k p q -> p k q"), w=["sp_raw"], chan="c0")
    tmpp = setup_ph.enter_context(nc.sbuf_tensor("su_tmpp", [128, 10, 64], F32))

    for l in range(2):
        T = lambda i: tmpp[:, i, :]
        lre, lim, lst = sp_raw[:, l, 0, :], sp_raw[:, l, 1, :], sp_raw[:, l, 2, :]
        k0 = ["tmpp"]

        def V(fn, r=k0, w=k0):
            op("dve", fn, r=list(r) + ["sp_raw"], w=w)

        def A(fn, r=k0, w=k0):
            op("act", fn, r=list(r) + ["sp_raw"], w=w)
        V(lambda: nc.vector.tensor_scalar(out=T(0), in0=lre, scalar1=-1e-4, scalar2=None, op0=ALU.min))
        A(lambda: nc.scalar.activation(out=T(1), in_=lst, func=AF.Exp))
        V(lambda: nc.vector.tensor_tensor(out=T(2), in0=T(0), in1=T(1), op=ALU.mult))
        A(lambda: nc.scalar.activation(out=T(2), in_=T(2), func=AF.Exp))
        V(lambda: nc.vector.tensor_tensor(out=T(3), in0=lim, in1=T(1), op=ALU.mult))
        V(lambda: nc.vector.tensor_scalar(out=T(4), in0=T(3), scalar1=float(1.0 / (2 * np.pi)), scalar2=12582912.0,
                                          op0=ALU.mult, op1=ALU.add))
        V(lambda: nc.vector.tensor_scalar(out=T(4), in0=T(4), scalar1=12582912.0, scalar2=None, op0=ALU.subtract))
        V(lambda: nc.vector.scalar_tensor_tensor(out=T(3), in0=T(4), scalar=-TWO_PI_HI, in1=T(3), op0=ALU.mult, op1=ALU.add))
        V(lambda: nc.vector.scalar_tensor_tensor(out=T(3), in0=T(4), scalar=-TWO_PI_LO, in1=T(3), op0=ALU.mult, op1=ALU.add))
        V(lambda: nc.vector.tensor_scalar(out=T(3), in0=T(3), scalar1=3.14159, scalar2=-3.14159, op0=ALU.min, op1=ALU.max))
        A(lambda: nc.scalar.activation(out=T(5), in_=T(3), func=AF.Sin))
        V(lambda: nc.vector.tensor_scalar(out=T(6), in0=T(3), scalar1=-1.0, scalar2=None, op0=ALU.mult))
        V(lambda: nc.vector.tensor_tensor(out=T(6), in0=T(6), in1=T(3), op=ALU.max))
        V(lambda: nc.vector.tensor_scalar(out=T(6), in0=T(6), scalar1=-1.0, scalar2=float(np.pi / 2), op0=ALU.mult, op1=ALU.add))
        A(lambda: nc.scalar.activation(out=T(6), in_=T(6), func=AF.Sin))
        V(lambda: nc.vector.tensor_tensor(out=lam[:, l, 0, :], in0=T(2), in1=T(6), op=ALU.mult), w=["lam", "tmpp"])
        V(lambda: nc.vector.tensor_tensor(out=lam[:, l, 1, :], in0=T(2), in1=T(5), op=ALU.mult), w=["lam", "tmpp"])
        V(lambda: nc.vector.tensor_tensor(out=T(7), in0=T(0), in1=T(0), op=ALU.mult))
        V(lambda: nc.vector.tensor_tensor(out=T(8), in0=lim, in1=lim, op=ALU.mult))
        V(lambda: nc.vector.tensor_tensor(out=T(7), in0=T(7), in1=T(8), op=ALU.add))
        V(lambda: nc.vector.reciprocal(out=T(7), in_=T(7)))
        V(lambda: nc.vector.tensor_scalar(out=T(8), in0=lam[:, l, 0, :], scalar1=-1.0, scalar2=None, op0=ALU.add),
          r=["tmpp", "lam"])
        V(lambda: nc.vector.tensor_tensor(out=T(1), in0=T(8), in1=T(0), op=ALU.mult))
        V(lambda: nc.vector.tensor_tensor(out=T(2), in0=lam[:, l, 1, :], in1=lim, op=ALU.mult), r=["tmpp", "lam"])
        V(lambda: nc.vector.tensor_tensor(out=T(1), in0=T(1), in1=T(2), op=ALU.add))
        V(lambda: nc.vector.tensor_tensor(out=lam[:, l, 2, :], in0=T(1), in1=T(7), op=ALU.mult), w=["lam", "tmpp"])
        V(lambda: nc.vector.tensor_tensor(out=T(1), in0=lam[:, l, 1, :], in1=T(0), op=ALU.mult), r=["tmpp", "lam"])
        V(lambda: nc.vector.tensor_tensor(out=T(2), in0=T(8), in1=lim, op=ALU.mult))
        V(lambda: nc.vector.tensor_tensor(out=T(1), in0=T(1), in1=T(2), op=ALU.subtract))
        V(lambda: nc.vector.tensor_tensor(out=lam[:, l, 3, :], in0=T(1), in1=T(7), op=ALU.mult), w=["lam", "tmpp"])
        V(lambda: nc.vector.tensor_scalar(out=lam[:, l, 4, :], in0=lam[:, l, 3, :], scalar1=-1.0, scalar2=None, op0=ALU.mult),
          r=["tmpp", "lam"], w=["lam", "tmpp"])
        V(lambda: nc.vector.tensor_scalar(out=lis[:, l, :, 0], in0=lam[:, l, 1, :], scalar1=-1.0, scalar2=None, op0=ALU.mult),
          r=["tmpp", "lam"], w=["lam", "tmpp"])
        V(lambda: nc.vector.tensor_copy(lis[:, l, :, 1], lam[:, l, 1, :]), r=["tmpp", "lam"], w=["lam", "tmpp"])

    B.barrier()
    setup_ph.close()
    wbf = [sb(f"wbf{i}", [128, 16, 128], BF16) for i in range(NWB)]
    bw_bf = sb("bw_bf", [128, 16, 2, 128], BF16)
    cw_bf = sb("cw_bf", [128, 64, 2, 32], BF16)
    wscr = nc.dram_tensor("wscr", [256, 128, 2048], BF16, kind="Internal").ap()
    s5scr = nc.dram_tensor("s5scr", [2, 2, 128, 4096], BF16, kind="Internal").ap()
    cur_ch = [0]
    tile_idx = [0]
    tmp_n = [0]

    def load_s5_weights(l):
        bwf = bw_bf[:].rearrange("p a b c -> p (a b c)")
        cwf = cw_bf[:].rearrange("p a b c -> p (a b c)")
        if cur_ch[0] > 0:
            dma("pool", bwf, s5scr[l, 0], r=[("s5scr", l)], w=["bw_bf"], chan="s5w")
            dma("pool", cwf, s5scr[l, 1], r=[("s5scr", l)], w=["cw_bf"], chan="s5w")
            return
        for ri in range(2):
            dma("pool", cw_bf[:, :, ri, :], cw[l, :, :, ri, :], w=["cw_bf"], chan="s5w")
        op("act", lambda: nc.scalar.activation(out=cw_bf[:, :, 1, :], in_=cw_bf[:, :, 1, :], func=AF.Copy, scale=-1.0),
           r=["cw_bf"], w=["cw_bf"])
        tp = ExitStack()
        tmp_n[0] += 1
        mk = lambda nm: tp.enter_context(nc.sbuf_tensor(f"tp_{nm}_{tmp_n[0]}", [128, 16, 128], F32))
        gl = [mk("glre"), mk("glim")]
        bwt = [mk("bwr"), mk("bwi")]
        Xd = mk("Xd")
        tm = mk("tm")
        for ri in range(2):
            dma("pool", bwt[ri][:], bw[l, :, :, ri, :], w=[("bwt", ri)], chan="s5w")
        for j in range(4):
            for gi in range(2):
                gcol = lam[:, l, 2 + gi, j:64:4]
                op("dve", lambda: nc.vector.tensor_tensor(out=Xd[:], in0=ident_f.unsqueeze(1).broadcast_to([128, 16, 128]),
                                                          in1=gcol.unsqueeze(2).broadcast_to([128, 16, 128]), op=ALU.mult),
                   r=["cst", "lam"], w=["Xd"])
                for q4 in range(4):
                    b_ = 4 * gi + q4
                    mm(ps[:, b_, :], ones_f, Xd[:, 4 * q4:4 * q4 + 4, :].rearrange("p a b -> p (a b)"), True, True,
                       r=["cst", "Xd"], w=[("ps", b_)])
                op("act", lambda: nc.scalar.activation(out=gl[gi][32 * j:32 * j + 32, :, :].rearrange("p a b -> p (a b)"),
                                                       in_=ps[32 * j:32 * j + 32, 4 * gi:4 * gi + 4, :].rearrange("p a b -> p (a b)"),
                                                       func=AF.Copy),
                   r=[("ps", 4 * gi + q4) for q4 in range(4)], w=[("gl", gi)])
        op("dve", lambda: nc.vector.tensor_tensor(out=Xd[:], in0=gl[0][:], in1=bwt[0][:], op=ALU.mult), r=[("gl", 0), ("bwt", 0)], w=["Xd"])
        op("dve", lambda: nc.vector.tensor_tensor(out=tm[:], in0=gl[1][:], in1=bwt[1][:], op=ALU.mult), r=[("gl", 1), ("bwt", 1)], w=["tm"])
        op("dve", lambda: nc.vector.tensor_tensor(out=bw_bf[:, :, 0, :], in0=Xd[:], in1=tm[:], op=ALU.subtract), r=["Xd", "tm"], w=["bw_bf"])
        op("dve", lambda: nc.vector.tensor_tensor(out=Xd[:], in0=gl[0][:], in1=bwt[1][:], op=ALU.mult), r=[("gl", 0), ("bwt", 1)], w=["Xd"])
        op("dve", lambda: nc.vector.tensor_tensor(out=tm[:], in0=gl[1][:], in1=bwt[0][:], op=ALU.mult), r=[("gl", 1), ("bwt", 0)], w=["tm"])
        op("dve", lambda: nc.vector.tensor_tensor(out=bw_bf[:, :, 1, :], in0=Xd[:], in1=tm[:], op=ALU.add), r=["Xd", "tm"], w=["bw_bf"])
        B.barrier()
        tp.close()
        dma("sp", s5scr[l, 0], bwf, r=["bw_bf"], w=[("s5scr", l)], chan="wsto")
        dma("sp", s5scr[l, 1], cwf, r=["cw_bf"], w=[("s5scr", l)], chan="wsto")

    xT = sb("xT", [128, KD, NTMAX])
    hT = sb("hT", [128, KD, NTMAX], BF16)
    yT = sb("yT", [128, KD, NTMAX], BF16)
    uT = sb("uT", [128, KD, NTMAX], BF16)
    gT = sb("gT", [128, KD, NTMAX], BF16)
    rstd = sb("rstd", [128, NTMAX])
    sq = [sb(f"sq{i}", [128, NTMAX]) for i in range(2)]
    t1 = sb("t1", [128, NTMAX])
    t2 = sb("t2", [128, NTMAX])
    s5c = sb("s5c", [128, 2, 64, 2])
    s5o = sb("s5o", [128, 2, 64])
    ST = sb("ST", [128, 2, 2048])
    ST_bf = sb("ST_bf", [128, 2048], BF16)
    convc = sb("convc", [128, 2, 32, 3])
    wdt2 = sb("wdt", [128, 2, 16, 32], BF16)
    ssq_acc = sb("ssq_acc", [128, NTMAX])
    phase_n = [0]

    def psb_(ph, name, shape, dt=F32):
        phase_n[0] += 1
        return ph.enter_context(nc.sbuf_tensor(f"ph_{name}_{phase_n[0]}", list(shape), dt))

    op("dve", lambda: nc.vector.memset(s5c[:], 0.0), w=["s5c"])
    op("dve", lambda: nc.vector.memset(ST[:], 0.0), w=["ST"])
    op("dve", lambda: nc.vector.memset(convc[:], 0.0), w=["convc"])

    wcnt = [0]

    def wtile(src, nk, ncols, scale=None):
        s = wcnt[0] % NWB
        wcnt[0] += 1
        idx = tile_idx[0]
        tile_idx[0] += 1
        out = wbf[s][:, 0:nk, 0:ncols]
        flat = wbf[s][:].rearrange("p a b -> p (a b)")
        if cur_ch[0] == 0:
            dma("pool", wbf[s][:], src.rearrange("(k p) c -> p k c", p=128), w=[("wbf", s)], chan=f"wl{s}")
            dma("sp", wscr[idx], flat, r=[("wbf", s)], w=[("wscr", idx)], chan="wsto")
        else:
            dma("pool", flat, wscr[idx], r=[("wscr", idx)], w=[("wbf", s)], chan=f"wl{s}")
        return out, ("wbf", s)

    def mm(out, lhsT, rhs, start, stop, r, w, **kw):
        op("pe", lambda: nc.tensor.matmul(out, lhsT, rhs, start=start, stop=stop, **kw), r=r, w=w)

    def proj(src, nk, rhs_t, rkey, NT, bank):
        wt, wk = wtile(src, nk, 128)
        outs = []
        for (c0, n, b) in ((0, NTP, bank), (NTP, NT - NTP, bank + 1)):
            if n <= 0:
                continue
            for k in range(nk):
                mm(ps[:, b, 0:n], wt[:, k, :], rhs_t[:, k, c0:c0 + n], k == 0, k == nk - 1, r=[wk, rkey], w=[("ps", b)])
            outs.append((ps[:, b, 0:n], c0, n, ("ps", b)))
        return outs

    def ssq_rstd(tag, blocks_fn, NT, width):
        for i in range(16):
            src, skey = blocks_fn(i)
            q = sq[i % 2]
            op("act", lambda: nc.scalar.activation(out=q[:, 0:NT], in_=src, func=AF.Square), r=[skey], w=[("sq", i % 2)])
            for (c0, n, b) in ((0, NTP, 6), (NTP, NT - NTP, 7)):
                if n > 0:
                    mm(ps[:, b, 0:n], ones_f, q[:, c0:c0 + n], i == 0, i == 15, r=[("sq", i % 2), "cst"], w=[("ps", b)])
        finish_rstd(NT, width)

    def finish_rstd(NT, width, src=None):
        for (c0, n, b) in ((0, NTP, 6), (NTP, NT - NTP, 7)):
            if n > 0 and src is not None:
                op("act", lambda: nc.scalar.activation(out=rstd[:, c0:c0 + n], in_=src[:, c0:c0 + n], func=AF.Sqrt,
                                                       scale=1.0 / width, bias=vec[:, VO["eps"]:VO["eps"] + 1]),
                   r=["ssq_acc", "vec"], w=["rstd"])
            elif n > 0:
                op("act", lambda: nc.scalar.activation(out=rstd[:, c0:c0 + n], in_=ps[:, b, 0:n], func=AF.Sqrt,
                                                       scale=1.0 / width, bias=vec[:, VO["eps"]:VO["eps"] + 1]),
                   r=[("ps", b), "vec"], w=["rstd"])
        op("dve", lambda: nc.vector.reciprocal(out=rstd[:, 0:NT], in_=rstd[:, 0:NT]), r=["rstd"], w=["rstd"])

    VO["eps"] = off

    for ch in range(NCH):
        NT = NTP + (NS if ch == 0 else 0)
        cur_ch[0] = ch
        tile_idx[0] = 0
        p0 = ch * NTP
        last = ch == NCH - 1
        units = [("p", 0, 128), ("p", 128, 128)] + ([("s", NTP, 64)] if ch == 0 else [])
        blocks = [("p", c, 0) for c in range(0, NTP, 32)] + ([("s", NTP, 0), ("s", NTP + 32, 8)] if ch == 0 else [])
        dma("sp", xT[:, :, 0:NTP], xpT[:, :, p0:p0 + NTP], w=["xT"], chan="xin")
        if ch == 0:
            dma("sp", xT[:, :, NTP:NT], xsT[:, :, :], w=["xT"], chan="xin")

        for l in range(2):
            ssq_rstd("n", lambda i: (xT[:, i, 0:NT], "xT"), NT, float(D))
            for k in range(KD):
                op("dve", lambda: nc.vector.scalar_tensor_tensor(out=hT[:, k, 0:NT], in0=xT[:, k, 0:NT],
                                                                 scalar=vcol("norm_w", l * 16 + k), in1=rstd[:, 0:NT],
                                                                 op0=ALU.mult, op1=ALU.mult),
                   r=["xT", "rstd", "vec"], w=["hT"])
            for blk in range(16):
                for (pa, c0, n, pk) in proj(w_in[l][:, blk * 128:(blk + 1) * 128], 16, hT, "hT", NT, (blk % 2) * 2):
                    op("act", lambda: nc.scalar.activation(out=uT[:, blk, c0:c0 + n], in_=pa, func=AF.Copy),
                       r=[pk], w=["uT"])
            if ch == 0 and l == 0:
                dbg("hT", hT[:, :, 0:NT], "hT", BF16)
                dbg("uT", uT[:, :, 0:NT], "uT", BF16)
            load_s5_weights(l)
            ph = ExitStack()
            bu2 = [psb_(ph, f"bu{i}", [128, 64, 2, 32]) for i in range(2)]
            hbf2 = [psb_(ph, f"hbf{i}", [128, 64, 2, 32], BF16) for i in range(2)]
            s5s = psb_(ph, "s5s", [128, 64, 2, NSQ])
            s5t = [psb_(ph, f"s5t{i}", [128, 64, 2, 8]) for i in range(2)]
            hc = [psb_(ph, f"hc{i}", [128, 64, 2]) for i in range(6)]
            if ch == 0:
                for ri in range(2):
                    dma("sp", s5s[:, :, ri, :], s5in[l, ri], w=["s5s"], chan="s5s")
            lr_, li_ = lam[:, l, 0, :], lam[:, l, 1, :]
            za_todo = list(range(16))
            nb = 32
            NHC = 6

            def stageA(k):
                kind, c0, q0 = blocks[k]
                bu = bu2[k % 2]
                for qd in range(4):
                    for bl in range(4):
                        blk = qd * 4 + bl
                        for j in range(4):
                            for ri in range(2):
                                o = ps[:, j, bl * 2 * nb + ri * nb: bl * 2 * nb + ri * nb + nb]
                                mm(o, bw_bf[32 * j:32 * j + 32, blk, ri, :], uT[32 * j:32 * j + 32, blk, c0:c0 + nb],
                                   True, True, r=["bw_bf", "uT"], w=[("ps", j)], tile_position=(32 * j, 0))
                    pv = ps[:, 0:4, 0:8 * nb].rearrange("p j (bl r t) -> p j bl r t", bl=4, r=2)
                    bv = bu[:, qd * 16:(qd + 1) * 16, :, :].rearrange("p (bl j) r t -> p j bl r t", j=4)
                    pk = [("ps", j) for j in range(4)]
                    for ri in range(2):
                        op("act", lambda: nc.scalar.activation(out=bv[:, :, :, ri, :], in_=pv[:, :, :, ri, :], func=AF.Copy),
                           r=pk, w=[("bu", k % 2)])

            def stageB(k):
                kind, c0, q0 = blocks[k]
                bu, hbf = bu2[k % 2], hbf2[k % 2]
                kb, kh = ("bu", k % 2), ("hbf", k % 2)
                if kind == "p":
                    shp = [128, 64, 2]
                    lr2 = lam[:, l, 0, :].unsqueeze(2).broadcast_to(shp)
                    li2 = lis[:, l, :, :]
                    tA = s5t[0][:].rearrange("p a b c -> p (a b c)")[:, 0:128].rearrange("p (a b) -> p a b", b=2)
                    tB = s5t[1][:].rearrange("p a b c -> p (a b c)")[:, 0:128].rearrange("p (a b) -> p a b", b=2)
                    for t in range(nb):
                        src_t = s5c[:, l, :, :] if t == 0 else hc[(t - 1) % NHC][:]
                        src_s = s5c[:, l, :, ::-1] if t == 0 else hc[(t - 1) % NHC][:, :, ::-1]
                        skey = "s5c" if t == 0 else ("hc", (t - 1) % NHC)
                        dst_t = s5c[:, l, :, :] if t == nb - 1 else hc[t % NHC][:]
                        dkey = "s5c" if t == nb - 1 else ("hc", t % NHC)
                        op("dve", lambda: nc.vector.tensor_tensor(out=tA, in0=src_t, in1=lr2, op=ALU.mult), r=[skey, "lam"], w=["s5t0"])
                        op("dve", lambda: nc.vector.tensor_tensor(out=tB, in0=src_s, in1=li2, op=ALU.mult), r=[skey, "lam"], w=["s5t1"])
                        op("dve", lambda: nc.vector.tensor_tensor(out=tA, in0=tA, in1=tB, op=ALU.add), r=["s5t0", "s5t1"], w=["s5t0"])
                        op("dve", lambda: nc.vector.tensor_tensor(out=dst_t, in0=tA, in1=bu[:, :, :, t], op=ALU.add), r=["s5t0", kb], w=[dkey])
                        op("act", lambda: nc.scalar.activation(out=hbf[:, :, :, t], in_=dst_t, func=AF.Copy), r=[dkey], w=[kh])
                    if last and c0 == NTP - nb:
                        op("act", lambda: nc.scalar.activation(out=s5o[:], in_=s5c[:, l, :, :].rearrange("p q r -> p r q"), func=AF.Copy),
                           r=["s5c"], w=["s5o"])
                        dma("sp", s5p_out[l].rearrange("r p q -> p r q"), s5o[:], r=["s5o"], chan="so")
                else:
                    shp = [128, 64, 2, 8]
                    lr2 = lam[:, l, 0, :].unsqueeze(2).unsqueeze(3).broadcast_to(shp)
                    li2 = lis[:, l, :, :].unsqueeze(3).broadcast_to(shp)
                    bu5 = bu[:].rearrange("p q r (s t) -> p q r s t", t=4)
                    col = lambda t: bu5[:, :, :, :, t]
                    cols_ = lambda t: bu5[:, :, ::-1, :, t]
                    prev0, prev0s = s5s[:, :, :, q0:q0 + 8], s5s[:, :, ::-1, q0:q0 + 8]
                    tA, tB = s5t[0][:], s5t[1][:]
                    for t in range(4):
                        pr = prev0 if t == 0 else col(t - 1)
                        prs = prev0s if t == 0 else cols_(t - 1)
                        rk = [kb, "s5s", "lam"]
                        op("dve", lambda: nc.vector.tensor_tensor(out=tA, in0=pr, in1=lr2, op=ALU.mult), r=rk, w=["s5t0"])
                        op("dve", lambda: nc.vector.tensor_tensor(out=tB, in0=prs, in1=li2, op=ALU.mult), r=rk, w=["s5t1"])
                        op("dve", lambda: nc.vector.tensor_tensor(out=tA, in0=tA, in1=tB, op=ALU.add), r=["s5t0", "s5t1"], w=["s5t0"])
                        op("dve", lambda: nc.vector.tensor_tensor(out=col(t), in0=col(t), in1=tA, op=ALU.add), r=["s5t0", kb], w=[kb])
                    op("act", lambda: nc.scalar.activation(out=s5s[:, :, :, q0:q0 + 8], in_=col(3), func=AF.Copy), r=[kb], w=["s5s"])
                    if q0 == 8:
                        for ri in range(2):
                            dma("sp", s5s_out[l, ri], s5s[:, :, ri, :], r=["s5s"], chan="so")
                    op("act", lambda: nc.scalar.activation(out=hbf[:], in_=bu[:], func=AF.Copy), r=[kb], w=[kh])

            def stageCmm(k):
                hbf = hbf2[k % 2]
                b = 4 + (k % 2)
                for blk in range(16):
                    for j in range(4):
                        pair = blk * 4 + j
                        for ri in range(2):
                            mm(ps[32 * j:32 * j + 32, b, blk * nb:(blk + 1) * nb], cw_bf[:, pair, ri, :], hbf[:, pair, ri, :], ri == 0, ri == 1,
                               r=["cw_bf", ("hbf", k % 2)], w=[("ps", b)], tile_position=(0, 32 * j), skip_group_check=True)

            def stageCdve(k):
                kind, c0, q0 = blocks[k]
                b = 4 + (k % 2)
                tmpv = s5t[1][:].rearrange("p a b c -> p (a b c)")[:, 0:16 * nb].rearrange("p (a t) -> p a t", t=nb)
                dcols = vec[:, VO["s5_d"] + l * 16: VO["s5_d"] + l * 16 + 16]
                op("dve", lambda: nc.vector.tensor_tensor(out=tmpv, in0=uT[:, :, c0:c0 + nb],
                                                          in1=dcols.unsqueeze(2).broadcast_to([128, 16, nb]), op=ALU.mult),
                   r=["uT", "vec"], w=["s5t1"])
                op("dve", lambda: nc.vector.tensor_tensor(out=gT[:, :, c0:c0 + nb], in0=tmpv,
                                                          in1=ps[:, b, :].rearrange("p (a t) -> p a t", t=nb), op=ALU.add),
                   r=["s5t1", ("ps", b)], w=["gT"])

            nblk = len(blocks)
            stageA(0)
            for k in range(nblk):
                if k + 1 < nblk:
                    stageA(k + 1)
                stageB(k)
                stageCmm(k)
                for _ in range(2):
                    if za_todo:
                        ob = za_todo.pop(0)
                        for (pa, zc0, zn, pk) in proj(w_in[l][:, 2048 + ob * 128: 2048 + (ob + 1) * 128], 16, hT, "hT", NT, 6):
                            op("act", lambda: nc.scalar.activation(out=yT[:, ob, zc0:zc0 + zn], in_=pa, func=AF.Silu), r=[pk], w=["yT"])
                if k >= 1:
                    stageCdve(k - 1)
            stageCdve(nblk - 1)
            for blk in range(16):
                xg = gT[:, blk, 0:NT]
                w2 = t2[:, 0:NT] if blk % 2 == 0 else t1[:, 0:NT]
                wk2 = "t2" if blk % 2 == 0 else "t1"
                op("act", lambda: nc.scalar.activation(out=w2, in_=xg, func=AF.Square), r=["gT"], w=[wk2])
                op("dve", lambda: nc.vector.tensor_scalar(out=w2, in0=w2, scalar1=0.044715, scalar2=1.0, op0=ALU.mult, op1=ALU.add),
                   r=[wk2], w=[wk2])
                op("dve", lambda: nc.vector.tensor_tensor(out=w2, in0=w2, in1=xg, op=ALU.mult), r=[wk2, "gT"], w=[wk2])
                op("act", lambda: nc.scalar.activation(out=w2, in_=w2, func=AF.Sigmoid, scale=1.5957691216057308), r=[wk2], w=[wk2])
                op("dve", lambda: nc.vector.tensor_tensor(out=xg, in0=xg, in1=w2, op=ALU.mult), r=[wk2, "gT"], w=["gT"])
            if ch == 0 and l == 0:
                dbg("gT", gT[:, :, 0:NT], "gT", BF16)
            B.barrier()
            ph.close()
            assert not za_todo
            pend = None

            def flush_ssq():
                ob_, q_ = pend
                for (c0_, n_, b_) in ((0, NTP, 6), (NTP, NT - NTP, 7)):
                    if n_ > 0:
                        mm(ps[:, b_, 0:n_], ones_f, q_[:, c0_:c0_ + n_], ob_ == 0, ob_ == 15, r=[("sq", ob_ % 2), "cst"], w=[("ps", b_)])
            for ob in range(16):
                pouts = proj(glu_w[l][:, ob * 128:(ob + 1) * 128], 16, gT, "gT", NT, (ob % 2) * 2)
                if pend is not None:
                    flush_ssq()
                for (pa, c0, n, pk) in pouts:
                    op("act", lambda: nc.scalar.activation(out=t1[:, c0:c0 + n], in_=pa, func=AF.Sigmoid,
                                                           bias=vcol("glu_b", l * 16 + ob)), r=[pk, "vec"], w=["t1"])
                op("dve", lambda: nc.vector.tensor_tensor(out=t1[:, 0:NT], in0=t1[:, 0:NT], in1=gT[:, ob, 0:NT], op=ALU.mult),
                   r=["t1", "gT"], w=["t1"])
                op("dve", lambda: nc.vector.tensor_tensor(out=t1[:, 0:NT], in0=t1[:, 0:NT], in1=yT[:, ob, 0:NT], op=ALU.mult),
                   r=["t1", "yT"], w=["t1"])
                q = sq[ob % 2]
                op("act", lambda: nc.scalar.activation(out=q[:, 0:NT], in_=t1[:, 0:NT], func=AF.Square), r=["t1"], w=[("sq", ob % 2)])
                pend = (ob, q)
                op("dve", lambda: nc.vector.tensor_scalar(out=yT[:, ob, 0:NT], in0=t1[:, 0:NT], scalar1=vcol("s5_nw", l * 16 + ob),
                                                          scalar2=None, op0=ALU.mult), r=["t1", "vec"], w=["yT"])
            flush_ssq()
            finish_rstd(NT, float(W_A))
            if ch == 0 and l == 0:
                dbg("yaT", yT[:, :, 0:NT], "yT", BF16)
                dbg("rstd_a", rstd[:, 0:NT], "rstd")

            def out_proj(row0):
                for db in range(16):
                    for (pa, c0, n, pk) in proj(w_out[l][row0:row0 + 2048, db * 128:(db + 1) * 128], 16, yT, "yT", NT, (db % 2) * 2):
                        op("dve", lambda: nc.vector.tensor_tensor(out=t2[:, c0:c0 + n], in0=pa, in1=rstd[:, c0:c0 + n], op=ALU.mult),
                           r=[pk, "rstd"], w=["t2"])
                        op("dve", lambda: nc.vector.tensor_tensor(out=xT[:, db, c0:c0 + n], in0=xT[:, db, c0:c0 + n], in1=t2[:, c0:c0 + n],
                                                                  op=ALU.add), r=["t2", "xT"], w=["xT"])
            out_proj(0)
            if ch == 0 and l == 0:
                dbg("x1a", xT[:, :, 0:NT], "xT")

            ph = ExitStack()
            cvs = psb_(ph, "cvs", [128, 32, NSQ, 3])
            raw = psb_(ph, "raw", [128, 3 + NTP])
            raws = psb_(ph, "raws", [128, NSQ, 7])
            dt_sb = psb_(ph, "dt_sb", [128, 32]); da_sb = psb_(ph, "da_sb", [128, 32]); cs_sb = psb_(ph, "cs_sb", [128, 32])
            ncs_sb = psb_(ph, "ncs_sb", [128, 32]); dte_sb = psb_(ph, "dte_sb", [128, 32]); dec_sb = psb_(ph, "dec_sb", [128, 32])
            dtd_sb = psb_(ph, "dtd_sb", [128, 32])
            Xg2 = [psb_(ph, f"Xg{i}", [128, 4, 128]) for i in range(2)]
            Ecs_f = psb_(ph, "Ecs", [128, 2048], BF16)
            Lt2 = [psb_(ph, f"Lt{i}", [128, 4, 128], BF16) for i in range(2)]
            MT_f = psb_(ph, "MT", [128, 2048], BF16)
            CpT_f = psb_(ph, "CpT", [128, 2048], BF16)
            xd = psb_(ph, "xd", [128, 2048], BF16); xdp = psb_(ph, "xdp", [128, 2048], BF16)
            Btok = psb_(ph, "Btok", [128, 1024], BF16); Bm = psb_(ph, "Bm", [128, 1024], BF16)
            h0_2 = [psb_(ph, f"h0{i}", [128, 16, 128]) for i in range(2)]
            h0T_2 = [psb_(ph, "h0T", [128, 2048])] * 2
            h0T_bf = psb_(ph, "h0T_bf", [128, 2048], BF16)
            decT = psb_(ph, "decT", [128, NSQ, 16]); Rq = psb_(ph, "Rq", [128, 2, NSQ, 16])
            for ob in range(16):
                for (pa, c0, n, pk) in proj(w_in[l][:, 8192 + ob * 128: 8192 + (ob + 1) * 128], 16, hT, "hT", NT, (ob % 2) * 2):
                    op("act", lambda: nc.scalar.activation(out=yT[:, ob, c0:c0 + n], in_=pa, func=AF.Silu), r=[pk], w=["yT"])
            wdt = wdt2[:, l, :, :]
            if ch == 0:
                dma("pool", wdt, w_in[l][:, 10240:10272].rearrange("(k p) c -> p k c", p=128), w=["wdt"], chan="wdt")
            if ch == 0:
                dma("sp", cvs[:], conv_in[l], w=["cvs"], chan="cvin")
            for blk in range(32):
                dst_t, dblk = (uT, blk) if blk < 16 else (gT, blk - 16)
                wk_ = lambda k: vcol("conv_w", (l * 4 + k) * 32 + blk)
                bcol = vcol("conv_b", l * 32 + blk)
                for (pa, c0, n, pk) in proj(w_in[l][:, 4096 + blk * 128: 4096 + (blk + 1) * 128], 16, hT, "hT", NT, (blk % 2) * 2):
                    if c0 == 0:
                        op("act", lambda: nc.scalar.activation(out=raw[:, 0:3], in_=convc[:, l, blk, :], func=AF.Copy),
                           r=["convc"], w=["raw"])
                        op("act", lambda: nc.scalar.activation(out=raw[:, 3:3 + NTP], in_=pa, func=AF.Copy), r=[pk], w=["raw"])
                        op("act", lambda: nc.scalar.activation(out=convc[:, l, blk, :], in_=raw[:, NTP:NTP + 3], func=AF.Copy),
                           r=["raw"], w=["convc"])
                        acc = t1[:, 0:NTP]
                        op("dve", lambda: nc.vector.tensor_scalar(out=acc, in0=raw[:, 3:3 + NTP], scalar1=wk_(3), scalar2=bcol,
                                                                  op0=ALU.mult, op1=ALU.add), r=["raw", "vec"], w=["t1"])
                        for k in range(3):
                            op("dve", lambda: nc.vector.scalar_tensor_tensor(out=acc, in0=raw[:, k:k + NTP], scalar=wk_(k), in1=acc,
                                                                             op0=ALU.mult, op1=ALU.add), r=["raw", "vec", "t1"], w=["t1"])
                        op("act", lambda: nc.scalar.activation(out=dst_t[:, dblk, 0:NTP], in_=acc, func=AF.Silu), r=["t1"],
                           w=["uT" if blk < 16 else "gT"])
                    else:
                        op("act", lambda: nc.scalar.activation(out=raws[:, :, 0:3], in_=cvs[:, blk, :, :], func=AF.Copy),
                           r=["cvs"], w=["raws"])
                        op("act", lambda: nc.scalar.activation(out=raws[:, :, 3:7], in_=pa.rearrange("p (s t) -> p s t", t=4), func=AF.Copy),
                           r=[pk], w=["raws"])
                        op("act", lambda: nc.scalar.activation(out=cvs[:, blk, :, :], in_=raws[:, :, 4:7], func=AF.Copy),
                           r=["raws"], w=["cvs"])
                        acc = t2[:, 0:NS].rearrange("p (s t) -> p s t", t=4)
                        op("dve", lambda: nc.vector.tensor_scalar(out=acc, in0=raws[:, :, 3:7], scalar1=wk_(3), scalar2=bcol,
                                                                  op0=ALU.mult, op1=ALU.add), r=["raws", "vec"], w=["t2"])
                        for k in range(3):
                            op("dve", lambda: nc.vector.scalar_tensor_tensor(out=acc, in0=raws[:, :, k:k + 4], scalar=wk_(k), in1=acc,
                                                                             op0=ALU.mult, op1=ALU.add), r=["raws", "vec", "t2"], w=["t2"])
                        op("act", lambda: nc.scalar.activation(out=dst_t[:, dblk, NTP:NT], in_=t2[:, 0:NS], func=AF.Silu), r=["t2"],
                           w=["uT" if blk < 16 else "gT"])
            if ch == 0:
                dma("sp", convs_out[l], cvs[:], r=["cvs"], chan="cvout")
            if last:
                dma("sp", convp_out[l], convc[:, l, :, :], r=["convc"], chan="cvout")
            if ch == 0 and l == 0:
                dbg("xsT", uT[:, :, 0:NT], "uT", BF16)
                dbg("bcT", gT[:, :, 0:NT], "gT", BF16)
                dbg("zsT", yT[:, :, 0:NT], "yT", BF16)
            BTt = lambda g: gT[:, g, :]
            CTt = lambda g: gT[:, 8 + g, :]

            for (kind, c0, TT) in units:
                samp = kind == "s"
                maskT = maskT_s if samp else maskT_p
                EL = EL_s if samp else EL_p
                ngr = negr_b[0:TT, 1 if samp else 0, 0:4 * TT]
                for k in range(16):
                    mm(ps[0:TT, 4, 0:32], hT[:, k, c0:c0 + TT], wdt[:, k, :], k == 0, k == 15, r=["hT", "wdt"], w=[("ps", 4)])
                U = lambda t: t[0:TT, :]
                op("dve", lambda: nc.vector.tensor_tensor(out=U(dt_sb), in0=ps[0:TT, 4, 0:32], in1=bc[0:TT, l, 0, :], op=ALU.add),
                   r=[("ps", 4), "bc"], w=["dt_sb"])
                op("act", lambda: nc.scalar.activation(out=U(dt_sb), in_=U(dt_sb), func=AF.Exp), r=["dt_sb"], w=["dt_sb"])
                op("act", lambda: nc.scalar.activation(out=U(dt_sb), in_=U(dt_sb), func=AF.Ln, bias=vec[0:TT, VO["eps"] + 1:VO["eps"] + 2]),
                   r=["dt_sb", "vec"], w=["dt_sb"])
                op("dve", lambda: nc.vector.tensor_tensor(out=U(da_sb), in0=U(dt_sb), in1=a_bc[0:TT, l, :], op=ALU.mult),
                   r=["dt_sb", "a_bc"], w=["da_sb"])
                mm(ps[0:TT, 4, 32:64], maskT[0:TT, 0:TT], U(da_sb), True, True, r=["cst", "da_sb"], w=[("ps", 4)])
                op("act", lambda: nc.scalar.activation(out=U(cs_sb), in_=ps[0:TT, 4, 32:64], func=AF.Copy), r=[("ps", 4)], w=["cs_sb"])
                op("dve", lambda: nc.vector.tensor_scalar(out=U(ncs_sb), in0=U(cs_sb), scalar1=-1.0, scalar2=None, op0=ALU.mult),
                   r=["cs_sb"], w=["ncs_sb"])
                mm(ps[0:TT, 4, 64:96], EL[0:TT, 0:TT], U(cs_sb), True, True, r=["cst", "cs_sb"], w=[("ps", 4)])
                op("dve", lambda: nc.vector.tensor_tensor(out=U(dte_sb), in0=ps[0:TT, 4, 64:96], in1=U(cs_sb), op=ALU.subtract),
                   r=[("ps", 4), "cs_sb"], w=["dte_sb"])
                op("act", lambda: nc.scalar.activation(out=U(dte_sb), in_=U(dte_sb), func=AF.Exp), r=["dte_sb"], w=["dte_sb"])
                op("dve", lambda: nc.vector.tensor_tensor(out=U(dtd_sb), in0=U(dte_sb), in1=U(dt_sb), op=ALU.mult),
                   r=["dte_sb", "dt_sb"], w=["dtd_sb"])
                if not samp:
                    mm(ps[:, 4, 96:128], EL_p, cs_sb[:, :], True, True, r=["cst", "cs_sb"], w=[("ps", 4)])
                    op("act", lambda: nc.scalar.activation(out=dec_sb[:], in_=ps[:, 4, 96:128], func=AF.Exp), r=[("ps", 4)], w=["dec_sb"])
                psb = ps[:, 5:7, :].rearrange("p a b -> p (a b)").bitcast(BF16)
                for blk in range(16):
                    op("pe", lambda: nc.tensor.transpose(psb[0:TT, blk * 128:(blk + 1) * 128], uT[:, blk, c0:c0 + TT], ident_b[:, :]),
                       r=["uT", "ident_b"], w=[("ps", 5), ("ps", 6)])
                pv3 = psb[0:TT, :].rearrange("p (h d) -> p h d", d=64)
                op("dve", lambda: nc.vector.tensor_tensor(out=xd[0:TT, :].rearrange("p (h d) -> p h d", d=64), in0=pv3,
                                                          in1=U(dt_sb).unsqueeze(2).broadcast_to([TT, 32, 64]), op=ALU.mult),
                   r=[("ps", 5), ("ps", 6), "dt_sb"], w=["xd"])
                op("dve", lambda: nc.vector.tensor_tensor(out=xdp[0:TT, :].rearrange("p (h d) -> p h d", d=64), in0=pv3,
                                                          in1=U(dtd_sb).unsqueeze(2).broadcast_to([TT, 32, 64]), op=ALU.mult),
                   r=[("ps", 5), ("ps", 6), "dtd_sb"], w=["xdp"])
                psb7 = ps[:, 7, :].bitcast(BF16)
                for g in range(8):
                    op("pe", lambda: nc.tensor.transpose(psb7[0:TT, g * 128:(g + 1) * 128], BTt(g)[:, c0:c0 + TT], ident_b[:, :]),
                       r=["gT", "ident_b"], w=[("ps", 7)])
                op("act", lambda: nc.scalar.activation(out=Btok[0:TT, :], in_=psb7[0:TT, :], func=AF.Copy), r=[("ps", 7)], w=["Btok"])
                if not samp:
                    op("act", lambda: nc.scalar.activation(out=ST_bf[:], in_=ST[:, l, :], func=AF.Copy), r=["ST"], w=["ST_bf"])
                ystarted = set()

                def views(g):
                    hs = slice(4 * g, 4 * g + 4)
                    if samp:
                        vw = lambda t_, np_: t_[0:np_, :].rearrange("p (h t) -> p h t", t=64)[:, hs, :]
                    else:
                        o_ = (g % 4) * 512
                        vw = lambda t_, np_: t_[0:np_, o_:o_ + 512].rearrange("p (h t) -> p h t", t=128)
                    return hs, vw(Ecs_f, 128), vw(MT_f, TT), vw(CpT_f, 128)

                def stage1(g):
                    par = g % 2
                    hs, Ec, Mg, Cg = views(g)
                    bA = 2 if par == 0 else 5
                    Xp, Ltp = Xg2[par], Lt2[par]
                    kx, kl, ke = ("Xg", par), ("Lt", par), ("Ecs", g % 4)
                    op("dve", lambda: nc.vector.tensor_tensor(out=Xp[0:TT, :, 0:TT],
                                                              in0=ident_f[0:TT, 0:TT].unsqueeze(1).broadcast_to([TT, 4, TT]),
                                                              in1=cs_sb[0:TT, hs].unsqueeze(2).broadcast_to([TT, 4, TT]), op=ALU.mult),
                       r=["cst", "cs_sb"], w=[kx])
                    for h4 in range(4):
                        mm(ps[:, bA, h4 * TT:(h4 + 1) * TT], ones_f[0:TT, :], Xp[0:TT, h4, 0:TT], h4 == 0, False,
                           r=["cst", kx], w=[("ps", bA)], skip_group_check=True)
                    op("act", lambda: nc.scalar.activation(out=Ec, in_=ps[:, bA, 0:4 * TT].rearrange("p (h t) -> p h t", h=4), func=AF.Exp),
                       r=[("ps", bA)], w=[ke])
                    mm(ps[0:TT, bA, 0:4 * TT], ident_b[0:TT, 0:TT], ngr, False, True, r=["ident_b", "negr_b"], w=[("ps", bA)],
                       skip_group_check=True)
                    for h4 in range(4):
                        op("act", lambda: nc.scalar.activation(out=Ltp[0:TT, h4, 0:TT], in_=ps[0:TT, bA, h4 * TT:(h4 + 1) * TT], func=AF.Exp,
                                                               bias=ncs_sb[0:TT, 4 * g + h4:4 * g + h4 + 1]),
                           r=[("ps", bA), "ncs_sb"], w=[kl])
                    bC = 3 if par == 0 else 6
                    mm(ps[0:TT, bC, 0:TT], BTt(g)[:, c0:c0 + TT], CTt(g)[:, c0:c0 + TT], True, True, r=["gT"], w=[("ps", bC)])

                def stage2(g):
                    par = g % 2
                    hs, Ec, Mg, Cg = views(g)
                    Ltp = Lt2[par]
                    kl, ke, km, kc = ("Lt", par), ("Ecs", g % 4), ("MT", g % 4), ("CpT", g % 4)
                    bC = 3 if par == 0 else 6
                    op("dve", lambda: nc.vector.tensor_tensor(out=Mg, in0=Ltp[0:TT, :, 0:TT],
                                                              in1=ps[0:TT, bC, 0:TT].unsqueeze(1).broadcast_to([TT, 4, TT]), op=ALU.mult),
                       r=[kl, ("ps", bC)], w=[km])
                    op("dve", lambda: nc.vector.tensor_tensor(out=Cg, in0=Ec,
                                                              in1=CTt(g)[:, c0:c0 + TT].unsqueeze(1).broadcast_to([128, 4, TT]), op=ALU.mult),
                       r=[ke, "gT"], w=[kc])
                    for h4 in range(4):
                        h = 4 * g + h4
                        hp, half = h // 2, h % 2
                        if samp:
                            o = ps[:, 0:2, :].rearrange("p a b -> p (a b)")[64 * half:64 * half + 64, hp * 64:(hp + 1) * 64]
                            okey = [("ps", 0), ("ps", 1)]
                            fk = (hp // 8, half)
                            st_flag = fk not in ystarted
                            ystarted.add(fk)
                        else:
                            o = ps[64 * half:64 * half + 64, 0, (hp % 2) * 128:(hp % 2) * 128 + 128]
                            okey = [("ps", 0)]
                            st_flag = True
                        mm(o, xd[0:TT, h * 64:(h + 1) * 64], Mg[:, h4, :], st_flag, False, r=["xd", km], w=okey,
                           tile_position=(0, 64 * half), skip_group_check=True)
                        if not samp:
                            mm(o, ST_bf[:, h * 64:(h + 1) * 64], Cg[:, h4, :], False, True, r=["ST_bf", kc], w=okey,
                               tile_position=(0, 64 * half))
                    if not samp:
                        ssd_tail(nc, op, mm, ps, g, c0, TT, l, uT, yT, t1, t2, sq, vcol, ones_f, ps[:, 0, 0:256], [("ps", 0)], ssq_acc)
                        mm(ps[:, 1, 0:256], Btok[0:TT, g * 128:(g + 1) * 128], xdp[0:TT, g * 256:(g + 1) * 256], True, True,
                           r=["Btok", "xdp"], w=[("ps", 1)])
                        sv = ST[:, l, g * 256:(g + 1) * 256].rearrange("p (h d) -> p h d", d=64)
                        op("dve", lambda: nc.vector.tensor_tensor(out=sv, in0=sv, in1=dec_sb[:, hs].unsqueeze(2).broadcast_to([128, 4, 64]),
                                                                  op=ALU.mult), r=["ST", "dec_sb", "ST_bf"], w=["ST"])
                        op("dve", lambda: nc.vector.tensor_tensor(out=ST[:, l, g * 256:(g + 1) * 256], in0=ST[:, l, g * 256:(g + 1) * 256],
                                                                  in1=ps[:, 1, 0:256], op=ALU.add), r=["ST", ("ps", 1)], w=["ST"])

                stage1(0)
                for g in range(8):
                    if g + 1 < 8:
                        stage1(g + 1)
                    stage2(g)
                if samp:
                    for h2 in range(2):
                        csv = cs_sb[0:64, :].rearrange("p (hp two) -> p two hp", two=2)[:, h2, :]
                        op("dve", lambda: nc.vector.tensor_tensor(out=Rq[0:64, h2, :, :], in0=csv.unsqueeze(1).broadcast_to([64, NSQ, 16]),
                                                                  in1=sellast[0:64, :].unsqueeze(2).broadcast_to([64, NSQ, 16]), op=ALU.mult),
                           r=["cs_sb", "cst"], w=["Rq"])
                        mm(ps[:, 5, 256 * h2:256 * h2 + 256], ones_f[0:64, :], Rq[0:64, h2, :, :].rearrange("p a b -> p (a b)"),
                           True, True, r=["cst", "Rq"], w=[("ps", 5)])
                    for h2 in range(2):
                        op("act", lambda: nc.scalar.activation(out=decT[64 * h2:64 * h2 + 64, :, :].rearrange("p a b -> p (a b)"),
                                                               in_=ps[64 * h2:64 * h2 + 64, 5, 256 * h2:256 * h2 + 256], func=AF.Exp),
                           r=[("ps", 5)], w=["decT"])
                    yall = ps[:, 0:2, :].rearrange("p a b -> p (a b)")
                    for q in range(NSQ):
                        h0, h0T = h0_2[q % 2], h0T_2[q % 2]
                        kh0, kh0T = ("h0", q % 2), "h0T"
                        dma("sp", h0T[:], ssd_inT[l, q], w=[kh0T], chan="h0T")
                        dma("sp", h0[:], ssd_in[l, q].rearrange("(hp two) d n -> (two d) hp n", two=2), w=[kh0], chan=f"h0{q % 2}")
                        op("act", lambda: nc.scalar.activation(out=h0T_bf[:], in_=h0T[:], func=AF.Copy), r=[kh0T], w=["h0T_bf"])
                        for h in range(32):
                            hp, half = h // 2, h % 2
                            o = yall[64 * half:64 * half + 64, hp * 64 + 4 * q: hp * 64 + 4 * q + 4]
                            mm(o, h0T_bf[:, h * 64:(h + 1) * 64], CpT_f[:, h * 64 + 4 * q: h * 64 + 4 * q + 4], False, q == NSQ - 1,
                               r=["h0T_bf"] + [("CpT", i) for i in range(4)], w=[("ps", 0), ("ps", 1)], tile_position=(0, 64 * half),
                               skip_group_check=True)
                        op("dve", lambda: nc.vector.tensor_scalar(out=Bm[0:64, :], in0=Btok[0:64, :], scalar1=selq[0:64, q:q + 1],
                                                                  scalar2=None, op0=ALU.mult), r=["Btok", "cst"], w=["Bm"])
                        for hp in range(16):
                            g = hp // 2
                            mm(ps[:, 4 + hp // 4, (hp % 4) * 128:(hp % 4) * 128 + 128], xdp[0:64, hp * 128:(hp + 1) * 128],
                               Bm[0:64, g * 128:(g + 1) * 128], True, True, r=["xdp", "Bm"], w=[("ps", 4 + hp // 4)])
                        upk = [("ps", 4), ("ps", 5), ("ps", 6), ("ps", 7)]
                        op("dve", lambda: nc.vector.tensor_tensor(out=h0[:], in0=h0[:],
                                                                  in1=decT[:, q, :].unsqueeze(2).broadcast_to([128, 16, 128]), op=ALU.mult),
                           r=[kh0, "decT"], w=[kh0])
                        op("dve", lambda: nc.vector.tensor_tensor(out=h0[:], in0=h0[:],
                                                                  in1=ps[:, 4:8, :].rearrange("p a (b n) -> p (a b) n", n=128), op=ALU.add),
                           r=[kh0] + upk, w=[kh0])
                        dma("sp", ssds_out[l, q].rearrange("(hp two) d n -> (two d) hp n", two=2), h0[:], r=[kh0], chan=f"h0o{q % 2}")
                    for g in range(8):
                        ssd_tail(nc, op, mm, ps, g, c0, TT, l, uT, yT, t1, t2, sq, vcol, ones_f,
                                 yall[:, g * 128:(g + 1) * 128], [("ps", 0), ("ps", 1)], ssq_acc)
            if last:
                dma("sp", ssdp_out[l], ST[:, l, :], r=["ST"], chan="stout")
            B.barrier()
            ph.close()
            finish_rstd(NT, float(W_A), ssq_acc)
            if ch == 0 and l == 0:
                dbg("ysT", yT[:, :, 0:NT], "yT", BF16)
                dbg("rstd_s", rstd[:, 0:NT], "rstd")
            out_proj(2048)
            if ch == 0 and l == 0:
                dbg("x1", xT[:, :, 0:NT], "xT")

        ssq_rstd("f", lambda i: (xT[:, i, 0:NT], "xT"), NT, float(D))
        for k in range(KD):
            op("dve", lambda: nc.vector.scalar_tensor_tensor(out=xT[:, k, 0:NT], in0=xT[:, k, 0:NT], scalar=vcol("final_w", k),
                                                             in1=rstd[:, 0:NT], op0=ALU.mult, op1=ALU.mult),
               r=["xT", "rstd", "vec"], w=["xT"])
        dma("sp", yT_out[:, :, p0:p0 + NTP], xT[:, :, 0:NTP], r=["xT"], chan="yout")
        if ch == 0:
            dma("sp", yT_out[:, :, SEQ:SEQ + NS], xT[:, :, NTP:NT], r=["xT"], chan="yout")
    B.finish()


def ssd_tail(nc, op, mm, ps, g, c0, TT, l, uT, yT, t1, t2, sq, vcol, ones_f, ypsum, ykeys, ssq_acc):
    for j in range(2):
        hp = 2 * g + j
        yp = ypsum[:, j * TT:(j + 1) * TT]
        ya = t1[:, 0:TT]
        op("dve", lambda: nc.vector.scalar_tensor_tensor(out=ya, in0=uT[:, hp, c0:c0 + TT], scalar=vcol("ssd_d", l * 16 + hp), in1=yp,
                                                         op0=ALU.mult, op1=ALU.add), r=["uT", "vec"] + ykeys, w=["t1"])
        op("dve", lambda: nc.vector.tensor_tensor(out=ya, in0=ya, in1=yT[:, hp, c0:c0 + TT], op=ALU.mult), r=["t1", "yT"], w=["t1"])
        q = sq[hp % 2]
        op("act", lambda: nc.scalar.activation(out=q[:, 0:TT], in_=ya, func=AF.Square), r=["t1"], w=[("sq", hp % 2)])
        mm(ps[:, 7, 0:TT], ones_f, q[:, 0:TT], hp == 0, hp == 15, r=[("sq", hp % 2), "cst"], w=[("ps", 7)])
        if hp == 15:
            op("act", lambda: nc.scalar.activation(out=ssq_acc[:, c0:c0 + TT], in_=ps[:, 7, 0:TT], func=AF.Copy),
               r=[("ps", 7)], w=["ssq_acc"])
        op("dve", lambda: nc.vector.tensor_scalar(out=yT[:, hp, c0:c0 + TT], in0=ya, scalar1=vcol("ssd_nw", l * 16 + hp),
                                                  scalar2=None, op0=ALU.mult), r=["t1", "vec"], w=["yT"])


def _consts():
    cst = np.zeros((128, 672), np.float32)
    cst[:, 0:128] = np.eye(128)
    cst[:, 128:256] = 1.0
    s = np.arange(128)
    cst[:, 256:384] = (s[:, None] <= s[None, :])
    s6 = np.arange(64)
    cst[:64, 384:448] = (s6[:, None] <= s6[None, :]) & (s6[:, None] // 4 == s6[None, :] // 4)
    cst[127, 448:576] = 1.0
    cst[:64, 576:640] = (s6[:, None] == 4 * (s6[None, :] // 4) + 3)
    cst[:64, 640:656] = (s6[:, None] // 4 == np.arange(16)[None, :])
    cst[:64, 656:672] = (s6[:, None] == 4 * np.arange(16)[None, :] + 3)
    neg = np.zeros((128, 2, 512), np.float32)
    mp = np.where(s[:, None] <= s[None, :], 0.0, -30000.0)
    neg[:, 0, :] = np.tile(mp, (1, 4))
    ms = np.where((s6[:, None] <= s6[None, :]) & (s6[:, None] // 4 == s6[None, :] // 4), 0.0, -30000.0)
    neg[:64, 1, 0:256] = np.tile(ms, (1, 4))
    return cst, neg.astype(ml_dtypes.bfloat16)


def kernel(x_prompt, x_sample, state_s5_re, state_s5_im, state_ssd, cache_conv,
           norm_w, w_in, s5_lambda_re, s5_lambda_im, s5_log_step, s5_b_re, s5_b_im,
           s5_c_re, s5_c_im, s5_d, s5_glu_w, s5_glu_b, s5_norm_w,
           conv_w, conv_b, dt_bias, a_log, ssd_d, ssd_norm_w, w_out, final_norm_w):
    f = lambda a: np.ascontiguousarray(np.asarray(a, dtype=np.float32))
    x_prompt, x_sample = f(x_prompt), f(x_sample)
    state_ssd = f(state_ssd)
    cst, neg = _consts()

    def chan(v):
        v = f(v)
        return v.reshape(v.shape[:-1] + (v.shape[-1] // 128, 128))

    cols = []
    cols.append(chan(norm_w).transpose(2, 0, 1).reshape(128, -1))
    cols.append(chan(final_norm_w).T)
    cols.append(chan(s5_d).transpose(2, 0, 1).reshape(128, -1))
    cols.append(chan(s5_glu_b).transpose(2, 0, 1).reshape(128, -1))
    cols.append(chan(s5_norm_w).transpose(2, 0, 1).reshape(128, -1))
    cols.append(chan(conv_w).transpose(3, 0, 1, 2).reshape(128, -1))
    cols.append(chan(conv_b).transpose(2, 0, 1).reshape(128, -1))
    cols.append(chan(ssd_norm_w).transpose(2, 0, 1).reshape(128, -1))
    sd = f(ssd_d)
    sdl = np.repeat(sd.reshape(2, 16, 2).transpose(2, 0, 1)[:, None, :, :], 64, axis=1).reshape(128, 32)
    cols.append(sdl)
    cols.append(np.full((128, 1), EPS, np.float32))
    cols.append(np.ones((128, 1), np.float32))
    vecs = f(np.concatenate(cols, axis=1))
    bc32 = np.broadcast_to(np.stack([f(dt_bias), f(a_log)], axis=1)[None], (128, 2, 2, 32)).copy()
    def gp(v):
        return f(v).reshape(2, 64, 2, 64).transpose(0, 2, 3, 1).reshape(2, 128, 64)
    lst = np.broadcast_to(f(s5_log_step)[:, :, None], (2, 128, 64))
    s5par = f(np.stack([gp(s5_lambda_re), gp(s5_lambda_im), gp(lst)], axis=1))
    bw = np.zeros((2, 4, 2, 16, 16, 2, 2, 64), np.float32)
    for ri, bsrc in enumerate((f(s5_b_re), f(s5_b_im))):
        bb = bsrc.reshape(2, 16, 4, 2, 64, 16)
        for g2 in range(2):
            bw[:, :, g2, :, :, ri, g2, :] = bb[:, :, :, g2, :, :].transpose(0, 2, 4, 1, 3)
    bw = f(bw.reshape(2, 128, 16, 2, 128))
    cw = np.zeros((2, 2, 64, 64, 2, 2, 16), np.float32)
    for ri, csrc in enumerate((f(s5_c_re), f(s5_c_im))):
        cc = csrc.reshape(2, 64, 2, 16, 64)
        for g2 in range(2):
            cw[:, g2, :, :, ri, g2, :] = cc[:, :, g2, :, :].transpose(0, 3, 1, 2)
    cw = f(cw.reshape(2, 128, 64, 2, 32))

    w_in, glu_w, w_out = f(w_in), f(s5_glu_w), f(w_out)
    s5re, s5im = f(state_s5_re), f(state_s5_im)
    cache_conv = f(cache_conv)
    in_maps = []
    for c in range(8):
        b = c % 4
        sl = slice(c * NSQ, (c + 1) * NSQ)
        xpT = f(x_prompt[b].reshape(SEQ, 16, 128).transpose(2, 1, 0))
        xsT = f(x_sample[sl].reshape(NS, 16, 128).transpose(2, 1, 0))
        s5 = np.stack([s5re[:, sl], s5im[:, sl]], axis=1)
        s5 = s5.reshape(2, 2, NSQ, 64, 2, 64).transpose(0, 1, 4, 5, 3, 2).reshape(2, 2, 128, 64, NSQ)
        ssd_c = state_ssd[:, sl]
        ssdT = f(ssd_c.transpose(0, 1, 4, 2, 3).reshape(2, NSQ, 128, 2048))
        cv = cache_conv[:, sl].reshape(2, NSQ, 3, 32, 128).transpose(0, 4, 3, 1, 2)
        in_maps.append({
            "xpT": xpT, "xsT": xsT, "s5in": f(s5), "ssd_in": f(ssd_c), "ssd_inT": ssdT, "conv_in": f(cv),
            "w_in": w_in, "glu_w": glu_w, "w_out": w_out, "vecs": vecs, "bc32": f(bc32), "s5par": s5par,
            "bw": bw, "cw": cw, "cst": cst, "negrep": neg,
        })
    nc = build({"vecs": list(vecs.shape), "cst": list(cst.shape)})
    res = run_bass_kernel_spmd(nc, in_maps, core_ids=list(range(8))).results
    global LAST_RES
    LAST_RES = res

    B4 = 4
    y_prompt = np.zeros((B4, SEQ, D), np.float32)
    y_sample = np.zeros((128, 4, D), np.float32)
    p_re = np.zeros((2, B4, 128, 64), np.float32); p_im = np.zeros_like(p_re)
    p_ssd = np.zeros((2, B4, 32, 64, 128), np.float32)
    p_conv = np.zeros((2, B4, 3, 4096), np.float32)
    s_re = np.zeros((2, 128, 128, 64), np.float32); s_im = np.zeros_like(s_re)
    s_ssd = np.zeros((2, 128, 32, 64, 128), np.float32)
    s_conv = np.zeros((2, 128, 3, 4096), np.float32)
    for c in range(8):
        r = res[c]
        sl = slice(c * NSQ, (c + 1) * NSQ)
        yT = r["yT_out"]
        ytok = yT.transpose(2, 1, 0).reshape(SEQ + NS, D)
        y_sample[sl] = ytok[SEQ:].reshape(NSQ, 4, D)
        s5s = r["s5s_out"].reshape(2, 2, 2, 64, 64, NSQ).transpose(0, 1, 5, 4, 2, 3).reshape(2, 2, NSQ, 128, 64)
        s_re[:, sl], s_im[:, sl] = s5s[:, 0], s5s[:, 1]
        s_ssd[:, sl] = r["ssds_out"]
        s_conv[:, sl] = r["convs_out"].transpose(0, 3, 4, 2, 1).reshape(2, NSQ, 3, 4096)
        if c < 4:
            y_prompt[c] = ytok[:SEQ]
            s5p = r["s5p_out"].reshape(2, 2, 2, 64, 64).transpose(0, 1, 4, 2, 3).reshape(2, 2, 128, 64)
            p_re[:, c], p_im[:, c] = s5p[:, 0], s5p[:, 1]
            p_ssd[:, c] = r["ssdp_out"].reshape(2, 128, 32, 64).transpose(0, 2, 3, 1)
            p_conv[:, c] = r["convp_out"].transpose(0, 3, 2, 1).reshape(2, 3, 4096)
    return (y_prompt, y_sample, p_re, p_im, p_ssd, p_conv, s_re, s_im, s_ssd, s_conv)
```

```python
import bisect
import numpy as np
import ml_dtypes
import concourse.bass as bass
import concourse.mybir as mybir
from concourse.bass_utils import run_bass_kernel_spmd
from contextlib import ExitStack

F32 = mybir.dt.float32
BF16 = mybir.dt.bfloat16
DEBUG = False
LAST_RES = None
AF = mybir.ActivationFunctionType
ALU = mybir.AluOpType

D = 2048
KD = 16
SEQ = 2048
NSQ = 16
NCH = 8
NTP = SEQ // NCH
NS = 64
NTMAX = NTP + NS
SB = 64
NWB = 4
W_A = 2048
IN_COLS = 10272
EPS = 1e-5
TWO_PI_HI = 6.28125
TWO_PI_LO = 0.0019353071795864769


class Eng:
    def __init__(self, name, e, sem, compute):
        self.name, self.e, self.sem, self.compute = name, e, sem, compute
        self.idx = 0
        self.inc_idx = []
        self.last = None
        self.waited = {}


class Builder:
    def __init__(self, nc, es):
        self.nc, self.es = nc, es
        self.engs = {}
        for name, e, comp in (("pe", nc.tensor, True), ("act", nc.scalar, True), ("dve", nc.vector, True),
                              ("sp", nc.sync, False), ("pool", nc.gpsimd, False)):
            self.engs[name] = Eng(name, e, es.enter_context(nc.semaphore("sem_" + name)), comp)
        self.res = {}
        self.chans = {}

    def chan(self, name):
        if name not in self.chans:
            self.chans[name] = [self.es.enter_context(self.nc.semaphore("ch_" + name)), 0]
        return self.chans[name]

    def _need(self, waiter, dep):
        if dep is None:
            return
        if dep[0] == "dma":
            _, cname, n = dep
            n = self.chans[cname][1]
            key = "dma:" + cname
            if waiter.waited.get(key, 0) >= n:
                return
            waiter.waited[key] = n
            waiter.e.wait_ge(self.chans[cname][0], 16 * n)
            return
        pname, i = dep
        p = self.engs[pname]
        if waiter.name == "pe" and pname == "pe":
            return
        k = bisect.bisect_left(p.inc_idx, i)
        if k == len(p.inc_idx):
            p.last.then_inc(p.sem, 1)
            p.inc_idx.append(p.idx - 1)
        v = k + 1
        if waiter.waited.get(pname, 0) >= v:
            return
        waiter.waited[pname] = v
        waiter.e.wait_ge(p.sem, v)

    def _deps(self, waiter, r, w):
        for key in r:
            st = self.res.get(key)
            if st:
                self._need(waiter, st["w"])
        for key in w:
            st = self.res.get(key)
            if st:
                self._need(waiter, st["w"])
                for dep in st["r"].values():
                    self._need(waiter, dep)

    def _record(self, me, r, w):
        for key in r:
            st = self.res.setdefault(key, {"w": None, "r": {}})
            st["r"][me[0] if me[0] != "dma" else "dma:" + me[1]] = me
        for key in w:
            self.res[key] = {"w": me, "r": {}}

    def op(self, eng, fn, r=(), w=()):
        E = self.engs[eng]
        self._deps(E, r, w)
        ins = fn()
        E.last = ins
        me = (eng, E.idx)
        E.idx += 1
        self._record(me, r, w)
        return ins

    def dma(self, eng, out, in_, r=(), w=(), chan="d"):
        E = self.engs[eng]
        self._deps(E, r, w)
        c = self.chan(chan)
        E.e.dma_start(out=out, in_=in_).then_inc(c[0], 16)
        c[1] += 1
        self._record(("dma", chan, c[1]), r, w)

    def barrier(self):
        for W in self.engs.values():
            for p in ("pe", "act", "dve"):
                P = self.engs[p]
                if P.last is not None and not (W.name == "pe" and p == "pe"):
                    self._need(W, (p, P.idx - 1))
            for cname, c in self.chans.items():
                if c[1]:
                    self._need(W, ("dma", cname, c[1]))

    def finish(self):
        E = self.engs["sp"]
        for p in ("pe", "act", "dve"):
            P = self.engs[p]
            if P.last is not None:
                self._need(E, (p, P.idx - 1))
        for cname, c in self.chans.items():
            if c[1]:
                self._need(E, ("dma", cname, c[1]))


def build(const_shapes):
    nc = bass.Bass("TRN2", target_bir_lowering=False)
    es = ExitStack()
    with es:
        _build(nc, es, const_shapes)
    return nc


def _build(nc, es, const_shapes):
    def din(name, shape, dt=F32):
        return nc.dram_tensor(name, list(shape), dt, kind="ExternalInput").ap()

    def dout(name, shape):
        return nc.dram_tensor(name, list(shape), F32, kind="ExternalOutput").ap()

    xpT = din("xpT", [128, KD, SEQ])
    xsT = din("xsT", [128, KD, NS])
    s5in = din("s5in", [2, 2, 128, 64, NSQ])
    ssd_in = din("ssd_in", [2, NSQ, 32, 64, 128])
    ssd_inT = din("ssd_inT", [2, NSQ, 128, 2048])
    conv_in = din("conv_in", [2, 128, 32, NSQ, 3])
    w_in = din("w_in", [2, D, IN_COLS])
    glu_w = din("glu_w", [2, W_A, W_A])
    w_out = din("w_out", [2, 4096, D])
    vecs = din("vecs", const_shapes["vecs"])
    bc32 = din("bc32", [128, 2, 2, 32])
    s5par = din("s5par", [2, 3, 128, 64])
    bw = din("bw", [2, 128, 16, 2, 128])
    cw = din("cw", [2, 128, 64, 2, 32])
    cst = din("cst", const_shapes["cst"])
    negrep = din("negrep", [128, 2, 512], BF16)

    yT_out = dout("yT_out", [128, KD, SEQ + NS])
    s5p_out = dout("s5p_out", [2, 2, 128, 64])
    s5s_out = dout("s5s_out", [2, 2, 128, 64, NSQ])
    ssdp_out = dout("ssdp_out", [2, 128, 2048])
    ssds_out = dout("ssds_out", [2, NSQ, 32, 64, 128])
    convp_out = dout("convp_out", [2, 128, 32, 3])
    convs_out = dout("convs_out", [2, 128, 32, NSQ, 3])

    B = Builder(nc, es)
    op, dma = B.op, B.dma
    dbg_n = [0]

    def dbg(name, ap, key, dt=F32):
        if not DEBUG:
            return
        shp = list(ap.shape)
        t = nc.dram_tensor("dbg_" + name, shp, dt, kind="ExternalOutput").ap()
        dbg_n[0] += 1
        dma("sp", t, ap, r=[key], chan="dbg")

    def sb(name, shape, dt=F32):
        return es.enter_context(nc.sbuf_tensor("sb_" + name, list(shape), dt))

    ps = es.enter_context(nc.psum_tensor("ps", [128, 8, 512], F32))

    NVEC = const_shapes["vecs"][1]
    vec = sb("vec", [128, NVEC])
    dma("sp", vec[:], vecs[:, :], w=["vec"], chan="c0")
    NCST = const_shapes["cst"][1]
    cs_t = sb("cst", [128, NCST])
    dma("sp", cs_t[:], cst[:, :], w=["cst"], chan="c0")
    negr_b = sb("negr_b", [128, 2, 512], BF16)
    dma("sp", negr_b[:], negrep[:, :, :], w=["negr_b"], chan="c0")
    bc = sb("bc", [128, 2, 2, 32])
    dma("sp", bc[:], bc32[:, :, :, :], w=["bc"], chan="c0")
    ident_f = cs_t[:, 0:128]
    ones_f = cs_t[:, 128:256]
    maskT_p = cs_t[:, 256:384]
    maskT_s = cs_t[:, 384:448]
    EL_p = cs_t[:, 448:576]
    EL_s = cs_t[:, 576:640]
    selq = cs_t[:, 640:656]
    sellast = cs_t[:, 656:672]
    ident_b = sb("ident_b", [128, 128], BF16)
    op("act", lambda: nc.scalar.activation(out=ident_b[:], in_=ident_f, func=AF.Copy), r=["cst"], w=["ident_b"])

    VO = {}
    off = 0
    for nm, n in (("norm_w", 2 * 16), ("final_w", 16), ("s5_d", 2 * 16), ("glu_b", 2 * 16), ("s5_nw", 2 * 16),
                  ("conv_w", 2 * 4 * 32), ("conv_b", 2 * 32), ("ssd_nw", 2 * 16), ("ssd_d", 2 * 16)):
        VO[nm] = off
        off += n

    def vcol(nm, i):
        c = VO[nm] + i
        return vec[:, c:c + 1]

    a_bc = sb("a_bc", [128, 2, 32])
    op("act", lambda: nc.scalar.activation(out=a_bc[:], in_=bc[:, :, 1, :], func=AF.Exp), r=["bc"], w=["a_bc"])
    op("dve", lambda: nc.vector.tensor_scalar(out=a_bc[:], in0=a_bc[:], scalar1=-1.0, scalar2=None, op0=ALU.mult),
       r=["a_bc"], w=["a_bc"])

    lam = sb("lam", [128, 2, 5, 64])
    lis = sb("lis", [128, 2, 64, 2])
    setup_ph = ExitStack()
    sp_raw = setup_ph.enter_context(nc.sbuf_tensor("su_sp_raw", [128, 2, 3, 64], F32))
    for l in range(2):
        dma("sp", sp_raw[:, l, :, :], s5par[l].rearrange("k p q -> p k q"), w=["sp_raw"], chan="c0")
    tmpp = setup_ph.enter_context(nc.sbuf_tensor("su_tmpp", [128, 10, 64], F32))

    for l in range(2):
        T = lambda i: tmpp[:, i, :]
        lre, lim, lst = sp_raw[:, l, 0, :], sp_raw[:, l, 1, :], sp_raw[:, l, 2, :]
        k0 = ["tmpp"]

        def V(fn, r=k0, w=k0):
            op("dve", fn, r=list(r) + ["sp_raw"], w=w)

        def A(fn, r=k0, w=k0):
            op("act", fn, r=list(r) + ["sp_raw"], w=w)
        V(lambda: nc.vector.tensor_scalar(out=T(0), in0=lre, scalar1=-1e-4, scalar2=None, op0=ALU.min))
        A(lambda: nc.scalar.activation(out=T(1), in_=lst, func=AF.Exp))
        V(lambda: nc.vector.tensor_tensor(out=T(2), in0=T(0), in1=T(1), op=ALU.mult))
        A(lambda: nc.scalar.activation(out=T(2), in_=T(2), func=AF.Exp))
        V(lambda: nc.vector.tensor_tensor(out=T(3), in0=lim, in1=T(1), op=ALU.mult))
        V(lambda: nc.vector.tensor_scalar(out=T(4), in0=T(3), scalar1=float(1.0 / (2 * np.pi)), scalar2=12582912.0,
                                          op0=ALU.mult, op1=ALU.add))
        V(lambda: nc.vector.tensor_scalar(out=T(4), in0=T(4), scalar1=12582912.0, scalar2=None, op0=ALU.subtract))
        V(lambda: nc.vector.scalar_tensor_tensor(out=T(3), in0=T(4), scalar=-TWO_PI_HI, in1=T(3), op0=ALU.mult, op1=ALU.add))
        V(lambda: nc.vector.scalar_tensor_tensor(out=T(3), in0=T(4), scalar=-TWO_PI_LO, in1=T(3), op0=ALU.mult, op1=ALU.add))
        V(lambda: nc.vector.tensor_scalar(out=T(3), in0=T(3), scalar1=3.14159, scalar2=-3.14159, op0=ALU.min, op1=ALU.max))
        A(lambda: nc.scalar.activation(out=T(5), in_=T(3), func=AF.Sin))
        V(lambda: nc.vector.tensor_scalar(out=T(6), in0=T(3), scalar1=-1.0, scalar2=None, op0=ALU.mult))
        V(lambda: nc.vector.tensor_tensor(out=T(6), in0=T(6), in1=T(3), op=ALU.max))
        V(lambda: nc.vector.tensor_scalar(out=T(6), in0=T(6), scalar1=-1.0, scalar2=float(np.pi / 2), op0=ALU.mult, op1=ALU.add))
        A(lambda: nc.scalar.activation(out=T(6), in_=T(6), func=AF.Sin))
        V(lambda: nc.vector.tensor_tensor(out=lam[:, l, 0, :], in0=T(2), in1=T(6), op=ALU.mult), w=["lam", "tmpp"])
        V(lambda: nc.vector.tensor_tensor(out=lam[:, l, 1, :], in0=T(2), in1=T(5), op=ALU.mult), w=["lam", "tmpp"])
        V(lambda: nc.vector.tensor_tensor(out=T(7), in0=T(0), in1=T(0), op=ALU.mult))
        V(lambda: nc.vector.tensor_tensor(out=T(8), in0=lim, in1=lim, op=ALU.mult))
        V(lambda: nc.vector.tensor_tensor(out=T(7), in0=T(7), in1=T(8), op=ALU.add))
        V(lambda: nc.vector.reciprocal(out=T(7), in_=T(7)))
        V(lambda: nc.vector.tensor_scalar(out=T(8), in0=lam[:, l, 0, :], scalar1=-1.0, scalar2=None, op0=ALU.add),
          r=["tmpp", "lam"])
        V(lambda: nc.vector.tensor_tensor(out=T(1), in0=T(8), in1=T(0), op=ALU.mult))
        V(lambda: nc.vector.tensor_tensor(out=T(2), in0=lam[:, l, 1, :], in1=lim, op=ALU.mult), r=["tmpp", "lam"])
        V(lambda: nc.vector.tensor_tensor(out=T(1), in0=T(1), in1=T(2), op=ALU.add))
        V(lambda: nc.vector.tensor_tensor(out=lam[:, l, 2, :], in0=T(1), in1=T(7), op=ALU.mult), w=["lam", "tmpp"])
        V(lambda: nc.vector.tensor_tensor(out=T(1), in0=lam[:, l, 1, :], in1=T(0), op=ALU.mult), r=["tmpp", "lam"])
        V(lambda: nc.vector.tensor_tensor(out=T(2), in0=T(8), in1=lim, op=ALU.mult))
        V(lambda: nc.vector.tensor_tensor(out=T(1), in0=T(1), in1=T(2), op=ALU.subtract))
        V(lambda: nc.vector.tensor_tensor(out=lam[:, l, 3, :], in0=T(1), in1=T(7), op=ALU.mult), w=["lam", "tmpp"])
        V(lambda: nc.vector.tensor_scalar(out=lam[:, l, 4, :], in0=lam[:, l, 3, :], scalar1=-1.0, scalar2=None, op0=ALU.mult),
          r=["tmpp", "lam"], w=["lam", "tmpp"])
        V(lambda: nc.vector.tensor_scalar(out=lis[:, l, :, 0], in0=lam[:, l, 1, :], scalar1=-1.0, scalar2=None, op0=ALU.mult),
          r=["tmpp", "lam"], w=["lam", "tmpp"])
        V(lambda: nc.vector.tensor_copy(lis[:, l, :, 1], lam[:, l, 1, :]), r=["tmpp", "lam"], w=["lam", "tmpp"])

    B.barrier()
    setup_ph.close()
    wbf = [sb(f"wbf{i}", [128, 16, 128], BF16) for i in range(NWB)]
    bw_bf = sb("bw_bf", [128, 16, 2, 128], BF16)
    cw_bf = sb("cw_bf", [128, 64, 2, 32], BF16)
    wscr = nc.dram_tensor("wscr", [256, 128, 2048], BF16, kind="Internal").ap()
    s5scr = nc.dram_tensor("s5scr", [2, 2, 128, 4096], BF16, kind="Internal").ap()
    cur_ch = [0]
    tile_idx = [0]
    tmp_n = [0]

    def load_s5_weights(l):
        bwf = bw_bf[:].rearrange("p a b c -> p (a b c)")
        cwf = cw_bf[:].rearrange("p a b c -> p (a b c)")
        if cur_ch[0] > 0:
            dma("pool", bwf, s5scr[l, 0], r=[("s5scr", l)], w=["bw_bf"], chan="s5w")
            dma("pool", cwf, s5scr[l, 1], r=[("s5scr", l)], w=["cw_bf"], chan="s5w")
            return
        for ri in range(2):
            dma("pool", cw_bf[:, :, ri, :], cw[l, :, :, ri, :], w=["cw_bf"], chan="s5w")
        op("act", lambda: nc.scalar.activation(out=cw_bf[:, :, 1, :], in_=cw_bf[:, :, 1, :], func=AF.Copy, scale=-1.0),
           r=["cw_bf"], w=["cw_bf"])
        tp = ExitStack()
        tmp_n[0] += 1
        mk = lambda nm: tp.enter_context(nc.sbuf_tensor(f"tp_{nm}_{tmp_n[0]}", [128, 16, 128], F32))
        gl = [mk("glre"), mk("glim")]
        bwt = [mk("bwr"), mk("bwi")]
        Xd = mk("Xd")
        tm = mk("tm")
        for ri in range(2):
            dma("pool", bwt[ri][:], bw[l, :, :, ri, :], w=[("bwt", ri)], chan="s5w")
        for j in range(4):
            for gi in range(2):
                gcol = lam[:, l, 2 + gi, j:64:4]
                op("dve", lambda: nc.vector.tensor_tensor(out=Xd[:], in0=ident_f.unsqueeze(1).broadcast_to([128, 16, 128]),
                                                          in1=gcol.unsqueeze(2).broadcast_to([128, 16, 128]), op=ALU.mult),
                   r=["cst", "lam"], w=["Xd"])
                for q4 in range(4):
                    b_ = 4 * gi + q4
                    mm(ps[:, b_, :], ones_f, Xd[:, 4 * q4:4 * q4 + 4, :].rearrange("p a b -> p (a b)"), True, True,
                       r=["cst", "Xd"], w=[("ps", b_)])
                op("act", lambda: nc.scalar.activation(out=gl[gi][32 * j:32 * j + 32, :, :].rearrange("p a b -> p (a b)"),
                                                       in_=ps[32 * j:32 * j + 32, 4 * gi:4 * gi + 4, :].rearrange("p a b -> p (a b)"),
                                                       func=AF.Copy),
                   r=[("ps", 4 * gi + q4) for q4 in range(4)], w=[("gl", gi)])
        op("dve", lambda: nc.vector.tensor_tensor(out=Xd[:], in0=gl[0][:], in1=bwt[0][:], op=ALU.mult), r=[("gl", 0), ("bwt", 0)], w=["Xd"])
        op("dve", lambda: nc.vector.tensor_tensor(out=tm[:], in0=gl[1][:], in1=bwt[1][:], op=ALU.mult), r=[("gl", 1), ("bwt", 1)], w=["tm"])
        op("dve", lambda: nc.vector.tensor_tensor(out=bw_bf[:, :, 0, :], in0=Xd[:], in1=tm[:], op=ALU.subtract), r=["Xd", "tm"], w=["bw_bf"])
        op("dve", lambda: nc.vector.tensor_tensor(out=Xd[:], in0=gl[0][:], in1=bwt[1][:], op=ALU.mult), r=[("gl", 0), ("bwt", 1)], w=["Xd"])
        op("dve", lambda: nc.vector.tensor_tensor(out=tm[:], in0=gl[1][:], in1=bwt[0][:], op=ALU.mult), r=[("gl", 1), ("bwt", 0)], w=["tm"])
        op("dve", lambda: nc.vector.tensor_tensor(out=bw_bf[:, :, 1, :], in0=Xd[:], in1=tm[:], op=ALU.add), r=["Xd", "tm"], w=["bw_bf"])
        B.barrier()
        tp.close()
        dma("sp", s5scr[l, 0], bwf, r=["bw_bf"], w=[("s5scr", l)], chan="wsto")
        dma("sp", s5scr[l, 1], cwf, r=["cw_bf"], w=[("s5scr", l)], chan="wsto")

    xT = sb("xT", [128, KD, NTMAX])
    hT = sb("hT", [128, KD, NTMAX], BF16)
    yT = sb("yT", [128, KD, NTMAX], BF16)
    uT = sb("uT", [128, KD, NTMAX], BF16)
    gT = sb("gT", [128, KD, NTMAX], BF16)
    rstd = sb("rstd", [128, NTMAX])
    sq = [sb(f"sq{i}", [128, NTMAX]) for i in range(2)]
    t1 = sb("t1", [128, NTMAX])
    t2 = sb("t2", [128, NTMAX])
    s5c = sb("s5c", [128, 2, 64, 2])
    s5o = sb("s5o", [128, 2, 64])
    ST = sb("ST", [128, 2, 2048])
    ST_bf = sb("ST_bf", [128, 2048], BF16)
    convc = sb("convc", [128, 2, 32, 3])
    wdt2 = sb("wdt", [128, 2, 16, 32], BF16)
    ssq_acc = sb("ssq_acc", [128, NTMAX])
    phase_n = [0]

    def psb_(ph, name, shape, dt=F32):
        phase_n[0] += 1
        return ph.enter_context(nc.sbuf_tensor(f"ph_{name}_{phase_n[0]}", list(shape), dt))

    op("dve", lambda: nc.vector.memset(s5c[:], 0.0), w=["s5c"])
    op("dve", lambda: nc.vector.memset(ST[:], 0.0), w=["ST"])
    op("dve", lambda: nc.vector.memset(convc[:], 0.0), w=["convc"])

    wcnt = [0]

    def wtile(src, nk, ncols, scale=None):
        s = wcnt[0] % NWB
        wcnt[0] += 1
        idx = tile_idx[0]
        tile_idx[0] += 1
        out = wbf[s][:, 0:nk, 0:ncols]
        flat = wbf[s][:].rearrange("p a b -> p (a b)")
        if cur_ch[0] == 0:
            dma("pool", wbf[s][:], src.rearrange("(k p) c -> p k c", p=128), w=[("wbf", s)], chan=f"wl{s}")
            dma("sp", wscr[idx], flat, r=[("wbf", s)], w=[("wscr", idx)], chan="wsto")
        else:
            dma("pool", flat, wscr[idx], r=[("wscr", idx)], w=[("wbf", s)], chan=f"wl{s}")
        return out, ("wbf", s)

    def mm(out, lhsT, rhs, start, stop, r, w, **kw):
        op("pe", lambda: nc.tensor.matmul(out, lhsT, rhs, start=start, stop=stop, **kw), r=r, w=w)

    def proj(src, nk, rhs_t, rkey, NT, bank):
        wt, wk = wtile(src, nk, 128)
        outs = []
        for (c0, n, b) in ((0, NTP, bank), (NTP, NT - NTP, bank + 1)):
            if n <= 0:
                continue
            for k in range(nk):
                mm(ps[:, b, 0:n], wt[:, k, :], rhs_t[:, k, c0:c0 + n], k == 0, k == nk - 1, r=[wk, rkey], w=[("ps", b)])
            outs.append((ps[:, b, 0:n], c0, n, ("ps", b)))
        return outs

    def ssq_rstd(tag, blocks_fn, NT, width):
        for i in range(16):
            src, skey = blocks_fn(i)
            q = sq[i % 2]
            op("act", lambda: nc.scalar.activation(out=q[:, 0:NT], in_=src, func=AF.Square), r=[skey], w=[("sq", i % 2)])
            for (c0, n, b) in ((0, NTP, 6), (NTP, NT - NTP, 7)):
                if n > 0:
                    mm(ps[:, b, 0:n], ones_f, q[:, c0:c0 + n], i == 0, i == 15, r=[("sq", i % 2), "cst"], w=[("ps", b)])
        finish_rstd(NT, width)

    def finish_rstd(NT, width, src=None):
        for (c0, n, b) in ((0, NTP, 6), (NTP, NT - NTP, 7)):
            if n > 0 and src is not None:
                op("act", lambda: nc.scalar.activation(out=rstd[:, c0:c0 + n], in_=src[:, c0:c0 + n], func=AF.Sqrt,
                                                       scale=1.0 / width, bias=vec[:, VO["eps"]:VO["eps"] + 1]),
                   r=["ssq_acc", "vec"], w=["rstd"])
            elif n > 0:
                op("act", lambda: nc.scalar.activation(out=rstd[:, c0:c0 + n], in_=ps[:, b, 0:n], func=AF.Sqrt,
                                                       scale=1.0 / width, bias=vec[:, VO["eps"]:VO["eps"] + 1]),
                   r=[("ps", b), "vec"], w=["rstd"])
        op("dve", lambda: nc.vector.reciprocal(out=rstd[:, 0:NT], in_=rstd[:, 0:NT]), r=["rstd"], w=["rstd"])

    VO["eps"] = off

    for ch in range(NCH):
        NT = NTP + (NS if ch == 0 else 0)
        cur_ch[0] = ch
        tile_idx[0] = 0
        p0 = ch * NTP
        last = ch == NCH - 1
        units = [("p", 0, 128), ("p", 128, 128)] + ([("s", NTP, 64)] if ch == 0 else [])
        blocks = [("p", c, 0) for c in range(0, NTP, 32)] + ([("s", NTP, 0), ("s", NTP + 32, 8)] if ch == 0 else [])
        dma("sp", xT[:, :, 0:NTP], xpT[:, :, p0:p0 + NTP], w=["xT"], chan="xin")
        if ch == 0:
            dma("sp", xT[:, :, NTP:NT], xsT[:, :, :], w=["xT"], chan="xin")

        for l in range(2):
            ssq_rstd("n", lambda i: (xT[:, i, 0:NT], "xT"), NT, float(D))
            for k in range(KD):
                op("dve", lambda: nc.vector.scalar_tensor_tensor(out=hT[:, k, 0:NT], in0=xT[:, k, 0:NT],
                                                                 scalar=vcol("norm_w", l * 16 + k), in1=rstd[:, 0:NT],
                                                                 op0=ALU.mult, op1=ALU.mult),
                   r=["xT", "rstd", "vec"], w=["hT"])
            def out_proj_tile(row0, db, bank):
                for (pa, c0, n, pk) in proj(w_out[l][row0:row0 + 2048, db * 128:(db + 1) * 128], 16, yT, "yT", NT, bank):
                    op("dve", lambda: nc.vector.tensor_tensor(out=t2[:, c0:c0 + n], in0=pa, in1=rstd[:, c0:c0 + n], op=ALU.mult),
                       r=[pk, "rstd"], w=["t2"])
                    op("dve", lambda: nc.vector.tensor_tensor(out=xT[:, db, c0:c0 + n], in0=xT[:, db, c0:c0 + n], in1=t2[:, c0:c0 + n],
                                                              op=ALU.add), r=["t2", "xT"], w=["xT"])

            def out_proj(row0):
                for db in range(16):
                    out_proj_tile(row0, db, (db % 2) * 2)

            ph = ExitStack()
            cvs = psb_(ph, "cvs", [128, 32, NSQ, 3])
            raw2 = [psb_(ph, f"raw{i}", [128, 3 + NTP]) for i in range(2)]
            raws = psb_(ph, "raws", [128, NSQ, 7])
            dt_sb = psb_(ph, "dt_sb", [128, 32]); da_sb = psb_(ph, "da_sb", [128, 32]); cs_sb = psb_(ph, "cs_sb", [128, 32])
            ncs_sb = psb_(ph, "ncs_sb", [128, 32]); dte_sb = psb_(ph, "dte_sb", [128, 32]); dec_sb = psb_(ph, "dec_sb", [128, 32])
            dtd_sb = psb_(ph, "dtd_sb", [128, 32])
            Ecs_f = psb_(ph, "Ecs", [128, 2048], BF16)
            Lt2 = [psb_(ph, f"Lt{i}", [128, 4, 128], BF16) for i in range(2)]
            MT_f = psb_(ph, "MT", [128, 2048], BF16)
            CpT_f = psb_(ph, "CpT", [128, 2048], BF16)
            xd = psb_(ph, "xd", [128, 2048], BF16); xdp = psb_(ph, "xdp", [128, 2048], BF16)
            Btok = psb_(ph, "Btok", [128, 1024], BF16); Bm = psb_(ph, "Bm", [128, 1024], BF16)
            h0_2 = [psb_(ph, f"h0{i}", [128, 16, 128]) for i in range(2)]
            h0T_2 = [psb_(ph, "h0T", [128, 2048])] * 2
            h0T_bf = psb_(ph, "h0T_bf", [128, 2048], BF16)
            decT = psb_(ph, "decT", [128, NSQ, 16]); Rq = psb_(ph, "Rq", [128, 2, NSQ, 16])
            for ob in range(16):
                for (pa, c0, n, pk) in proj(w_in[l][:, 8192 + ob * 128: 8192 + (ob + 1) * 128], 16, hT, "hT", NT, (ob % 2) * 2):
                    op("act", lambda: nc.scalar.activation(out=yT[:, ob, c0:c0 + n], in_=pa, func=AF.Silu), r=[pk], w=["yT"])
            wdt = wdt2[:, l, :, :]
            if ch == 0:
                dma("pool", wdt, w_in[l][:, 10240:10272].rearrange("(k p) c -> p k c", p=128), w=["wdt"], chan="wdt")
            if ch == 0:
                dma("sp", cvs[:], conv_in[l], w=["cvs"], chan="cvin")
            for blk in range(32):
                dst_t, dblk = (uT, blk) if blk < 16 else (gT, blk - 16)
                wk_ = lambda k: vcol("conv_w", (l * 4 + k) * 32 + blk)
                bcol = vcol("conv_b", l * 32 + blk)
                for (pa, c0, n, pk) in proj(w_in[l][:, 4096 + blk * 128: 4096 + (blk + 1) * 128], 16, hT, "hT", NT, (blk % 2) * 2):
                    if c0 == 0:
                        raw, kraw = raw2[blk % 2], ("raw", blk % 2)
                        acc, kacc = (t1[:, 0:NTP], "t1") if blk % 2 == 0 else (sq[0][:, 0:NTP], ("sq", 0))
                        op("act", lambda: nc.scalar.activation(out=raw[:, 0:3], in_=convc[:, l, blk, :], func=AF.Copy),
                           r=["convc"], w=[kraw])
                        op("act", lambda: nc.scalar.activation(out=raw[:, 3:3 + NTP], in_=pa, func=AF.Copy), r=[pk], w=[kraw])
                        op("act", lambda: nc.scalar.activation(out=convc[:, l, blk, :], in_=raw[:, NTP:NTP + 3], func=AF.Copy),
                           r=[kraw], w=["convc"])
                        op("dve", lambda: nc.vector.tensor_scalar(out=acc, in0=raw[:, 3:3 + NTP], scalar1=wk_(3), scalar2=bcol,
                                                                  op0=ALU.mult, op1=ALU.add), r=[kraw, "vec"], w=[kacc])
                        for k in range(3):
                            op("dve", lambda: nc.vector.scalar_tensor_tensor(out=acc, in0=raw[:, k:k + NTP], scalar=wk_(k), in1=acc,
                                                                             op0=ALU.mult, op1=ALU.add), r=[kraw, "vec", kacc], w=[kacc])
                        op("act", lambda: nc.scalar.activation(out=dst_t[:, dblk, 0:NTP], in_=acc, func=AF.Silu), r=[kacc],
                           w=["uT" if blk < 16 else "gT"])
                    else:
                        op("act", lambda: nc.scalar.activation(out=raws[:, :, 0:3], in_=cvs[:, blk, :, :], func=AF.Copy),
                           r=["cvs"], w=["raws"])
                        op("act", lambda: nc.scalar.activation(out=raws[:, :, 3:7], in_=pa.rearrange("p (s t) -> p s t", t=4), func=AF.Copy),
                           r=[pk], w=["raws"])
                        op("act", lambda: nc.scalar.activation(out=cvs[:, blk, :, :], in_=raws[:, :, 4:7], func=AF.Copy),
                           r=["raws"], w=["cvs"])
                        acc = t2[:, 0:NS].rearrange("p (s t) -> p s t", t=4)
                        op("dve", lambda: nc.vector.tensor_scalar(out=acc, in0=raws[:, :, 3:7], scalar1=wk_(3), scalar2=bcol,
                                                                  op0=ALU.mult, op1=ALU.add), r=["raws", "vec"], w=["t2"])
                        for k in range(3):
                            op("dve", lambda: nc.vector.scalar_tensor_tensor(out=acc, in0=raws[:, :, k:k + 4], scalar=wk_(k), in1=acc,
                                                                             op0=ALU.mult, op1=ALU.add), r=["raws", "vec", "t2"], w=["t2"])
                        op("act", lambda: nc.scalar.activation(out=dst_t[:, dblk, NTP:NT], in_=t2[:, 0:NS], func=AF.Silu), r=["t2"],
                           w=["uT" if blk < 16 else "gT"])
            if ch == 0:
                dma("sp", convs_out[l], cvs[:], r=["cvs"], chan="cvout")
            if last:
                dma("sp", convp_out[l], convc[:, l, :, :], r=["convc"], chan="cvout")
            if ch == 0 and l == 0:
                dbg("xsT", uT[:, :, 0:NT], "uT", BF16)
                dbg("bcT", gT[:, :, 0:NT], "gT", BF16)
                dbg("zsT", yT[:, :, 0:NT], "yT", BF16)
            BTt = lambda g: gT[:, g, :]
            CTt = lambda g: gT[:, 8 + g, :]

            for (kind, c0, TT) in units:
                samp = kind == "s"
                maskT = maskT_s if samp else maskT_p
                EL = EL_s if samp else EL_p
                ngr = negr_b[0:TT, 1 if samp else 0, 0:4 * TT]
                for k in range(16):
                    mm(ps[0:TT, 4, 0:32], hT[:, k, c0:c0 + TT], wdt[:, k, :], k == 0, k == 15, r=["hT", "wdt"], w=[("ps", 4)])
                U = lambda t: t[0:TT, :]
                op("dve", lambda: nc.vector.tensor_tensor(out=U(dt_sb), in0=ps[0:TT, 4, 0:32], in1=bc[0:TT, l, 0, :], op=ALU.add),
                   r=[("ps", 4), "bc"], w=["dt_sb"])
                op("act", lambda: nc.scalar.activation(out=U(dt_sb), in_=U(dt_sb), func=AF.Exp), r=["dt_sb"], w=["dt_sb"])
                op("act", lambda: nc.scalar.activation(out=U(dt_sb), in_=U(dt_sb), func=AF.Ln, bias=vec[0:TT, VO["eps"] + 1:VO["eps"] + 2]),
                   r=["dt_sb", "vec"], w=["dt_sb"])
                op("dve", lambda: nc.vector.tensor_tensor(out=U(da_sb), in0=U(dt_sb), in1=a_bc[0:TT, l, :], op=ALU.mult),
                   r=["dt_sb", "a_bc"], w=["da_sb"])
                mm(ps[0:TT, 4, 32:64], maskT[0:TT, 0:TT], U(da_sb), True, True, r=["cst", "da_sb"], w=[("ps", 4)])
                op("act", lambda: nc.scalar.activation(out=U(cs_sb), in_=ps[0:TT, 4, 32:64], func=AF.Copy), r=[("ps", 4)], w=["cs_sb"])
                op("dve", lambda: nc.vector.tensor_scalar(out=U(ncs_sb), in0=U(cs_sb), scalar1=-1.0, scalar2=None, op0=ALU.mult),
                   r=["cs_sb"], w=["ncs_sb"])
                mm(ps[0:TT, 4, 64:96], EL[0:TT, 0:TT], U(cs_sb), True, True, r=["cst", "cs_sb"], w=[("ps", 4)])
                op("dve", lambda: nc.vector.tensor_tensor(out=U(dte_sb), in0=ps[0:TT, 4, 64:96], in1=U(cs_sb), op=ALU.subtract),
                   r=[("ps", 4), "cs_sb"], w=["dte_sb"])
                op("act", lambda: nc.scalar.activation(out=U(dte_sb), in_=U(dte_sb), func=AF.Exp), r=["dte_sb"], w=["dte_sb"])
                op("dve", lambda: nc.vector.tensor_tensor(out=U(dtd_sb), in0=U(dte_sb), in1=U(dt_sb), op=ALU.mult),
                   r=["dte_sb", "dt_sb"], w=["dtd_sb"])
                if not samp:
                    mm(ps[:, 4, 96:128], EL_p, cs_sb[:, :], True, True, r=["cst", "cs_sb"], w=[("ps", 4)])
                    op("act", lambda: nc.scalar.activation(out=dec_sb[:], in_=ps[:, 4, 96:128], func=AF.Exp), r=[("ps", 4)], w=["dec_sb"])
                psb = ps[:, 5:7, :].rearrange("p a b -> p (a b)").bitcast(BF16)
                for blk in range(16):
                    op("pe", lambda: nc.tensor.transpose(psb[0:TT, blk * 128:(blk + 1) * 128], uT[:, blk, c0:c0 + TT], ident_b[:, :]),
                       r=["uT", "ident_b"], w=[("ps", 5), ("ps", 6)])
                pv3 = psb[0:TT, :].rearrange("p (h d) -> p h d", d=64)
                op("dve", lambda: nc.vector.tensor_tensor(out=xd[0:TT, :].rearrange("p (h d) -> p h d", d=64), in0=pv3,
                                                          in1=U(dt_sb).unsqueeze(2).broadcast_to([TT, 32, 64]), op=ALU.mult),
                   r=[("ps", 5), ("ps", 6), "dt_sb"], w=["xd"])
                op("dve", lambda: nc.vector.tensor_tensor(out=xdp[0:TT, :].rearrange("p (h d) -> p h d", d=64), in0=pv3,
                                                          in1=U(dtd_sb).unsqueeze(2).broadcast_to([TT, 32, 64]), op=ALU.mult),
                   r=[("ps", 5), ("ps", 6), "dtd_sb"], w=["xdp"])
                psb7 = ps[:, 7, :].bitcast(BF16)
                for g in range(8):
                    op("pe", lambda: nc.tensor.transpose(psb7[0:TT, g * 128:(g + 1) * 128], BTt(g)[:, c0:c0 + TT], ident_b[:, :]),
                       r=["gT", "ident_b"], w=[("ps", 7)])
                op("act", lambda: nc.scalar.activation(out=Btok[0:TT, :], in_=psb7[0:TT, :], func=AF.Copy), r=[("ps", 7)], w=["Btok"])
                if not samp:
                    op("act", lambda: nc.scalar.activation(out=ST_bf[:], in_=ST[:, l, :], func=AF.Copy), r=["ST"], w=["ST_bf"])
                ystarted = set()

                def views(g):
                    hs = slice(4 * g, 4 * g + 4)
                    if samp:
                        vw = lambda t_, np_: t_[0:np_, :].rearrange("p (h t) -> p h t", t=64)[:, hs, :]
                    else:
                        o_ = (g % 4) * 512
                        vw = lambda t_, np_: t_[0:np_, o_:o_ + 512].rearrange("p (h t) -> p h t", t=128)
                    return hs, vw(Ecs_f, 128), vw(MT_f, TT), vw(CpT_f, 128)

                def stage1(g):
                    par = g % 2
                    hs, Ec, Mg, Cg = views(g)
                    bA = 2 if par == 0 else 5
                    Ltp = Lt2[par]
                    kl, ke = ("Lt", par), ("Ecs", g % 4)
                    for h4 in range(4):
                        mm(ps[:, bA, h4 * TT:(h4 + 1) * TT], cs_sb[0:TT, 4 * g + h4:4 * g + h4 + 1].broadcast_to([TT, 128]),
                           ident_f[0:TT, 0:TT], h4 == 0, False, r=["cst", "cs_sb"], w=[("ps", bA)], skip_group_check=True)
                    op("act", lambda: nc.scalar.activation(out=Ec, in_=ps[:, bA, 0:4 * TT].rearrange("p (h t) -> p h t", h=4), func=AF.Exp),
                       r=[("ps", bA)], w=[ke])
                    mm(ps[0:TT, bA, 0:4 * TT], ident_b[0:TT, 0:TT], ngr, False, True, r=["ident_b", "negr_b"], w=[("ps", bA)],
                       skip_group_check=True)
                    for h4 in range(4):
                        op("act", lambda: nc.scalar.activation(out=Ltp[0:TT, h4, 0:TT], in_=ps[0:TT, bA, h4 * TT:(h4 + 1) * TT], func=AF.Exp,
                                                               bias=ncs_sb[0:TT, 4 * g + h4:4 * g + h4 + 1]),
                           r=[("ps", bA), "ncs_sb"], w=[kl])
                    bC = 3 if par == 0 else 6
                    mm(ps[0:TT, bC, 0:TT], BTt(g)[:, c0:c0 + TT], CTt(g)[:, c0:c0 + TT], True, True, r=["gT"], w=[("ps", bC)],
                       skip_group_check=True)
                    if not samp:
                        mm(ps[:, bC, 128:384], Btok[0:TT, g * 128:(g + 1) * 128], xdp[0:TT, g * 256:(g + 1) * 256], False, True,
                           r=["Btok", "xdp"], w=[("ps", bC)], skip_group_check=True)

                def stage2(g):
                    par = g % 2
                    hs, Ec, Mg, Cg = views(g)
                    Ltp = Lt2[par]
                    kl, ke, km, kc = ("Lt", par), ("Ecs", g % 4), ("MT", g % 4), ("CpT", g % 4)
                    bC = 3 if par == 0 else 6
                    op("dve", lambda: nc.vector.tensor_tensor(out=Mg, in0=Ltp[0:TT, :, 0:TT],
                                                              in1=ps[0:TT, bC, 0:TT].unsqueeze(1).broadcast_to([TT, 4, TT]), op=ALU.mult),
                       r=[kl, ("ps", bC)], w=[km])
                    op("dve", lambda: nc.vector.tensor_tensor(out=Cg, in0=Ec,
                                                              in1=CTt(g)[:, c0:c0 + TT].unsqueeze(1).broadcast_to([128, 4, TT]), op=ALU.mult),
                       r=[ke, "gT"], w=[kc])
                    for h4 in range(4):
                        h = 4 * g + h4
                        hp, half = h // 2, h % 2
                        if samp:
                            o = ps[:, 0:2, :].rearrange("p a b -> p (a b)")[64 * half:64 * half + 64, hp * 64:(hp + 1) * 64]
                            okey = [("ps", 0), ("ps", 1)]
                            fk = (hp // 8, half)
                            st_flag = fk not in ystarted
                            ystarted.add(fk)
                        else:
                            o = ps[64 * half:64 * half + 64, 0, (hp % 2) * 128:(hp % 2) * 128 + 128]
                            okey = [("ps", 0)]
                            st_flag = True
                        mm(o, xd[0:TT, h * 64:(h + 1) * 64], Mg[:, h4, :], st_flag, False, r=["xd", km], w=okey,
                           tile_position=(0, 64 * half), skip_group_check=True)
                        if not samp:
                            mm(o, ST_bf[:, h * 64:(h + 1) * 64], Cg[:, h4, :], False, True, r=["ST_bf", kc], w=okey,
                               tile_position=(0, 64 * half))
                    if not samp:
                        sv = ST[:, l, g * 256:(g + 1) * 256].rearrange("p (h d) -> p h d", d=64)
                        op("dve", lambda: nc.vector.tensor_tensor(out=sv, in0=sv, in1=dec_sb[:, hs].unsqueeze(2).broadcast_to([128, 4, 64]),
                                                                  op=ALU.mult), r=["ST", "dec_sb", "ST_bf"], w=["ST"])
                        op("dve", lambda: nc.vector.tensor_tensor(out=ST[:, l, g * 256:(g + 1) * 256], in0=ST[:, l, g * 256:(g + 1) * 256],
                                                                  in1=ps[:, bC, 128:384], op=ALU.add), r=["ST", ("ps", bC)], w=["ST"])
                        ssd_tail(nc, op, mm, ps, g, c0, TT, l, uT, yT, t1, t2, sq, vcol, ones_f, ps[:, 0, 0:256], [("ps", 0)], ssq_acc)

                stage1(0)
                for g in range(8):
                    if g + 1 < 8:
                        stage1(g + 1)
                    stage2(g)
                if samp:
                    for h2 in range(2):
                        csv = cs_sb[0:64, :].rearrange("p (hp two) -> p two hp", two=2)[:, h2, :]
                        op("dve", lambda: nc.vector.tensor_tensor(out=Rq[0:64, h2, :, :], in0=csv.unsqueeze(1).broadcast_to([64, NSQ, 16]),
                                                                  in1=sellast[0:64, :].unsqueeze(2).broadcast_to([64, NSQ, 16]), op=ALU.mult),
                           r=["cs_sb", "cst"], w=["Rq"])
                        mm(ps[:, 5, 256 * h2:256 * h2 + 256], ones_f[0:64, :], Rq[0:64, h2, :, :].rearrange("p a b -> p (a b)"),
                           True, True, r=["cst", "Rq"], w=[("ps", 5)])
                    for h2 in range(2):
                        op("act", lambda: nc.scalar.activation(out=decT[64 * h2:64 * h2 + 64, :, :].rearrange("p a b -> p (a b)"),
                                                               in_=ps[64 * h2:64 * h2 + 64, 5, 256 * h2:256 * h2 + 256], func=AF.Exp),
                           r=[("ps", 5)], w=["decT"])
                    yall = ps[:, 0:2, :].rearrange("p a b -> p (a b)")
                    for q in range(NSQ):
                        h0, h0T = h0_2[q % 2], h0T_2[q % 2]
                        kh0, kh0T = ("h0", q % 2), "h0T"
                        dma("sp", h0T[:], ssd_inT[l, q], w=[kh0T], chan="h0T")
                        dma("sp", h0[:], ssd_in[l, q].rearrange("(hp two) d n -> (two d) hp n", two=2), w=[kh0], chan=f"h0{q % 2}")
                        op("act", lambda: nc.scalar.activation(out=h0T_bf[:], in_=h0T[:], func=AF.Copy), r=[kh0T], w=["h0T_bf"])
                        for h in range(32):
                            hp, half = h // 2, h % 2
                            o = yall[64 * half:64 * half + 64, hp * 64 + 4 * q: hp * 64 + 4 * q + 4]
                            mm(o, h0T_bf[:, h * 64:(h + 1) * 64], CpT_f[:, h * 64 + 4 * q: h * 64 + 4 * q + 4], False, q == NSQ - 1,
                               r=["h0T_bf"] + [("CpT", i) for i in range(4)], w=[("ps", 0), ("ps", 1)], tile_position=(0, 64 * half),
                               skip_group_check=True)
                        op("dve", lambda: nc.vector.tensor_scalar(out=Bm[0:64, :], in0=Btok[0:64, :], scalar1=selq[0:64, q:q + 1],
                                                                  scalar2=None, op0=ALU.mult), r=["Btok", "cst"], w=["Bm"])
                        for hp in range(16):
                            g = hp // 2
                            mm(ps[:, 4 + hp // 4, (hp % 4) * 128:(hp % 4) * 128 + 128], xdp[0:64, hp * 128:(hp + 1) * 128],
                               Bm[0:64, g * 128:(g + 1) * 128], True, True, r=["xdp", "Bm"], w=[("ps", 4 + hp // 4)])
                        upk = [("ps", 4), ("ps", 5), ("ps", 6), ("ps", 7)]
                        op("dve", lambda: nc.vector.tensor_tensor(out=h0[:], in0=h0[:],
                                                                  in1=decT[:, q, :].unsqueeze(2).broadcast_to([128, 16, 128]), op=ALU.mult),
                           r=[kh0, "decT"], w=[kh0])
                        op("dve", lambda: nc.vector.tensor_tensor(out=h0[:], in0=h0[:],
                                                                  in1=ps[:, 4:8, :].rearrange("p a (b n) -> p (a b) n", n=128), op=ALU.add),
                           r=[kh0] + upk, w=[kh0])
                        dma("sp", ssds_out[l, q].rearrange("(hp two) d n -> (two d) hp n", two=2), h0[:], r=[kh0], chan=f"h0o{q % 2}")
                    for g in range(8):
                        ssd_tail(nc, op, mm, ps, g, c0, TT, l, uT, yT, t1, t2, sq, vcol, ones_f,
                                 yall[:, g * 128:(g + 1) * 128], [("ps", 0), ("ps", 1)], ssq_acc)
            if last:
                dma("sp", ssdp_out[l], ST[:, l, :], r=["ST"], chan="stout")
            B.barrier()
            ph.close()
            finish_rstd(NT, float(W_A), ssq_acc)
            if ch == 0 and l == 0:
                dbg("ysT", yT[:, :, 0:NT], "yT", BF16)
                dbg("rstd_s", rstd[:, 0:NT], "rstd")
            for blk in range(16):
                for (pa, c0, n, pk) in proj(w_in[l][:, blk * 128:(blk + 1) * 128], 16, hT, "hT", NT, (blk % 2) * 2):
                    op("act", lambda: nc.scalar.activation(out=uT[:, blk, c0:c0 + n], in_=pa, func=AF.Copy),
                       r=[pk], w=["uT"])
            if ch == 0 and l == 0:
                dbg("hT", hT[:, :, 0:NT], "hT", BF16)
                dbg("uT", uT[:, :, 0:NT], "uT", BF16)
            load_s5_weights(l)
            ph = ExitStack()
            bu2 = [psb_(ph, f"bu{i}", [128, 64, 2, 32]) for i in range(2)]
            hbf2 = [psb_(ph, f"hbf{i}", [128, 64, 2, 32], BF16) for i in range(2)]
            s5s = psb_(ph, "s5s", [128, 64, 2, NSQ])
            s5t = [psb_(ph, f"s5t{i}", [128, 64, 2, 8]) for i in range(2)]
            hc = [psb_(ph, f"hc{i}", [128, 64, 2]) for i in range(6)]
            if ch == 0:
                for ri in range(2):
                    dma("sp", s5s[:, :, ri, :], s5in[l, ri], w=["s5s"], chan="s5s")
            lr_, li_ = lam[:, l, 0, :], lam[:, l, 1, :]
            za_todo = [("o", i) for i in range(16)] + [("z", i) for i in range(16)]
            nb = 32
            NHC = 6

            def stageA(k):
                kind, c0, q0 = blocks[k]
                bu = bu2[k % 2]
                for qd in range(4):
                    for bl in range(4):
                        blk = qd * 4 + bl
                        for j in range(4):
                            for ri in range(2):
                                o = ps[:, j, bl * 2 * nb + ri * nb: bl * 2 * nb + ri * nb + nb]
                                mm(o, bw_bf[32 * j:32 * j + 32, blk, ri, :], uT[32 * j:32 * j + 32, blk, c0:c0 + nb],
                                   True, True, r=["bw_bf", "uT"], w=[("ps", j)], tile_position=(32 * j, 0))
                    pv = ps[:, 0:4, 0:8 * nb].rearrange("p j (bl r t) -> p j bl r t", bl=4, r=2)
                    bv = bu[:, qd * 16:(qd + 1) * 16, :, :].rearrange("p (bl j) r t -> p j bl r t", j=4)
                    pk = [("ps", j) for j in range(4)]
                    for ri in range(2):
                        op("act", lambda: nc.scalar.activation(out=bv[:, :, :, ri, :], in_=pv[:, :, :, ri, :], func=AF.Copy),
                           r=pk, w=[("bu", k % 2)])

            def stageB(k):
                kind, c0, q0 = blocks[k]
                bu, hbf = bu2[k % 2], hbf2[k % 2]
                kb, kh = ("bu", k % 2), ("hbf", k % 2)
                if kind == "p":
                    shp = [128, 64, 2]
                    lr2 = lam[:, l, 0, :].unsqueeze(2).broadcast_to(shp)
                    li2 = lis[:, l, :, :]
                    tA = s5t[0][:].rearrange("p a b c -> p (a b c)")[:, 0:128].rearrange("p (a b) -> p a b", b=2)
                    tB = s5t[1][:].rearrange("p a b c -> p (a b c)")[:, 0:128].rearrange("p (a b) -> p a b", b=2)
                    for t in range(nb):
                        src_t = s5c[:, l, :, :] if t == 0 else hc[(t - 1) % NHC][:]
                        src_s = s5c[:, l, :, ::-1] if t == 0 else hc[(t - 1) % NHC][:, :, ::-1]
                        skey = "s5c" if t == 0 else ("hc", (t - 1) % NHC)
                        dst_t = s5c[:, l, :, :] if t == nb - 1 else hc[t % NHC][:]
                        dkey = "s5c" if t == nb - 1 else ("hc", t % NHC)
                        op("dve", lambda: nc.vector.tensor_tensor(out=tA, in0=src_t, in1=lr2, op=ALU.mult), r=[skey, "lam"], w=["s5t0"])
                        op("dve", lambda: nc.vector.tensor_tensor(out=tB, in0=src_s, in1=li2, op=ALU.mult), r=[skey, "lam"], w=["s5t1"])
                        op("dve", lambda: nc.vector.tensor_tensor(out=tA, in0=tA, in1=tB, op=ALU.add), r=["s5t0", "s5t1"], w=["s5t0"])
                        op("dve", lambda: nc.vector.tensor_tensor(out=dst_t, in0=tA, in1=bu[:, :, :, t], op=ALU.add), r=["s5t0", kb], w=[dkey])
                        op("act", lambda: nc.scalar.activation(out=hbf[:, :, :, t], in_=dst_t, func=AF.Copy), r=[dkey], w=[kh])
                    if last and c0 == NTP - nb:
                        op("act", lambda: nc.scalar.activation(out=s5o[:], in_=s5c[:, l, :, :].rearrange("p q r -> p r q"), func=AF.Copy),
                           r=["s5c"], w=["s5o"])
                        dma("sp", s5p_out[l].rearrange("r p q -> p r q"), s5o[:], r=["s5o"], chan="so")
                else:
                    shp = [128, 64, 2, 8]
                    lr2 = lam[:, l, 0, :].unsqueeze(2).unsqueeze(3).broadcast_to(shp)
                    li2 = lis[:, l, :, :].unsqueeze(3).broadcast_to(shp)
                    bu5 = bu[:].rearrange("p q r (s t) -> p q r s t", t=4)
                    col = lambda t: bu5[:, :, :, :, t]
                    cols_ = lambda t: bu5[:, :, ::-1, :, t]
                    prev0, prev0s = s5s[:, :, :, q0:q0 + 8], s5s[:, :, ::-1, q0:q0 + 8]
                    tA, tB = s5t[0][:], s5t[1][:]
                    for t in range(4):
                        pr = prev0 if t == 0 else col(t - 1)
                        prs = prev0s if t == 0 else cols_(t - 1)
                        rk = [kb, "s5s", "lam"]
                        op("dve", lambda: nc.vector.tensor_tensor(out=tA, in0=pr, in1=lr2, op=ALU.mult), r=rk, w=["s5t0"])
                        op("dve", lambda: nc.vector.tensor_tensor(out=tB, in0=prs, in1=li2, op=ALU.mult), r=rk, w=["s5t1"])
                        op("dve", lambda: nc.vector.tensor_tensor(out=tA, in0=tA, in1=tB, op=ALU.add), r=["s5t0", "s5t1"], w=["s5t0"])
                        op("dve", lambda: nc.vector.tensor_tensor(out=col(t), in0=col(t), in1=tA, op=ALU.add), r=["s5t0", kb], w=[kb])
                    op("act", lambda: nc.scalar.activation(out=s5s[:, :, :, q0:q0 + 8], in_=col(3), func=AF.Copy), r=[kb], w=["s5s"])
                    if q0 == 8:
                        for ri in range(2):
                            dma("sp", s5s_out[l, ri], s5s[:, :, ri, :], r=["s5s"], chan="so")
                    op("act", lambda: nc.scalar.activation(out=hbf[:], in_=bu[:], func=AF.Copy), r=[kb], w=[kh])

            def stageCmm(k):
                hbf = hbf2[k % 2]
                b = 4 + (k % 2)
                for blk in range(16):
                    for j in range(4):
                        pair = blk * 4 + j
                        for ri in range(2):
                            mm(ps[32 * j:32 * j + 32, b, blk * nb:(blk + 1) * nb], cw_bf[:, pair, ri, :], hbf[:, pair, ri, :], ri == 0, ri == 1,
                               r=["cw_bf", ("hbf", k % 2)], w=[("ps", b)], tile_position=(0, 32 * j), skip_group_check=True)

            def stageCdve(k):
                kind, c0, q0 = blocks[k]
                b = 4 + (k % 2)
                tmpv = s5t[1][:].rearrange("p a b c -> p (a b c)")[:, 0:16 * nb].rearrange("p (a t) -> p a t", t=nb)
                dcols = vec[:, VO["s5_d"] + l * 16: VO["s5_d"] + l * 16 + 16]
                op("dve", lambda: nc.vector.tensor_tensor(out=tmpv, in0=uT[:, :, c0:c0 + nb],
                                                          in1=dcols.unsqueeze(2).broadcast_to([128, 16, nb]), op=ALU.mult),
                   r=["uT", "vec"], w=["s5t1"])
                op("dve", lambda: nc.vector.tensor_tensor(out=gT[:, :, c0:c0 + nb], in0=tmpv,
                                                          in1=ps[:, b, :].rearrange("p (a t) -> p a t", t=nb), op=ALU.add),
                   r=["s5t1", ("ps", b)], w=["gT"])

            nblk = len(blocks)
            stageA(0)
            for k in range(nblk):
                if k + 1 < nblk:
                    stageA(k + 1)
                stageB(k)
                stageCmm(k)
                for _ in range(4):
                    if za_todo:
                        fk_, ob = za_todo.pop(0)
                        if fk_ == "o":
                            out_proj_tile(2048, ob, 6)
                            continue
                        for (pa, zc0, zn, pk) in proj(w_in[l][:, 2048 + ob * 128: 2048 + (ob + 1) * 128], 16, hT, "hT", NT, 6):
                            op("act", lambda: nc.scalar.activation(out=yT[:, ob, zc0:zc0 + zn], in_=pa, func=AF.Silu), r=[pk], w=["yT"])
                if k >= 1:
                    stageCdve(k - 1)
            stageCdve(nblk - 1)
            for blk in range(16):
                xg = gT[:, blk, 0:NT]
                w2 = t2[:, 0:NT] if blk % 2 == 0 else t1[:, 0:NT]
                wk2 = "t2" if blk % 2 == 0 else "t1"
                op("act", lambda: nc.scalar.activation(out=w2, in_=xg, func=AF.Square), r=["gT"], w=[wk2])
                op("dve", lambda: nc.vector.tensor_scalar(out=w2, in0=w2, scalar1=0.044715, scalar2=1.0, op0=ALU.mult, op1=ALU.add),
                   r=[wk2], w=[wk2])
                op("dve", lambda: nc.vector.tensor_tensor(out=w2, in0=w2, in1=xg, op=ALU.mult), r=[wk2, "gT"], w=[wk2])
                op("act", lambda: nc.scalar.activation(out=w2, in_=w2, func=AF.Sigmoid, scale=1.5957691216057308), r=[wk2], w=[wk2])
                op("dve", lambda: nc.vector.tensor_tensor(out=xg, in0=xg, in1=w2, op=ALU.mult), r=[wk2, "gT"], w=["gT"])
            if ch == 0 and l == 0:
                dbg("gT", gT[:, :, 0:NT], "gT", BF16)
            B.barrier()
            ph.close()
            assert not za_todo
            pend = None

            def flush_ssq():
                ob_, q_ = pend
                for (c0_, n_, b_) in ((0, NTP, 6), (NTP, NT - NTP, 7)):
                    if n_ > 0:
                        mm(ps[:, b_, 0:n_], ones_f, q_[:, c0_:c0_ + n_], ob_ == 0, ob_ == 15, r=[("sq", ob_ % 2), "cst"], w=[("ps", b_)])
            for ob in range(16):
                pouts = proj(glu_w[l][:, ob * 128:(ob + 1) * 128], 16, gT, "gT", NT, (ob % 2) * 2)
                if pend is not None:
                    flush_ssq()
                for (pa, c0, n, pk) in pouts:
                    op("act", lambda: nc.scalar.activation(out=t1[:, c0:c0 + n], in_=pa, func=AF.Sigmoid,
                                                           bias=vcol("glu_b", l * 16 + ob)), r=[pk, "vec"], w=["t1"])
                op("dve", lambda: nc.vector.tensor_tensor(out=t1[:, 0:NT], in0=t1[:, 0:NT], in1=gT[:, ob, 0:NT], op=ALU.mult),
                   r=["t1", "gT"], w=["t1"])
                op("dve", lambda: nc.vector.tensor_tensor(out=t1[:, 0:NT], in0=t1[:, 0:NT], in1=yT[:, ob, 0:NT], op=ALU.mult),
                   r=["t1", "yT"], w=["t1"])
                q = sq[ob % 2]
                op("act", lambda: nc.scalar.activation(out=q[:, 0:NT], in_=t1[:, 0:NT], func=AF.Square), r=["t1"], w=[("sq", ob % 2)])
                pend = (ob, q)
                op("dve", lambda: nc.vector.tensor_scalar(out=yT[:, ob, 0:NT], in0=t1[:, 0:NT], scalar1=vcol("s5_nw", l * 16 + ob),
                                                          scalar2=None, op0=ALU.mult), r=["t1", "vec"], w=["yT"])
            flush_ssq()
            finish_rstd(NT, float(W_A))
            if ch == 0 and l == 0:
                dbg("yaT", yT[:, :, 0:NT], "yT", BF16)
                dbg("rstd_a", rstd[:, 0:NT], "rstd")

            out_proj(0)
            if ch == 0 and l == 0:
                dbg("x1a", xT[:, :, 0:NT], "xT")


        ssq_rstd("f", lambda i: (xT[:, i, 0:NT], "xT"), NT, float(D))
        for k in range(KD):
            op("dve", lambda: nc.vector.scalar_tensor_tensor(out=xT[:, k, 0:NT], in0=xT[:, k, 0:NT], scalar=vcol("final_w", k),
                                                             in1=rstd[:, 0:NT], op0=ALU.mult, op1=ALU.mult),
               r=["xT", "rstd", "vec"], w=["xT"])
        dma("sp", yT_out[:, :, p0:p0 + NTP], xT[:, :, 0:NTP], r=["xT"], chan="yout")
        if ch == 0:
            dma("sp", yT_out[:, :, SEQ:SEQ + NS], xT[:, :, NTP:NT], r=["xT"], chan="yout")
    B.finish()


def ssd_tail(nc, op, mm, ps, g, c0, TT, l, uT, yT, t1, t2, sq, vcol, ones_f, ypsum, ykeys, ssq_acc):
    for j in range(2):
        hp = 2 * g + j
        yp = ypsum[:, j * TT:(j + 1) * TT]
        ya = t1[:, 0:TT]
        op("dve", lambda: nc.vector.scalar_tensor_tensor(out=ya, in0=uT[:, hp, c0:c0 + TT], scalar=vcol("ssd_d", l * 16 + hp), in1=yp,
                                                         op0=ALU.mult, op1=ALU.add), r=["uT", "vec"] + ykeys, w=["t1"])
        op("dve", lambda: nc.vector.tensor_tensor(out=ya, in0=ya, in1=yT[:, hp, c0:c0 + TT], op=ALU.mult), r=["t1", "yT"], w=["t1"])
        q = sq[hp % 2]
        op("act", lambda: nc.scalar.activation(out=q[:, 0:TT], in_=ya, func=AF.Square), r=["t1"], w=[("sq", hp % 2)])
        mm(ps[:, 7, 0:TT], ones_f, q[:, 0:TT], hp == 0, hp == 15, r=[("sq", hp % 2), "cst"], w=[("ps", 7)])
        if hp == 15:
            op("act", lambda: nc.scalar.activation(out=ssq_acc[:, c0:c0 + TT], in_=ps[:, 7, 0:TT], func=AF.Copy),
               r=[("ps", 7)], w=["ssq_acc"])
        op("dve", lambda: nc.vector.tensor_scalar(out=yT[:, hp, c0:c0 + TT], in0=ya, scalar1=vcol("ssd_nw", l * 16 + hp),
                                                  scalar2=None, op0=ALU.mult), r=["t1", "vec"], w=["yT"])


def _consts():
    cst = np.zeros((128, 672), np.float32)
    cst[:, 0:128] = np.eye(128)
    cst[:, 128:256] = 1.0
    s = np.arange(128)
    cst[:, 256:384] = (s[:, None] <= s[None, :])
    s6 = np.arange(64)
    cst[:64, 384:448] = (s6[:, None] <= s6[None, :]) & (s6[:, None] // 4 == s6[None, :] // 4)
    cst[127, 448:576] = 1.0
    cst[:64, 576:640] = (s6[:, None] == 4 * (s6[None, :] // 4) + 3)
    cst[:64, 640:656] = (s6[:, None] // 4 == np.arange(16)[None, :])
    cst[:64, 656:672] = (s6[:, None] == 4 * np.arange(16)[None, :] + 3)
    neg = np.zeros((128, 2, 512), np.float32)
    mp = np.where(s[:, None] <= s[None, :], 0.0, -30000.0)
    neg[:, 0, :] = np.tile(mp, (1, 4))
    ms = np.where((s6[:, None] <= s6[None, :]) & (s6[:, None] // 4 == s6[None, :] // 4), 0.0, -30000.0)
    neg[:64, 1, 0:256] = np.tile(ms, (1, 4))
    return cst, neg.astype(ml_dtypes.bfloat16)


def kernel(x_prompt, x_sample, state_s5_re, state_s5_im, state_ssd, cache_conv,
           norm_w, w_in, s5_lambda_re, s5_lambda_im, s5_log_step, s5_b_re, s5_b_im,
           s5_c_re, s5_c_im, s5_d, s5_glu_w, s5_glu_b, s5_norm_w,
           conv_w, conv_b, dt_bias, a_log, ssd_d, ssd_norm_w, w_out, final_norm_w):
    f = lambda a: np.ascontiguousarray(np.asarray(a, dtype=np.float32))
    x_prompt, x_sample = f(x_prompt), f(x_sample)
    state_ssd = f(state_ssd)
    cst, neg = _consts()

    def chan(v):
        v = f(v)
        return v.reshape(v.shape[:-1] + (v.shape[-1] // 128, 128))

    cols = []
    cols.append(chan(norm_w).transpose(2, 0, 1).reshape(128, -1))
    cols.append(chan(final_norm_w).T)
    cols.append(chan(s5_d).transpose(2, 0, 1).reshape(128, -1))
    cols.append(chan(s5_glu_b).transpose(2, 0, 1).reshape(128, -1))
    cols.append(chan(s5_norm_w).transpose(2, 0, 1).reshape(128, -1))
    cols.append(chan(conv_w).transpose(3, 0, 1, 2).reshape(128, -1))
    cols.append(chan(conv_b).transpose(2, 0, 1).reshape(128, -1))
    cols.append(chan(ssd_norm_w).transpose(2, 0, 1).reshape(128, -1))
    sd = f(ssd_d)
    sdl = np.repeat(sd.reshape(2, 16, 2).transpose(2, 0, 1)[:, None, :, :], 64, axis=1).reshape(128, 32)
    cols.append(sdl)
    cols.append(np.full((128, 1), EPS, np.float32))
    cols.append(np.ones((128, 1), np.float32))
    vecs = f(np.concatenate(cols, axis=1))
    bc32 = np.broadcast_to(np.stack([f(dt_bias), f(a_log)], axis=1)[None], (128, 2, 2, 32)).copy()
    def gp(v):
        return f(v).reshape(2, 64, 2, 64).transpose(0, 2, 3, 1).reshape(2, 128, 64)
    lst = np.broadcast_to(f(s5_log_step)[:, :, None], (2, 128, 64))
    s5par = f(np.stack([gp(s5_lambda_re), gp(s5_lambda_im), gp(lst)], axis=1))
    bw = np.zeros((2, 4, 2, 16, 16, 2, 2, 64), np.float32)
    for ri, bsrc in enumerate((f(s5_b_re), f(s5_b_im))):
        bb = bsrc.reshape(2, 16, 4, 2, 64, 16)
        for g2 in range(2):
            bw[:, :, g2, :, :, ri, g2, :] = bb[:, :, :, g2, :, :].transpose(0, 2, 4, 1, 3)
    bw = f(bw.reshape(2, 128, 16, 2, 128))
    cw = np.zeros((2, 2, 64, 64, 2, 2, 16), np.float32)
    for ri, csrc in enumerate((f(s5_c_re), f(s5_c_im))):
        cc = csrc.reshape(2, 64, 2, 16, 64)
        for g2 in range(2):
            cw[:, g2, :, :, ri, g2, :] = cc[:, :, g2, :, :].transpose(0, 3, 1, 2)
    cw = f(cw.reshape(2, 128, 64, 2, 32))

    w_in, glu_w, w_out = f(w_in), f(s5_glu_w), f(w_out)
    s5re, s5im = f(state_s5_re), f(state_s5_im)
    cache_conv = f(cache_conv)
    in_maps = []
    for c in range(8):
        b = c % 4
        sl = slice(c * NSQ, (c + 1) * NSQ)
        xpT = f(x_prompt[b].reshape(SEQ, 16, 128).transpose(2, 1, 0))
        xsT = f(x_sample[sl].reshape(NS, 16, 128).transpose(2, 1, 0))
        s5 = np.stack([s5re[:, sl], s5im[:, sl]], axis=1)
        s5 = s5.reshape(2, 2, NSQ, 64, 2, 64).transpose(0, 1, 4, 5, 3, 2).reshape(2, 2, 128, 64, NSQ)
        ssd_c = state_ssd[:, sl]
        ssdT = f(ssd_c.transpose(0, 1, 4, 2, 3).reshape(2, NSQ, 128, 2048))
        cv = cache_conv[:, sl].reshape(2, NSQ, 3, 32, 128).transpose(0, 4, 3, 1, 2)
        in_maps.append({
            "xpT": xpT, "xsT": xsT, "s5in": f(s5), "ssd_in": f(ssd_c), "ssd_inT": ssdT, "conv_in": f(cv),
            "w_in": w_in, "glu_w": glu_w, "w_out": w_out, "vecs": vecs, "bc32": f(bc32), "s5par": s5par,
            "bw": bw, "cw": cw, "cst": cst, "negrep": neg,
        })
    nc = build({"vecs": list(vecs.shape), "cst": list(cst.shape)})
    res = run_bass_kernel_spmd(nc, in_maps, core_ids=list(range(8))).results
    global LAST_RES
    LAST_RES = res

    B4 = 4
    y_prompt = np.zeros((B4, SEQ, D), np.float32)
    y_sample = np.zeros((128, 4, D), np.float32)
    p_re = np.zeros((2, B4, 128, 64), np.float32); p_im = np.zeros_like(p_re)
    p_ssd = np.zeros((2, B4, 32, 64, 128), np.float32)
    p_conv = np.zeros((2, B4, 3, 4096), np.float32)
    s_re = np.zeros((2, 128, 128, 64), np.float32); s_im = np.zeros_like(s_re)
    s_ssd = np.zeros((2, 128, 32, 64, 128), np.float32)
    s_conv = np.zeros((2, 128, 3, 4096), np.float32)
    for c in range(8):
        r = res[c]
        sl = slice(c * NSQ, (c + 1) * NSQ)
        yT = r["yT_out"]
        ytok = yT.transpose(2, 1, 0).reshape(SEQ + NS, D)
        y_sample[sl] = ytok[SEQ:].reshape(NSQ, 4, D)
        s5s = r["s5s_out"].reshape(2, 2, 2, 64, 64, NSQ).transpose(0, 1, 5, 4, 2, 3).reshape(2, 2, NSQ, 128, 64)
        s_re[:, sl], s_im[:, sl] = s5s[:, 0], s5s[:, 1]
        s_ssd[:, sl] = r["ssds_out"]
        s_conv[:, sl] = r["convs_out"].transpose(0, 3, 4, 2, 1).reshape(2, NSQ, 3, 4096)
        if c < 4:
            y_prompt[c] = ytok[:SEQ]
            s5p = r["s5p_out"].reshape(2, 2, 2, 64, 64).transpose(0, 1, 4, 2, 3).reshape(2, 2, 128, 64)
            p_re[:, c], p_im[:, c] = s5p[:, 0], s5p[:, 1]
            p_ssd[:, c] = r["ssdp_out"].reshape(2, 128, 32, 64).transpose(0, 2, 3, 1)
            p_conv[:, c] = r["convp_out"].transpose(0, 3, 2, 1).reshape(2, 3, 4096)
    return (y_prompt, y_sample, p_re, p_im, p_ssd, p_conv, s_re, s_im, s_ssd, s_conv)
```

```python
import bisect
import numpy as np
import ml_dtypes
import concourse.bass as bass
import concourse.mybir as mybir
from concourse.bass_utils import run_bass_kernel_spmd
from contextlib import ExitStack

F32 = mybir.dt.float32
BF16 = mybir.dt.bfloat16
DEBUG = False
LAST_RES = None
AF = mybir.ActivationFunctionType
ALU = mybir.AluOpType

D = 2048
KD = 16
SEQ = 2048
NSQ = 16
NCH = 8
NTP = SEQ // NCH
NS = 64
NTMAX = NTP + NS
SB = 64
NWB = 4
W_A = 2048
IN_COLS = 10272
EPS = 1e-5
TWO_PI_HI = 6.28125
TWO_PI_LO = 0.0019353071795864769


class Eng:
    def __init__(self, name, e, sem, compute):
        self.name, self.e, self.sem, self.compute = name, e, sem, compute
        self.idx = 0
        self.inc_idx = []
        self.last = None
        self.waited = {}


class Builder:
    def __init__(self, nc, es):
        self.nc, self.es = nc, es
        self.engs = {}
        for name, e, comp in (("pe", nc.tensor, True), ("act", nc.scalar, True), ("dve", nc.vector, True),
                              ("sp", nc.sync, False), ("pool", nc.gpsimd, False)):
            self.engs[name] = Eng(name, e, es.enter_context(nc.semaphore("sem_" + name)), comp)
        self.res = {}
        self.chans = {}

    def chan(self, name):
        if name not in self.chans:
            self.chans[name] = [self.es.enter_context(self.nc.semaphore("ch_" + name)), 0]
        return self.chans[name]

    def _need(self, waiter, dep):
        if dep is None:
            return
        if dep[0] == "dma":
            _, cname, n = dep
            n = self.chans[cname][1]
            key = "dma:" + cname
            if waiter.waited.get(key, 0) >= n:
                return
            waiter.waited[key] = n
            waiter.e.wait_ge(self.chans[cname][0], 16 * n)
            return
        pname, i = dep
        p = self.engs[pname]
        if waiter.name == "pe" and pname == "pe":
            return
        k = bisect.bisect_left(p.inc_idx, i)
        if k == len(p.inc_idx):
            p.last.then_inc(p.sem, 1)
            p.inc_idx.append(p.idx - 1)
        v = k + 1
        if waiter.waited.get(pname, 0) >= v:
            return
        waiter.waited[pname] = v
        waiter.e.wait_ge(p.sem, v)

    def _deps(self, waiter, r, w):
        for key in r:
            st = self.res.get(key)
            if st:
                self._need(waiter, st["w"])
        for key in w:
            st = self.res.get(key)
            if st:
                self._need(waiter, st["w"])
                for dep in st["r"].values():
                    self._need(waiter, dep)

    def _record(self, me, r, w):
        for key in r:
            st = self.res.setdefault(key, {"w": None, "r": {}})
            st["r"][me[0] if me[0] != "dma" else "dma:" + me[1]] = me
        for key in w:
            self.res[key] = {"w": me, "r": {}}

    def op(self, eng, fn, r=(), w=()):
        E = self.engs[eng]
        self._deps(E, r, w)
        ins = fn()
        E.last = ins
        me = (eng, E.idx)
        E.idx += 1
        self._record(me, r, w)
        return ins

    def dma(self, eng, out, in_, r=(), w=(), chan="d"):
        E = self.engs[eng]
        self._deps(E, r, w)
        c = self.chan(chan)
        E.e.dma_start(out=out, in_=in_).then_inc(c[0], 16)
        c[1] += 1
        self._record(("dma", chan, c[1]), r, w)

    def barrier(self):
        for W in self.engs.values():
            for p in ("pe", "act", "dve"):
                P = self.engs[p]
                if P.last is not None and not (W.name == "pe" and p == "pe"):
                    self._need(W, (p, P.idx - 1))
            for cname, c in self.chans.items():
                if c[1]:
                    self._need(W, ("dma", cname, c[1]))

    def finish(self):
        E = self.engs["sp"]
        for p in ("pe", "act", "dve"):
            P = self.engs[p]
            if P.last is not None:
                self._need(E, (p, P.idx - 1))
        for cname, c in self.chans.items():
            if c[1]:
                self._need(E, ("dma", cname, c[1]))


def build(const_shapes):
    nc = bass.Bass("TRN2", target_bir_lowering=False)
    es = ExitStack()
    with es:
        _build(nc, es, const_shapes)
    return nc


def _build(nc, es, const_shapes):
    def din(name, shape, dt=F32):
        return nc.dram_tensor(name, list(shape), dt, kind="ExternalInput").ap()

    def dout(name, shape):
        return nc.dram_tensor(name, list(shape), F32, kind="ExternalOutput").ap()

    xpT = din("xpT", [128, KD, SEQ])
    xsT = din("xsT", [128, KD, NS])
    s5in = din("s5in", [2, 2, 128, 64, NSQ])
    ssd_in = din("ssd_in", [2, NSQ, 32, 64, 128])
    ssd_inT = din("ssd_inT", [2, NSQ, 128, 2048])
    conv_in = din("conv_in", [2, 128, 32, NSQ, 3])
    w_in = din("w_in", [2, D, IN_COLS])
    glu_w = din("glu_w", [2, W_A, W_A])
    w_out = din("w_out", [2, 4096, D])
    vecs = din("vecs", const_shapes["vecs"])
    bc32 = din("bc32", [128, 2, 2, 32])
    s5par = din("s5par", [2, 3, 128, 64])
    bw = din("bw", [2, 128, 16, 2, 128])
    cw = din("cw", [2, 128, 64, 2, 32])
    cst = din("cst", const_shapes["cst"])
    negrep = din("negrep", [128, 2, 512], BF16)

    yT_out = dout("yT_out", [128, KD, SEQ + NS])
    s5p_out = dout("s5p_out", [2, 2, 128, 64])
    s5s_out = dout("s5s_out", [2, 2, 128, 64, NSQ])
    ssdp_out = dout("ssdp_out", [2, 128, 2048])
    ssds_out = dout("ssds_out", [2, NSQ, 32, 64, 128])
    convp_out = dout("convp_out", [2, 128, 32, 3])
    convs_out = dout("convs_out", [2, 128, 32, NSQ, 3])

    B = Builder(nc, es)
    op, dma = B.op, B.dma
    dbg_n = [0]

    def dbg(name, ap, key, dt=F32):
        if not DEBUG:
            return
        shp = list(ap.shape)
        t = nc.dram_tensor("dbg_" + name, shp, dt, kind="ExternalOutput").ap()
        dbg_n[0] += 1
        dma("sp", t, ap, r=[key], chan="dbg")

    def sb(name, shape, dt=F32):
        return es.enter_context(nc.sbuf_tensor("sb_" + name, list(shape), dt))

    ps = es.enter_context(nc.psum_tensor("ps", [128, 8, 512], F32))

    NVEC = const_shapes["vecs"][1]
    vec = sb("vec", [128, NVEC])
    dma("sp", vec[:], vecs[:, :], w=["vec"], chan="c0")
    NCST = const_shapes["cst"][1]
    cs_t = sb("cst", [128, NCST])
    dma("sp", cs_t[:], cst[:, :], w=["cst"], chan="c0")
    negr_b = sb("negr_b", [128, 2, 512], BF16)
    dma("sp", negr_b[:], negrep[:, :, :], w=["negr_b"], chan="c0")
    bc = sb("bc", [128, 2, 2, 32])
    dma("sp", bc[:], bc32[:, :, :, :], w=["bc"], chan="c0")
    ident_f = cs_t[:, 0:128]
    ones_f = cs_t[:, 128:256]
    maskT_p = cs_t[:, 256:384]
    maskT_s = cs_t[:, 384:448]
    EL_p = cs_t[:, 448:576]
    EL_s = cs_t[:, 576:640]
    selq = cs_t[:, 640:656]
    sellast = cs_t[:, 656:672]
    ident_b = sb("ident_b", [128, 128], BF16)
    op("act", lambda: nc.scalar.activation(out=ident_b[:], in_=ident_f, func=AF.Copy), r=["cst"], w=["ident_b"])

    VO = {}
    off = 0
    for nm, n in (("norm_w", 2 * 16), ("final_w", 16), ("s5_d", 2 * 16), ("glu_b", 2 * 16), ("s5_nw", 2 * 16),
                  ("conv_w", 2 * 4 * 32), ("conv_b", 2 * 32), ("ssd_nw", 2 * 16), ("ssd_d", 2 * 16)):
        VO[nm] = off
        off += n

    def vcol(nm, i):
        c = VO[nm] + i
        return vec[:, c:c + 1]

    a_bc = sb("a_bc", [128, 2, 32])
    op("act", lambda: nc.scalar.activation(out=a_bc[:], in_=bc[:, :, 1, :], func=AF.Exp), r=["bc"], w=["a_bc"])
    op("dve", lambda: nc.vector.tensor_scalar(out=a_bc[:], in0=a_bc[:], scalar1=-1.0, scalar2=None, op0=ALU.mult),
       r=["a_bc"], w=["a_bc"])

    lam = sb("lam", [128, 2, 5, 64])
    lis = sb("lis", [128, 2, 64, 2])
    setup_ph = ExitStack()
    sp_raw = setup_ph.enter_context(nc.sbuf_tensor("su_sp_raw", [128, 2, 3, 64], F32))
    for l in range(2):
        dma("sp", sp_raw[:, l, :, :], s5par[l].rearrange("k p q -> p k q"), w=["sp_raw"], chan="c0")
    tmpp = setup_ph.enter_context(nc.sbuf_tensor("su_tmpp", [128, 10, 64], F32))

    for l in range(2):
        T = lambda i: tmpp[:, i, :]
        lre, lim, lst = sp_raw[:, l, 0, :], sp_raw[:, l, 1, :], sp_raw[:, l, 2, :]
        k0 = ["tmpp"]

        def V(fn, r=k0, w=k0):
            op("dve", fn, r=list(r) + ["sp_raw"], w=w)

        def A(fn, r=k0, w=k0):
            op("act", fn, r=list(r) + ["sp_raw"], w=w)
        V(lambda: nc.vector.tensor_scalar(out=T(0), in0=lre, scalar1=-1e-4, scalar2=None, op0=ALU.min))
        A(lambda: nc.scalar.activation(out=T(1), in_=lst, func=AF.Exp))
        V(lambda: nc.vector.tensor_tensor(out=T(2), in0=T(0), in1=T(1), op=ALU.mult))
        A(lambda: nc.scalar.activation(out=T(2), in_=T(2), func=AF.Exp))
        V(lambda: nc.vector.tensor_tensor(out=T(3), in0=lim, in1=T(1), op=ALU.mult))
        V(lambda: nc.vector.tensor_scalar(out=T(4), in0=T(3), scalar1=float(1.0 / (2 * np.pi)), scalar2=12582912.0,
                                          op0=ALU.mult, op1=ALU.add))
        V(lambda: nc.vector.tensor_scalar(out=T(4), in0=T(4), scalar1=12582912.0, scalar2=None, op0=ALU.subtract))
        V(lambda: nc.vector.scalar_tensor_tensor(out=T(3), in0=T(4), scalar=-TWO_PI_HI, in1=T(3), op0=ALU.mult, op1=ALU.add))
        V(lambda: nc.vector.scalar_tensor_tensor(out=T(3), in0=T(4), scalar=-TWO_PI_LO, in1=T(3), op0=ALU.mult, op1=ALU.add))
        V(lambda: nc.vector.tensor_scalar(out=T(3), in0=T(3), scalar1=3.14159, scalar2=-3.14159, op0=ALU.min, op1=ALU.max))
        A(lambda: nc.scalar.activation(out=T(5), in_=T(3), func=AF.Sin))
        V(lambda: nc.vector.tensor_scalar(out=T(6), in0=T(3), scalar1=-1.0, scalar2=None, op0=ALU.mult))
        V(lambda: nc.vector.tensor_tensor(out=T(6), in0=T(6), in1=T(3), op=ALU.max))
        V(lambda: nc.vector.tensor_scalar(out=T(6), in0=T(6), scalar1=-1.0, scalar2=float(np.pi / 2), op0=ALU.mult, op1=ALU.add))
        A(lambda: nc.scalar.activation(out=T(6), in_=T(6), func=AF.Sin))
        V(lambda: nc.vector.tensor_tensor(out=lam[:, l, 0, :], in0=T(2), in1=T(6), op=ALU.mult), w=["lam", "tmpp"])
        V(lambda: nc.vector.tensor_tensor(out=lam[:, l, 1, :], in0=T(2), in1=T(5), op=ALU.mult), w=["lam", "tmpp"])
        V(lambda: nc.vector.tensor_tensor(out=T(7), in0=T(0), in1=T(0), op=ALU.mult))
        V(lambda: nc.vector.tensor_tensor(out=T(8), in0=lim, in1=lim, op=ALU.mult))
        V(lambda: nc.vector.tensor_tensor(out=T(7), in0=T(7), in1=T(8), op=ALU.add))
        V(lambda: nc.vector.reciprocal(out=T(7), in_=T(7)))
        V(lambda: nc.vector.tensor_scalar(out=T(8), in0=lam[:, l, 0, :], scalar1=-1.0, scalar2=None, op0=ALU.add),
          r=["tmpp", "lam"])
        V(lambda: nc.vector.tensor_tensor(out=T(1), in0=T(8), in1=T(0), op=ALU.mult))
        V(lambda: nc.vector.tensor_tensor(out=T(2), in0=lam[:, l, 1, :], in1=lim, op=ALU.mult), r=["tmpp", "lam"])
        V(lambda: nc.vector.tensor_tensor(out=T(1), in0=T(1), in1=T(2), op=ALU.add))
        V(lambda: nc.vector.tensor_tensor(out=lam[:, l, 2, :], in0=T(1), in1=T(7), op=ALU.mult), w=["lam", "tmpp"])
        V(lambda: nc.vector.tensor_tensor(out=T(1), in0=lam[:, l, 1, :], in1=T(0), op=ALU.mult), r=["tmpp", "lam"])
        V(lambda: nc.vector.tensor_tensor(out=T(2), in0=T(8), in1=lim, op=ALU.mult))
        V(lambda: nc.vector.tensor_tensor(out=T(1), in0=T(1), in1=T(2), op=ALU.subtract))
        V(lambda: nc.vector.tensor_tensor(out=lam[:, l, 3, :], in0=T(1), in1=T(7), op=ALU.mult), w=["lam", "tmpp"])
        V(lambda: nc.vector.tensor_scalar(out=lam[:, l, 4, :], in0=lam[:, l, 3, :], scalar1=-1.0, scalar2=None, op0=ALU.mult),
          r=["tmpp", "lam"], w=["lam", "tmpp"])
        V(lambda: nc.vector.tensor_scalar(out=lis[:, l, :, 0], in0=lam[:, l, 1, :], scalar1=-1.0, scalar2=None, op0=ALU.mult),
          r=["tmpp", "lam"], w=["lam", "tmpp"])
        V(lambda: nc.vector.tensor_copy(lis[:, l, :, 1], lam[:, l, 1, :]), r=["tmpp", "lam"], w=["lam", "tmpp"])

    B.barrier()
    setup_ph.close()
    wbf = [sb(f"wbf{i}", [128, 16, 128], BF16) for i in range(NWB)]
    bw_bf = sb("bw_bf", [128, 16, 2, 128], BF16)
    cw_bf = sb("cw_bf", [128, 64, 2, 32], BF16)
    wscr = nc.dram_tensor("wscr", [256, 128, 2048], BF16, kind="Internal").ap()
    s5scr = nc.dram_tensor("s5scr", [2, 2, 128, 4096], BF16, kind="Internal").ap()
    cur_ch = [0]
    tile_idx = [0]
    tmp_n = [0]

    def load_s5_weights(l):
        bwf = bw_bf[:].rearrange("p a b c -> p (a b c)")
        cwf = cw_bf[:].rearrange("p a b c -> p (a b c)")
        if cur_ch[0] > 0:
            dma("pool", bwf, s5scr[l, 0], r=[("s5scr", l)], w=["bw_bf"], chan="s5w")
            dma("pool", cwf, s5scr[l, 1], r=[("s5scr", l)], w=["cw_bf"], chan="s5w")
            return
        for ri in range(2):
            dma("pool", cw_bf[:, :, ri, :], cw[l, :, :, ri, :], w=["cw_bf"], chan="s5w")
        op("act", lambda: nc.scalar.activation(out=cw_bf[:, :, 1, :], in_=cw_bf[:, :, 1, :], func=AF.Copy, scale=-1.0),
           r=["cw_bf"], w=["cw_bf"])
        tp = ExitStack()
        tmp_n[0] += 1
        mk = lambda nm: tp.enter_context(nc.sbuf_tensor(f"tp_{nm}_{tmp_n[0]}", [128, 16, 128], F32))
        gl = [mk("glre"), mk("glim")]
        bwt = [mk("bwr"), mk("bwi")]
        Xd = mk("Xd")
        tm = mk("tm")
        for ri in range(2):
            dma("pool", bwt[ri][:], bw[l, :, :, ri, :], w=[("bwt", ri)], chan="s5w")
        for j in range(4):
            for gi in range(2):
                gcol = lam[:, l, 2 + gi, j:64:4]
                op("dve", lambda: nc.vector.tensor_tensor(out=Xd[:], in0=ident_f.unsqueeze(1).broadcast_to([128, 16, 128]),
                                                          in1=gcol.unsqueeze(2).broadcast_to([128, 16, 128]), op=ALU.mult),
                   r=["cst", "lam"], w=["Xd"])
                for q4 in range(4):
                    b_ = 4 * gi + q4
                    mm(ps[:, b_, :], ones_f, Xd[:, 4 * q4:4 * q4 + 4, :].rearrange("p a b -> p (a b)"), True, True,
                       r=["cst", "Xd"], w=[("ps", b_)])
                op("act", lambda: nc.scalar.activation(out=gl[gi][32 * j:32 * j + 32, :, :].rearrange("p a b -> p (a b)"),
                                                       in_=ps[32 * j:32 * j + 32, 4 * gi:4 * gi + 4, :].rearrange("p a b -> p (a b)"),
                                                       func=AF.Copy),
                   r=[("ps", 4 * gi + q4) for q4 in range(4)], w=[("gl", gi)])
        op("dve", lambda: nc.vector.tensor_tensor(out=Xd[:], in0=gl[0][:], in1=bwt[0][:], op=ALU.mult), r=[("gl", 0), ("bwt", 0)], w=["Xd"])
        op("dve", lambda: nc.vector.tensor_tensor(out=tm[:], in0=gl[1][:], in1=bwt[1][:], op=ALU.mult), r=[("gl", 1), ("bwt", 1)], w=["tm"])
        op("dve", lambda: nc.vector.tensor_tensor(out=bw_bf[:, :, 0, :], in0=Xd[:], in1=tm[:], op=ALU.subtract), r=["Xd", "tm"], w=["bw_bf"])
        op("dve", lambda: nc.vector.tensor_tensor(out=Xd[:], in0=gl[0][:], in1=bwt[1][:], op=ALU.mult), r=[("gl", 0), ("bwt", 1)], w=["Xd"])
        op("dve", lambda: nc.vector.tensor_tensor(out=tm[:], in0=gl[1][:], in1=bwt[0][:], op=ALU.mult), r=[("gl", 1), ("bwt", 0)], w=["tm"])
        op("dve", lambda: nc.vector.tensor_tensor(out=bw_bf[:, :, 1, :], in0=Xd[:], in1=tm[:], op=ALU.add), r=["Xd", "tm"], w=["bw_bf"])
        B.barrier()
        tp.close()
        dma("sp", s5scr[l, 0], bwf, r=["bw_bf"], w=[("s5scr", l)], chan="wsto")
        dma("sp", s5scr[l, 1], cwf, r=["cw_bf"], w=[("s5scr", l)], chan="wsto")

    xT = sb("xT", [128, KD, NTMAX])
    hT = sb("hT", [128, KD, NTMAX], BF16)
    yT = sb("yT", [128, KD, NTMAX], BF16)
    uT = sb("uT", [128, KD, NTMAX], BF16)
    gT = sb("gT", [128, KD, NTMAX], BF16)
    rstd = sb("rstd", [128, NTMAX])
    sq = [sb(f"sq{i}", [128, NTMAX]) for i in range(2)]
    t1 = sb("t1", [128, NTMAX])
    t2 = sb("t2", [128, NTMAX])
    s5c = sb("s5c", [128, 2, 64, 2])
    s5o = sb("s5o", [128, 2, 64])
    ST = sb("ST", [128, 2, 2048])
    ST_bf = sb("ST_bf", [128, 2048], BF16)
    convc = sb("convc", [128, 2, 32, 3])
    wdt2 = sb("wdt", [128, 2, 16, 32], BF16)
    ssq_acc = sb("ssq_acc", [128, NTMAX])
    phase_n = [0]

    def psb_(ph, name, shape, dt=F32):
        phase_n[0] += 1
        return ph.enter_context(nc.sbuf_tensor(f"ph_{name}_{phase_n[0]}", list(shape), dt))

    op("dve", lambda: nc.vector.memset(s5c[:], 0.0), w=["s5c"])
    op("dve", lambda: nc.vector.memset(ST[:], 0.0), w=["ST"])
    op("dve", lambda: nc.vector.memset(convc[:], 0.0), w=["convc"])

    wcnt = [0]

    def wtile(src, nk, ncols, scale=None):
        s = wcnt[0] % NWB
        wcnt[0] += 1
        idx = tile_idx[0]
        tile_idx[0] += 1
        out = wbf[s][:, 0:nk, 0:ncols]
        flat = wbf[s][:].rearrange("p a b -> p (a b)")
        if cur_ch[0] == 0:
            dma("pool", wbf[s][:], src.rearrange("(k p) c -> p k c", p=128), w=[("wbf", s)], chan=f"wl{s}")
            dma("sp", wscr[idx], flat, r=[("wbf", s)], w=[("wscr", idx)], chan="wsto")
        else:
            dma("pool", flat, wscr[idx], r=[("wscr", idx)], w=[("wbf", s)], chan=f"wl{s}")
        return out, ("wbf", s)

    def mm(out, lhsT, rhs, start, stop, r, w, **kw):
        op("pe", lambda: nc.tensor.matmul(out, lhsT, rhs, start=start, stop=stop, **kw), r=r, w=w)

    def proj(src, nk, rhs_t, rkey, NT, bank):
        wt, wk = wtile(src, nk, 128)
        outs = []
        for (c0, n, b) in ((0, NTP, bank), (NTP, NT - NTP, bank + 1)):
            if n <= 0:
                continue
            for k in range(nk):
                mm(ps[:, b, 0:n], wt[:, k, :], rhs_t[:, k, c0:c0 + n], k == 0, k == nk - 1, r=[wk, rkey], w=[("ps", b)])
            outs.append((ps[:, b, 0:n], c0, n, ("ps", b)))
        return outs

    def ssq_rstd(tag, blocks_fn, NT, width):
        for i in range(16):
            src, skey = blocks_fn(i)
            q = sq[i % 2]
            op("act", lambda: nc.scalar.activation(out=q[:, 0:NT], in_=src, func=AF.Square), r=[skey], w=[("sq", i % 2)])
            for (c0, n, b) in ((0, NTP, 6), (NTP, NT - NTP, 7)):
                if n > 0:
                    mm(ps[:, b, 0:n], ones_f, q[:, c0:c0 + n], i == 0, i == 15, r=[("sq", i % 2), "cst"], w=[("ps", b)])
        finish_rstd(NT, width)

    def finish_rstd(NT, width, src=None):
        for (c0, n, b) in ((0, NTP, 6), (NTP, NT - NTP, 7)):
            if n > 0 and src is not None:
                op("act", lambda: nc.scalar.activation(out=rstd[:, c0:c0 + n], in_=src[:, c0:c0 + n], func=AF.Sqrt,
                                                       scale=1.0 / width, bias=vec[:, VO["eps"]:VO["eps"] + 1]),
                   r=["ssq_acc", "vec"], w=["rstd"])
            elif n > 0:
                op("act", lambda: nc.scalar.activation(out=rstd[:, c0:c0 + n], in_=ps[:, b, 0:n], func=AF.Sqrt,
                                                       scale=1.0 / width, bias=vec[:, VO["eps"]:VO["eps"] + 1]),
                   r=[("ps", b), "vec"], w=["rstd"])
        op("dve", lambda: nc.vector.reciprocal(out=rstd[:, 0:NT], in_=rstd[:, 0:NT]), r=["rstd"], w=["rstd"])

    VO["eps"] = off

    for ch in range(NCH):
        NT = NTP + (NS if ch == 0 else 0)
        cur_ch[0] = ch
        tile_idx[0] = 0
        p0 = ch * NTP
        last = ch == NCH - 1
        units = [("p", 0, 128), ("p", 128, 128)] + ([("s", NTP, 64)] if ch == 0 else [])
        blocks = [("p", c, 0) for c in range(0, NTP, 32)] + ([("s", NTP, 0), ("s", NTP + 32, 8)] if ch == 0 else [])
        dma("sp", xT[:, :, 0:NTP], xpT[:, :, p0:p0 + NTP], w=["xT"], chan="xin")
        if ch == 0:
            dma("sp", xT[:, :, NTP:NT], xsT[:, :, :], w=["xT"], chan="xin")

        for l in range(2):
            ssq_rstd("n", lambda i: (xT[:, i, 0:NT], "xT"), NT, float(D))
            for k in range(KD):
                op("dve", lambda: nc.vector.scalar_tensor_tensor(out=hT[:, k, 0:NT], in0=xT[:, k, 0:NT],
                                                                 scalar=vcol("norm_w", l * 16 + k), in1=rstd[:, 0:NT],
                                                                 op0=ALU.mult, op1=ALU.mult),
                   r=["xT", "rstd", "vec"], w=["hT"])
            for blk in range(16):
                for (pa, c0, n, pk) in proj(w_in[l][:, blk * 128:(blk + 1) * 128], 16, hT, "hT", NT, (blk % 2) * 2):
                    op("act", lambda: nc.scalar.activation(out=uT[:, blk, c0:c0 + n], in_=pa, func=AF.Copy),
                       r=[pk], w=["uT"])
            if ch == 0 and l == 0:
                dbg("hT", hT[:, :, 0:NT], "hT", BF16)
                dbg("uT", uT[:, :, 0:NT], "uT", BF16)
            load_s5_weights(l)
            ph = ExitStack()
            bu2 = [psb_(ph, f"bu{i}", [128, 64, 2, 32]) for i in range(2)]
            hbf2 = [psb_(ph, f"hbf{i}", [128, 64, 2, 32], BF16) for i in range(2)]
            s5s = psb_(ph, "s5s", [128, 64, 2, NSQ])
            s5t = [psb_(ph, f"s5t{i}", [128, 64, 2, 8]) for i in range(2)]
            hc = [psb_(ph, f"hc{i}", [128, 64, 2]) for i in range(6)]
            if ch == 0:
                for ri in range(2):
                    dma("sp", s5s[:, :, ri, :], s5in[l, ri], w=["s5s"], chan="s5s")
            lr_, li_ = lam[:, l, 0, :], lam[:, l, 1, :]
            za_todo = list(range(16))
            nb = 32
            NHC = 6

            def stageA(k):
                kind, c0, q0 = blocks[k]
                bu = bu2[k % 2]
                for qd in range(4):
                    for bl in range(4):
                        blk = qd * 4 + bl
                        for j in range(4):
                            for ri in range(2):
                                o = ps[:, j, bl * 2 * nb + ri * nb: bl * 2 * nb + ri * nb + nb]
                                mm(o, bw_bf[32 * j:32 * j + 32, blk, ri, :], uT[32 * j:32 * j + 32, blk, c0:c0 + nb],
                                   True, True, r=["bw_bf", "uT"], w=[("ps", j)], tile_position=(32 * j, 0))
                    pv = ps[:, 0:4, 0:8 * nb].rearrange("p j (bl r t) -> p j bl r t", bl=4, r=2)
                    bv = bu[:, qd * 16:(qd + 1) * 16, :, :].rearrange("p (bl j) r t -> p j bl r t", j=4)
                    pk = [("ps", j) for j in range(4)]
                    for ri in range(2):
                        op("act", lambda: nc.scalar.activation(out=bv[:, :, :, ri, :], in_=pv[:, :, :, ri, :], func=AF.Copy),
                           r=pk, w=[("bu", k % 2)])

            def stageB(k):
                kind, c0, q0 = blocks[k]
                bu, hbf = bu2[k % 2], hbf2[k % 2]
                kb, kh = ("bu", k % 2), ("hbf", k % 2)
                if kind == "p":
                    shp = [128, 64, 2]
                    lr2 = lam[:, l, 0, :].unsqueeze(2).broadcast_to(shp)
                    li2 = lis[:, l, :, :]
                    tA = s5t[0][:].rearrange("p a b c -> p (a b c)")[:, 0:128].rearrange("p (a b) -> p a b", b=2)
                    tB = s5t[1][:].rearrange("p a b c -> p (a b c)")[:, 0:128].rearrange("p (a b) -> p a b", b=2)
                    for t in range(nb):
                        src_t = s5c[:, l, :, :] if t == 0 else hc[(t - 1) % NHC][:]
                        src_s = s5c[:, l, :, ::-1] if t == 0 else hc[(t - 1) % NHC][:, :, ::-1]
                        skey = "s5c" if t == 0 else ("hc", (t - 1) % NHC)
                        dst_t = s5c[:, l, :, :] if t == nb - 1 else hc[t % NHC][:]
                        dkey = "s5c" if t == nb - 1 else ("hc", t % NHC)
                        op("dve", lambda: nc.vector.tensor_tensor(out=tA, in0=src_t, in1=lr2, op=ALU.mult), r=[skey, "lam"], w=["s5t0"])
                        op("dve", lambda: nc.vector.tensor_tensor(out=tB, in0=src_s, in1=li2, op=ALU.mult), r=[skey, "lam"], w=["s5t1"])
                        op("dve", lambda: nc.vector.tensor_tensor(out=tA, in0=tA, in1=tB, op=ALU.add), r=["s5t0", "s5t1"], w=["s5t0"])
                        op("dve", lambda: nc.vector.tensor_tensor(out=dst_t, in0=tA, in1=bu[:, :, :, t], op=ALU.add), r=["s5t0", kb], w=[dkey])
                        op("act", lambda: nc.scalar.activation(out=hbf[:, :, :, t], in_=dst_t, func=AF.Copy), r=[dkey], w=[kh])
                    if last and c0 == NTP - nb:
                        op("act", lambda: nc.scalar.activation(out=s5o[:], in_=s5c[:, l, :, :].rearrange("p q r -> p r q"), func=AF.Copy),
                           r=["s5c"], w=["s5o"])
                        dma("sp", s5p_out[l].rearrange("r p q -> p r q"), s5o[:], r=["s5o"], chan="so")
                else:
                    shp = [128, 64, 2, 8]
                    lr2 = lam[:, l, 0, :].unsqueeze(2).unsqueeze(3).broadcast_to(shp)
                    li2 = lis[:, l, :, :].unsqueeze(3).broadcast_to(shp)
                    bu5 = bu[:].rearrange("p q r (s t) -> p q r s t", t=4)
                    col = lambda t: bu5[:, :, :, :, t]
                    cols_ = lambda t: bu5[:, :, ::-1, :, t]
                    prev0, prev0s = s5s[:, :, :, q0:q0 + 8], s5s[:, :, ::-1, q0:q0 + 8]
                    tA, tB = s5t[0][:], s5t[1][:]
                    for t in range(4):
                        pr = prev0 if t == 0 else col(t - 1)
                        prs = prev0s if t == 0 else cols_(t - 1)
                        rk = [kb, "s5s", "lam"]
                        op("dve", lambda: nc.vector.tensor_tensor(out=tA, in0=pr, in1=lr2, op=ALU.mult), r=rk, w=["s5t0"])
                        op("dve", lambda: nc.vector.tensor_tensor(out=tB, in0=prs, in1=li2, op=ALU.mult), r=rk, w=["s5t1"])
                        op("dve", lambda: nc.vector.tensor_tensor(out=tA, in0=tA, in1=tB, op=ALU.add), r=["s5t0", "s5t1"], w=["s5t0"])
                        op("dve", lambda: nc.vector.tensor_tensor(out=col(t), in0=col(t), in1=tA, op=ALU.add), r=["s5t0", kb], w=[kb])
                    op("act", lambda: nc.scalar.activation(out=s5s[:, :, :, q0:q0 + 8], in_=col(3), func=AF.Copy), r=[kb], w=["s5s"])
                    if q0 == 8:
                        for ri in range(2):
                            dma("sp", s5s_out[l, ri], s5s[:, :, ri, :], r=["s5s"], chan="so")
                    op("act", lambda: nc.scalar.activation(out=hbf[:], in_=bu[:], func=AF.Copy), r=[kb], w=[kh])

            def stageCmm(k):
                hbf = hbf2[k % 2]
                b = 4 + (k % 2)
                for blk in range(16):
                    for j in range(4):
                        pair = blk * 4 + j
                        for ri in range(2):
                            mm(ps[32 * j:32 * j + 32, b, blk * nb:(blk + 1) * nb], cw_bf[:, pair, ri, :], hbf[:, pair, ri, :], ri == 0, ri == 1,
                               r=["cw_bf", ("hbf", k % 2)], w=[("ps", b)], tile_position=(0, 32 * j), skip_group_check=True)

            def stageCdve(k):
                kind, c0, q0 = blocks[k]
                b = 4 + (k % 2)
                tmpv = s5t[1][:].rearrange("p a b c -> p (a b c)")[:, 0:16 * nb].rearrange("p (a t) -> p a t", t=nb)
                dcols = vec[:, VO["s5_d"] + l * 16: VO["s5_d"] + l * 16 + 16]
                op("dve", lambda: nc.vector.tensor_tensor(out=tmpv, in0=uT[:, :, c0:c0 + nb],
                                                          in1=dcols.unsqueeze(2).broadcast_to([128, 16, nb]), op=ALU.mult),
                   r=["uT", "vec"], w=["s5t1"])
                op("dve", lambda: nc.vector.tensor_tensor(out=gT[:, :, c0:c0 + nb], in0=tmpv,
                                                          in1=ps[:, b, :].rearrange("p (a t) -> p a t", t=nb), op=ALU.add),
                   r=["s5t1", ("ps", b)], w=["gT"])

            nblk = len(blocks)
            stageA(0)
            for k in range(nblk):
                if k + 1 < nblk:
                    stageA(k + 1)
                stageB(k)
                stageCmm(k)
                for _ in range(2):
                    if za_todo:
                        ob = za_todo.pop(0)
                        for (pa, zc0, zn, pk) in proj(w_in[l][:, 2048 + ob * 128: 2048 + (ob + 1) * 128], 16, hT, "hT", NT, 6):
                            op("act", lambda: nc.scalar.activation(out=yT[:, ob, zc0:zc0 + zn], in_=pa, func=AF.Silu), r=[pk], w=["yT"])
                if k >= 1:
                    stageCdve(k - 1)
            stageCdve(nblk - 1)
            for blk in range(16):
                xg = gT[:, blk, 0:NT]
                w2 = t2[:, 0:NT] if blk % 2 == 0 else t1[:, 0:NT]
                wk2 = "t2" if blk % 2 == 0 else "t1"
                op("act", lambda: nc.scalar.activation(out=w2, in_=xg, func=AF.Square), r=["gT"], w=[wk2])
                op("dve", lambda: nc.vector.tensor_scalar(out=w2, in0=w2, scalar1=0.044715, scalar2=1.0, op0=ALU.mult, op1=ALU.add),
                   r=[wk2], w=[wk2])
                op("dve", lambda: nc.vector.tensor_tensor(out=w2, in0=w2, in1=xg, op=ALU.mult), r=[wk2, "gT"], w=[wk2])
                op("act", lambda: nc.scalar.activation(out=w2, in_=w2, func=AF.Sigmoid, scale=1.5957691216057308), r=[wk2], w=[wk2])
                op("dve", lambda: nc.vector.tensor_tensor(out=xg, in0=xg, in1=w2, op=ALU.mult), r=[wk2, "gT"], w=["gT"])
            if ch == 0 and l == 0:
                dbg("gT", gT[:, :, 0:NT], "gT", BF16)
            B.barrier()
            ph.close()
            assert not za_todo
            pend = None

            def flush_ssq():
                ob_, q_ = pend
                for (c0_, n_, b_) in ((0, NTP, 6), (NTP, NT - NTP, 7)):
                    if n_ > 0:
                        mm(ps[:, b_, 0:n_], ones_f, q_[:, c0_:c0_ + n_], ob_ == 0, ob_ == 15, r=[("sq", ob_ % 2), "cst"], w=[("ps", b_)])
            for ob in range(16):
                pouts = proj(glu_w[l][:, ob * 128:(ob + 1) * 128], 16, gT, "gT", NT, (ob % 2) * 2)
                if pend is not None:
                    flush_ssq()
                for (pa, c0, n, pk) in pouts:
                    op("act", lambda: nc.scalar.activation(out=t1[:, c0:c0 + n], in_=pa, func=AF.Sigmoid,
                                                           bias=vcol("glu_b", l * 16 + ob)), r=[pk, "vec"], w=["t1"])
                op("dve", lambda: nc.vector.tensor_tensor(out=t1[:, 0:NT], in0=t1[:, 0:NT], in1=gT[:, ob, 0:NT], op=ALU.mult),
                   r=["t1", "gT"], w=["t1"])
                op("dve", lambda: nc.vector.tensor_tensor(out=t1[:, 0:NT], in0=t1[:, 0:NT], in1=yT[:, ob, 0:NT], op=ALU.mult),
                   r=["t1", "yT"], w=["t1"])
                q = sq[ob % 2]
                op("act", lambda: nc.scalar.activation(out=q[:, 0:NT], in_=t1[:, 0:NT], func=AF.Square), r=["t1"], w=[("sq", ob % 2)])
                pend = (ob, q)
                op("dve", lambda: nc.vector.tensor_scalar(out=yT[:, ob, 0:NT], in0=t1[:, 0:NT], scalar1=vcol("s5_nw", l * 16 + ob),
                                                          scalar2=None, op0=ALU.mult), r=["t1", "vec"], w=["yT"])
            flush_ssq()
            finish_rstd(NT, float(W_A))
            if ch == 0 and l == 0:
                dbg("yaT", yT[:, :, 0:NT], "yT", BF16)
                dbg("rstd_a", rstd[:, 0:NT], "rstd")

            def out_proj(row0):
                for db in range(16):
                    for (pa, c0, n, pk) in proj(w_out[l][row0:row0 + 2048, db * 128:(db + 1) * 128], 16, yT, "yT", NT, (db % 2) * 2):
                        op("dve", lambda: nc.vector.tensor_tensor(out=t2[:, c0:c0 + n], in0=pa, in1=rstd[:, c0:c0 + n], op=ALU.mult),
                           r=[pk, "rstd"], w=["t2"])
                        op("dve", lambda: nc.vector.tensor_tensor(out=xT[:, db, c0:c0 + n], in0=xT[:, db, c0:c0 + n], in1=t2[:, c0:c0 + n],
                                                                  op=ALU.add), r=["t2", "xT"], w=["xT"])
            out_proj(0)
            if ch == 0 and l == 0:
                dbg("x1a", xT[:, :, 0:NT], "xT")

            ph = ExitStack()
            cvs = psb_(ph, "cvs", [128, 32, NSQ, 3])
            raw2 = [psb_(ph, f"raw{i}", [128, 3 + NTP]) for i in range(2)]
            raws = psb_(ph, "raws", [128, NSQ, 7])
            dt_sb = psb_(ph, "dt_sb", [128, 32]); da_sb = psb_(ph, "da_sb", [128, 32]); cs_sb = psb_(ph, "cs_sb", [128, 32])
            ncs_sb = psb_(ph, "ncs_sb", [128, 32]); dte_sb = psb_(ph, "dte_sb", [128, 32]); dec_sb = psb_(ph, "dec_sb", [128, 32])
            dtd_sb = psb_(ph, "dtd_sb", [128, 32])
            Ecs_f = psb_(ph, "Ecs", [128, 2048], BF16)
            Lt2 = [psb_(ph, f"Lt{i}", [128, 4, 128], BF16) for i in range(2)]
            MT_f = psb_(ph, "MT", [128, 2048], BF16)
            CpT_f = psb_(ph, "CpT", [128, 2048], BF16)
            xd = psb_(ph, "xd", [128, 2048], BF16); xdp = psb_(ph, "xdp", [128, 2048], BF16)
            Btok = psb_(ph, "Btok", [128, 1024], BF16); Bm = psb_(ph, "Bm", [128, 1024], BF16)
            h0_2 = [psb_(ph, f"h0{i}", [128, 16, 128]) for i in range(2)]
            h0T_2 = [psb_(ph, "h0T", [128, 2048])] * 2
            h0T_bf = psb_(ph, "h0T_bf", [128, 2048], BF16)
            decT = psb_(ph, "decT", [128, NSQ, 16]); Rq = psb_(ph, "Rq", [128, 2, NSQ, 16])
            for ob in range(16):
                for (pa, c0, n, pk) in proj(w_in[l][:, 8192 + ob * 128: 8192 + (ob + 1) * 128], 16, hT, "hT", NT, (ob % 2) * 2):
                    op("act", lambda: nc.scalar.activation(out=yT[:, ob, c0:c0 + n], in_=pa, func=AF.Silu), r=[pk], w=["yT"])
            wdt = wdt2[:, l, :, :]
            if ch == 0:
                dma("pool", wdt, w_in[l][:, 10240:10272].rearrange("(k p) c -> p k c", p=128), w=["wdt"], chan="wdt")
            if ch == 0:
                dma("sp", cvs[:], conv_in[l], w=["cvs"], chan="cvin")
            for blk in range(32):
                dst_t, dblk = (uT, blk) if blk < 16 else (gT, blk - 16)
                wk_ = lambda k: vcol("conv_w", (l * 4 + k) * 32 + blk)
                bcol = vcol("conv_b", l * 32 + blk)
                for (pa, c0, n, pk) in proj(w_in[l][:, 4096 + blk * 128: 4096 + (blk + 1) * 128], 16, hT, "hT", NT, (blk % 2) * 2):
                    if c0 == 0:
                        raw, kraw = raw2[blk % 2], ("raw", blk % 2)
                        acc, kacc = (t1[:, 0:NTP], "t1") if blk % 2 == 0 else (sq[0][:, 0:NTP], ("sq", 0))
                        op("act", lambda: nc.scalar.activation(out=raw[:, 0:3], in_=convc[:, l, blk, :], func=AF.Copy),
                           r=["convc"], w=[kraw])
                        op("act", lambda: nc.scalar.activation(out=raw[:, 3:3 + NTP], in_=pa, func=AF.Copy), r=[pk], w=[kraw])
                        op("act", lambda: nc.scalar.activation(out=convc[:, l, blk, :], in_=raw[:, NTP:NTP + 3], func=AF.Copy),
                           r=[kraw], w=["convc"])
                        op("dve", lambda: nc.vector.tensor_scalar(out=acc, in0=raw[:, 3:3 + NTP], scalar1=wk_(3), scalar2=bcol,
                                                                  op0=ALU.mult, op1=ALU.add), r=[kraw, "vec"], w=[kacc])
                        for k in range(3):
                            op("dve", lambda: nc.vector.scalar_tensor_tensor(out=acc, in0=raw[:, k:k + NTP], scalar=wk_(k), in1=acc,
                                                                             op0=ALU.mult, op1=ALU.add), r=[kraw, "vec", kacc], w=[kacc])
                        op("act", lambda: nc.scalar.activation(out=dst_t[:, dblk, 0:NTP], in_=acc, func=AF.Silu), r=[kacc],
                           w=["uT" if blk < 16 else "gT"])
                    else:
                        op("act", lambda: nc.scalar.activation(out=raws[:, :, 0:3], in_=cvs[:, blk, :, :], func=AF.Copy),
                           r=["cvs"], w=["raws"])
                        op("act", lambda: nc.scalar.activation(out=raws[:, :, 3:7], in_=pa.rearrange("p (s t) -> p s t", t=4), func=AF.Copy),
                           r=[pk], w=["raws"])
                        op("act", lambda: nc.scalar.activation(out=cvs[:, blk, :, :], in_=raws[:, :, 4:7], func=AF.Copy),
                           r=["raws"], w=["cvs"])
                        acc = t2[:, 0:NS].rearrange("p (s t) -> p s t", t=4)
                        op("dve", lambda: nc.vector.tensor_scalar(out=acc, in0=raws[:, :, 3:7], scalar1=wk_(3), scalar2=bcol,
                                                                  op0=ALU.mult, op1=ALU.add), r=["raws", "vec"], w=["t2"])
                        for k in range(3):
                            op("dve", lambda: nc.vector.scalar_tensor_tensor(out=acc, in0=raws[:, :, k:k + 4], scalar=wk_(k), in1=acc,
                                                                             op0=ALU.mult, op1=ALU.add), r=["raws", "vec", "t2"], w=["t2"])
                        op("act", lambda: nc.scalar.activation(out=dst_t[:, dblk, NTP:NT], in_=t2[:, 0:NS], func=AF.Silu), r=["t2"],
                           w=["uT" if blk < 16 else "gT"])
            if ch == 0:
                dma("sp", convs_out[l], cvs[:], r=["cvs"], chan="cvout")
            if last:
                dma("sp", convp_out[l], convc[:, l, :, :], r=["convc"], chan="cvout")
            if ch == 0 and l == 0:
                dbg("xsT", uT[:, :, 0:NT], "uT", BF16)
                dbg("bcT", gT[:, :, 0:NT], "gT", BF16)
                dbg("zsT", yT[:, :, 0:NT], "yT", BF16)
            BTt = lambda g: gT[:, g, :]
            CTt = lambda g: gT[:, 8 + g, :]

            for (kind, c0, TT) in units:
                samp = kind == "s"
                maskT = maskT_s if samp else maskT_p
                EL = EL_s if samp else EL_p
                ngr = negr_b[0:TT, 1 if samp else 0, 0:4 * TT]
                for k in range(16):
                    mm(ps[0:TT, 4, 0:32], hT[:, k, c0:c0 + TT], wdt[:, k, :], k == 0, k == 15, r=["hT", "wdt"], w=[("ps", 4)])
                U = lambda t: t[0:TT, :]
                op("dve", lambda: nc.vector.tensor_tensor(out=U(dt_sb), in0=ps[0:TT, 4, 0:32], in1=bc[0:TT, l, 0, :], op=ALU.add),
                   r=[("ps", 4), "bc"], w=["dt_sb"])
                op("act", lambda: nc.scalar.activation(out=U(dt_sb), in_=U(dt_sb), func=AF.Exp), r=["dt_sb"], w=["dt_sb"])
                op("act", lambda: nc.scalar.activation(out=U(dt_sb), in_=U(dt_sb), func=AF.Ln, bias=vec[0:TT, VO["eps"] + 1:VO["eps"] + 2]),
                   r=["dt_sb", "vec"], w=["dt_sb"])
                op("dve", lambda: nc.vector.tensor_tensor(out=U(da_sb), in0=U(dt_sb), in1=a_bc[0:TT, l, :], op=ALU.mult),
                   r=["dt_sb", "a_bc"], w=["da_sb"])
                mm(ps[0:TT, 4, 32:64], maskT[0:TT, 0:TT], U(da_sb), True, True, r=["cst", "da_sb"], w=[("ps", 4)])
                op("act", lambda: nc.scalar.activation(out=U(cs_sb), in_=ps[0:TT, 4, 32:64], func=AF.Copy), r=[("ps", 4)], w=["cs_sb"])
                op("dve", lambda: nc.vector.tensor_scalar(out=U(ncs_sb), in0=U(cs_sb), scalar1=-1.0, scalar2=None, op0=ALU.mult),
                   r=["cs_sb"], w=["ncs_sb"])
                mm(ps[0:TT, 4, 64:96], EL[0:TT, 0:TT], U(cs_sb), True, True, r=["cst", "cs_sb"], w=[("ps", 4)])
                op("dve", lambda: nc.vector.tensor_tensor(out=U(dte_sb), in0=ps[0:TT, 4, 64:96], in1=U(cs_sb), op=ALU.subtract),
                   r=[("ps", 4), "cs_sb"], w=["dte_sb"])
                op("act", lambda: nc.scalar.activation(out=U(dte_sb), in_=U(dte_sb), func=AF.Exp), r=["dte_sb"], w=["dte_sb"])
                op("dve", lambda: nc.vector.tensor_tensor(out=U(dtd_sb), in0=U(dte_sb), in1=U(dt_sb), op=ALU.mult),
                   r=["dte_sb", "dt_sb"], w=["dtd_sb"])
                if not samp:
                    mm(ps[:, 4, 96:128], EL_p, cs_sb[:, :], True, True, r=["cst", "cs_sb"], w=[("ps", 4)])
                    op("act", lambda: nc.scalar.activation(out=dec_sb[:], in_=ps[:, 4, 96:128], func=AF.Exp), r=[("ps", 4)], w=["dec_sb"])
                psb = ps[:, 5:7, :].rearrange("p a b -> p (a b)").bitcast(BF16)
                for blk in range(16):
                    op("pe", lambda: nc.tensor.transpose(psb[0:TT, blk * 128:(blk + 1) * 128], uT[:, blk, c0:c0 + TT], ident_b[:, :]),
                       r=["uT", "ident_b"], w=[("ps", 5), ("ps", 6)])
                pv3 = psb[0:TT, :].rearrange("p (h d) -> p h d", d=64)
                op("dve", lambda: nc.vector.tensor_tensor(out=xd[0:TT, :].rearrange("p (h d) -> p h d", d=64), in0=pv3,
                                                          in1=U(dt_sb).unsqueeze(2).broadcast_to([TT, 32, 64]), op=ALU.mult),
                   r=[("ps", 5), ("ps", 6), "dt_sb"], w=["xd"])
                op("dve", lambda: nc.vector.tensor_tensor(out=xdp[0:TT, :].rearrange("p (h d) -> p h d", d=64), in0=pv3,
                                                          in1=U(dtd_sb).unsqueeze(2).broadcast_to([TT, 32, 64]), op=ALU.mult),
                   r=[("ps", 5), ("ps", 6), "dtd_sb"], w=["xdp"])
                psb7 = ps[:, 7, :].bitcast(BF16)
                for g in range(8):
                    op("pe", lambda: nc.tensor.transpose(psb7[0:TT, g * 128:(g + 1) * 128], BTt(g)[:, c0:c0 + TT], ident_b[:, :]),
                       r=["gT", "ident_b"], w=[("ps", 7)])
                op("act", lambda: nc.scalar.activation(out=Btok[0:TT, :], in_=psb7[0:TT, :], func=AF.Copy), r=[("ps", 7)], w=["Btok"])
                if not samp:
                    op("act", lambda: nc.scalar.activation(out=ST_bf[:], in_=ST[:, l, :], func=AF.Copy), r=["ST"], w=["ST_bf"])
                ystarted = set()

                def views(g):
                    hs = slice(4 * g, 4 * g + 4)
                    if samp:
                        vw = lambda t_, np_: t_[0:np_, :].rearrange("p (h t) -> p h t", t=64)[:, hs, :]
                    else:
                        o_ = (g % 4) * 512
                        vw = lambda t_, np_: t_[0:np_, o_:o_ + 512].rearrange("p (h t) -> p h t", t=128)
                    return hs, vw(Ecs_f, 128), vw(MT_f, TT), vw(CpT_f, 128)

                def stage1(g):
                    par = g % 2
                    hs, Ec, Mg, Cg = views(g)
                    bA = 2 if par == 0 else 5
                    Ltp = Lt2[par]
                    kl, ke = ("Lt", par), ("Ecs", g % 4)
                    for h4 in range(4):
                        mm(ps[:, bA, h4 * TT:(h4 + 1) * TT], cs_sb[0:TT, 4 * g + h4:4 * g + h4 + 1].broadcast_to([TT, 128]),
                           ident_f[0:TT, 0:TT], h4 == 0, False, r=["cst", "cs_sb"], w=[("ps", bA)], skip_group_check=True)
                    op("act", lambda: nc.scalar.activation(out=Ec, in_=ps[:, bA, 0:4 * TT].rearrange("p (h t) -> p h t", h=4), func=AF.Exp),
                       r=[("ps", bA)], w=[ke])
                    mm(ps[0:TT, bA, 0:4 * TT], ident_b[0:TT, 0:TT], ngr, False, True, r=["ident_b", "negr_b"], w=[("ps", bA)],
                       skip_group_check=True)
                    for h4 in range(4):
                        op("act", lambda: nc.scalar.activation(out=Ltp[0:TT, h4, 0:TT], in_=ps[0:TT, bA, h4 * TT:(h4 + 1) * TT], func=AF.Exp,
                                                               bias=ncs_sb[0:TT, 4 * g + h4:4 * g + h4 + 1]),
                           r=[("ps", bA), "ncs_sb"], w=[kl])
                    bC = 3 if par == 0 else 6
                    mm(ps[0:TT, bC, 0:TT], BTt(g)[:, c0:c0 + TT], CTt(g)[:, c0:c0 + TT], True, True, r=["gT"], w=[("ps", bC)],
                       skip_group_check=True)
                    if not samp:
                        mm(ps[:, bC, 128:384], Btok[0:TT, g * 128:(g + 1) * 128], xdp[0:TT, g * 256:(g + 1) * 256], False, True,
                           r=["Btok", "xdp"], w=[("ps", bC)], skip_group_check=True)

                def stage2(g):
                    par = g % 2
                    hs, Ec, Mg, Cg = views(g)
                    Ltp = Lt2[par]
                    kl, ke, km, kc = ("Lt", par), ("Ecs", g % 4), ("MT", g % 4), ("CpT", g % 4)
                    bC = 3 if par == 0 else 6
                    op("dve", lambda: nc.vector.tensor_tensor(out=Mg, in0=Ltp[0:TT, :, 0:TT],
                                                              in1=ps[0:TT, bC, 0:TT].unsqueeze(1).broadcast_to([TT, 4, TT]), op=ALU.mult),
                       r=[kl, ("ps", bC)], w=[km])
                    op("dve", lambda: nc.vector.tensor_tensor(out=Cg, in0=Ec,
                                                              in1=CTt(g)[:, c0:c0 + TT].unsqueeze(1).broadcast_to([128, 4, TT]), op=ALU.mult),
                       r=[ke, "gT"], w=[kc])
                    for h4 in range(4):
                        h = 4 * g + h4
                        hp, half = h // 2, h % 2
                        if samp:
                            o = ps[:, 0:2, :].rearrange("p a b -> p (a b)")[64 * half:64 * half + 64, hp * 64:(hp + 1) * 64]
                            okey = [("ps", 0), ("ps", 1)]
                            fk = (hp // 8, half)
                            st_flag = fk not in ystarted
                            ystarted.add(fk)
                        else:
                            o = ps[64 * half:64 * half + 64, 0, (hp % 2) * 128:(hp % 2) * 128 + 128]
                            okey = [("ps", 0)]
                            st_flag = True
                        mm(o, xd[0:TT, h * 64:(h + 1) * 64], Mg[:, h4, :], st_flag, False, r=["xd", km], w=okey,
                           tile_position=(0, 64 * half), skip_group_check=True)
                        if not samp:
                            mm(o, ST_bf[:, h * 64:(h + 1) * 64], Cg[:, h4, :], False, True, r=["ST_bf", kc], w=okey,
                               tile_position=(0, 64 * half))
                    if not samp:
                        sv = ST[:, l, g * 256:(g + 1) * 256].rearrange("p (h d) -> p h d", d=64)
                        op("dve", lambda: nc.vector.tensor_tensor(out=sv, in0=sv, in1=dec_sb[:, hs].unsqueeze(2).broadcast_to([128, 4, 64]),
                                                                  op=ALU.mult), r=["ST", "dec_sb", "ST_bf"], w=["ST"])
                        op("dve", lambda: nc.vector.tensor_tensor(out=ST[:, l, g * 256:(g + 1) * 256], in0=ST[:, l, g * 256:(g + 1) * 256],
                                                                  in1=ps[:, bC, 128:384], op=ALU.add), r=["ST", ("ps", bC)], w=["ST"])
                        ssd_tail(nc, op, mm, ps, g, c0, TT, l, uT, yT, t1, t2, sq, vcol, ones_f, ps[:, 0, 0:256], [("ps", 0)], ssq_acc)

                stage1(0)
                for g in range(8):
                    if g + 1 < 8:
                        stage1(g + 1)
                    stage2(g)
                if samp:
                    for h2 in range(2):
                        csv = cs_sb[0:64, :].rearrange("p (hp two) -> p two hp", two=2)[:, h2, :]
                        op("dve", lambda: nc.vector.tensor_tensor(out=Rq[0:64, h2, :, :], in0=csv.unsqueeze(1).broadcast_to([64, NSQ, 16]),
                                                                  in1=sellast[0:64, :].unsqueeze(2).broadcast_to([64, NSQ, 16]), op=ALU.mult),
                           r=["cs_sb", "cst"], w=["Rq"])
                        mm(ps[:, 5, 256 * h2:256 * h2 + 256], ones_f[0:64, :], Rq[0:64, h2, :, :].rearrange("p a b -> p (a b)"),
                           True, True, r=["cst", "Rq"], w=[("ps", 5)])
                    for h2 in range(2):
                        op("act", lambda: nc.scalar.activation(out=decT[64 * h2:64 * h2 + 64, :, :].rearrange("p a b -> p (a b)"),
                                                               in_=ps[64 * h2:64 * h2 + 64, 5, 256 * h2:256 * h2 + 256], func=AF.Exp),
                           r=[("ps", 5)], w=["decT"])
                    yall = ps[:, 0:2, :].rearrange("p a b -> p (a b)")
                    for q in range(NSQ):
                        h0, h0T = h0_2[q % 2], h0T_2[q % 2]
                        kh0, kh0T = ("h0", q % 2), "h0T"
                        dma("sp", h0T[:], ssd_inT[l, q], w=[kh0T], chan="h0T")
                        dma("sp", h0[:], ssd_in[l, q].rearrange("(hp two) d n -> (two d) hp n", two=2), w=[kh0], chan=f"h0{q % 2}")
                        op("act", lambda: nc.scalar.activation(out=h0T_bf[:], in_=h0T[:], func=AF.Copy), r=[kh0T], w=["h0T_bf"])
                        for h in range(32):
                            hp, half = h // 2, h % 2
                            o = yall[64 * half:64 * half + 64, hp * 64 + 4 * q: hp * 64 + 4 * q + 4]
                            mm(o, h0T_bf[:, h * 64:(h + 1) * 64], CpT_f[:, h * 64 + 4 * q: h * 64 + 4 * q + 4], False, q == NSQ - 1,
                               r=["h0T_bf"] + [("CpT", i) for i in range(4)], w=[("ps", 0), ("ps", 1)], tile_position=(0, 64 * half),
                               skip_group_check=True)
                        op("dve", lambda: nc.vector.tensor_scalar(out=Bm[0:64, :], in0=Btok[0:64, :], scalar1=selq[0:64, q:q + 1],
                                                                  scalar2=None, op0=ALU.mult), r=["Btok", "cst"], w=["Bm"])
                        for hp in range(16):
                            g = hp // 2
                            mm(ps[:, 4 + hp // 4, (hp % 4) * 128:(hp % 4) * 128 + 128], xdp[0:64, hp * 128:(hp + 1) * 128],
                               Bm[0:64, g * 128:(g + 1) * 128], True, True, r=["xdp", "Bm"], w=[("ps", 4 + hp // 4)])
                        upk = [("ps", 4), ("ps", 5), ("ps", 6), ("ps", 7)]
                        op("dve", lambda: nc.vector.tensor_tensor(out=h0[:], in0=h0[:],
                                                                  in1=decT[:, q, :].unsqueeze(2).broadcast_to([128, 16, 128]), op=ALU.mult),
                           r=[kh0, "decT"], w=[kh0])
                        op("dve", lambda: nc.vector.tensor_tensor(out=h0[:], in0=h0[:],
                                                                  in1=ps[:, 4:8, :].rearrange("p a (b n) -> p (a b) n", n=128), op=ALU.add),
                           r=[kh0] + upk, w=[kh0])
                        dma("sp", ssds_out[l, q].rearrange("(hp two) d n -> (two d) hp n", two=2), h0[:], r=[kh0], chan=f"h0o{q % 2}")
                    for g in range(8):
                        ssd_tail(nc, op, mm, ps, g, c0, TT, l, uT, yT, t1, t2, sq, vcol, ones_f,
                                 yall[:, g * 128:(g + 1) * 128], [("ps", 0), ("ps", 1)], ssq_acc)
            if last:
                dma("sp", ssdp_out[l], ST[:, l, :], r=["ST"], chan="stout")
            B.barrier()
            ph.close()
            finish_rstd(NT, float(W_A), ssq_acc)
            if ch == 0 and l == 0:
                dbg("ysT", yT[:, :, 0:NT], "yT", BF16)
                dbg("rstd_s", rstd[:, 0:NT], "rstd")
            out_proj(2048)
            if ch == 0 and l == 0:
                dbg("x1", xT[:, :, 0:NT], "xT")

        ssq_rstd("f", lambda i: (xT[:, i, 0:NT], "xT"), NT, float(D))
        for k in range(KD):
            op("dve", lambda: nc.vector.scalar_tensor_tensor(out=xT[:, k, 0:NT], in0=xT[:, k, 0:NT], scalar=vcol("final_w", k),
                                                             in1=rstd[:, 0:NT], op0=ALU.mult, op1=ALU.mult),
               r=["xT", "rstd", "vec"], w=["xT"])
        dma("sp", yT_out[:, :, p0:p0 + NTP], xT[:, :, 0:NTP], r=["xT"], chan="yout")
        if ch == 0:
            dma("sp", yT_out[:, :, SEQ:SEQ + NS], xT[:, :, NTP:NT], r=["xT"], chan="yout")
    B.finish()


def ssd_tail(nc, op, mm, ps, g, c0, TT, l, uT, yT, t1, t2, sq, vcol, ones_f, ypsum, ykeys, ssq_acc):
    for j in range(2):
        hp = 2 * g + j
        yp = ypsum[:, j * TT:(j + 1) * TT]
        ya = t1[:, 0:TT]
        op("dve", lambda: nc.vector.scalar_tensor_tensor(out=ya, in0=uT[:, hp, c0:c0 + TT], scalar=vcol("ssd_d", l * 16 + hp), in1=yp,
                                                         op0=ALU.mult, op1=ALU.add), r=["uT", "vec"] + ykeys, w=["t1"])
        op("dve", lambda: nc.vector.tensor_tensor(out=ya, in0=ya, in1=yT[:, hp, c0:c0 + TT], op=ALU.mult), r=["t1", "yT"], w=["t1"])
        q = sq[hp % 2]
        op("act", lambda: nc.scalar.activation(out=q[:, 0:TT], in_=ya, func=AF.Square), r=["t1"], w=[("sq", hp % 2)])
        mm(ps[:, 7, 0:TT], ones_f, q[:, 0:TT], hp == 0, hp == 15, r=[("sq", hp % 2), "cst"], w=[("ps", 7)])
        if hp == 15:
            op("act", lambda: nc.scalar.activation(out=ssq_acc[:, c0:c0 + TT], in_=ps[:, 7, 0:TT], func=AF.Copy),
               r=[("ps", 7)], w=["ssq_acc"])
        op("dve", lambda: nc.vector.tensor_scalar(out=yT[:, hp, c0:c0 + TT], in0=ya, scalar1=vcol("ssd_nw", l * 16 + hp),
                                                  scalar2=None, op0=ALU.mult), r=["t1", "vec"], w=["yT"])


def _consts():
    cst = np.zeros((128, 672), np.float32)
    cst[:, 0:128] = np.eye(128)
    cst[:, 128:256] = 1.0
    s = np.arange(128)
    cst[:, 256:384] = (s[:, None] <= s[None, :])
    s6 = np.arange(64)
    cst[:64, 384:448] = (s6[:, None] <= s6[None, :]) & (s6[:, None] // 4 == s6[None, :] // 4)
    cst[127, 448:576] = 1.0
    cst[:64, 576:640] = (s6[:, None] == 4 * (s6[None, :] // 4) + 3)
    cst[:64, 640:656] = (s6[:, None] // 4 == np.arange(16)[None, :])
    cst[:64, 656:672] = (s6[:, None] == 4 * np.arange(16)[None, :] + 3)
    neg = np.zeros((128, 2, 512), np.float32)
    mp = np.where(s[:, None] <= s[None, :], 0.0, -30000.0)
    neg[:, 0, :] = np.tile(mp, (1, 4))
    ms = np.where((s6[:, None] <= s6[None, :]) & (s6[:, None] // 4 == s6[None, :] // 4), 0.0, -30000.0)
    neg[:64, 1, 0:256] = np.tile(ms, (1, 4))
    return cst, neg.astype(ml_dtypes.bfloat16)


def kernel(x_prompt, x_sample, state_s5_re, state_s5_im, state_ssd, cache_conv,
           norm_w, w_in, s5_lambda_re, s5_lambda_im, s5_log_step, s5_b_re, s5_b_im,
           s5_c_re, s5_c_im, s5_d, s5_glu_w, s5_glu_b, s5_norm_w,
           conv_w, conv_b, dt_bias, a_log, ssd_d, ssd_norm_w, w_out, final_norm_w):
    f = lambda a: np.ascontiguousarray(np.asarray(a, dtype=np.float32))
    x_prompt, x_sample = f(x_prompt), f(x_sample)
    state_ssd = f(state_ssd)
    cst, neg = _consts()

    def chan(v):
        v = f(v)
        return v.reshape(v.shape[:-1] + (v.shape[-1] // 128, 128))

    cols = []
    cols.append(chan(norm_w).transpose(2, 0, 1).reshape(128, -1))
    cols.append(chan(final_norm_w).T)
    cols.append(chan(s5_d).transpose(2, 0, 1).reshape(128, -1))
    cols.append(chan(s5_glu_b).transpose(2, 0, 1).reshape(128, -1))
    cols.append(chan(s5_norm_w).transpose(2, 0, 1).reshape(128, -1))
    cols.append(chan(conv_w).transpose(3, 0, 1, 2).reshape(128, -1))
    cols.append(chan(conv_b).transpose(2, 0, 1).reshape(128, -1))
    cols.append(chan(ssd_norm_w).transpose(2, 0, 1).reshape(128, -1))
    sd = f(ssd_d)
    sdl = np.repeat(sd.reshape(2, 16, 2).transpose(2, 0, 1)[:, None, :, :], 64, axis=1).reshape(128, 32)
    cols.append(sdl)
    cols.append(np.full((128, 1), EPS, np.float32))
    cols.append(np.ones((128, 1), np.float32))
    vecs = f(np.concatenate(cols, axis=1))
    bc32 = np.broadcast_to(np.stack([f(dt_bias), f(a_log)], axis=1)[None], (128, 2, 2, 32)).copy()
    def gp(v):
        return f(v).reshape(2, 64, 2, 64).transpose(0, 2, 3, 1).reshape(2, 128, 64)
    lst = np.broadcast_to(f(s5_log_step)[:, :, None], (2, 128, 64))
    s5par = f(np.stack([gp(s5_lambda_re), gp(s5_lambda_im), gp(lst)], axis=1))
    bw = np.zeros((2, 4, 2, 16, 16, 2, 2, 64), np.float32)
    for ri, bsrc in enumerate((f(s5_b_re), f(s5_b_im))):
        bb = bsrc.reshape(2, 16, 4, 2, 64, 16)
        for g2 in range(2):
            bw[:, :, g2, :, :, ri, g2, :] = bb[:, :, :, g2, :, :].transpose(0, 2, 4, 1, 3)
    bw = f(bw.reshape(2, 128, 16, 2, 128))
    cw = np.zeros((2, 2, 64, 64, 2, 2, 16), np.float32)
    for ri, csrc in enumerate((f(s5_c_re), f(s5_c_im))):
        cc = csrc.reshape(2, 64, 2, 16, 64)
        for g2 in range(2):
            cw[:, g2, :, :, ri, g2, :] = cc[:, :, g2, :, :].transpose(0, 3, 1, 2)
    cw = f(cw.reshape(2, 128, 64, 2, 32))

    w_in, glu_w, w_out = f(w_in), f(s5_glu_w), f(w_out)
    s5re, s5im = f(state_s5_re), f(state_s5_im)
    cache_conv = f(cache_conv)
    in_maps = []
    for c in range(8):
        b = c % 4
        sl = slice(c * NSQ, (c + 1) * NSQ)
        xpT = f(x_prompt[b].reshape(SEQ, 16, 128).transpose(2, 1, 0))
        xsT = f(x_sample[sl].reshape(NS, 16, 128).transpose(2, 1, 0))
        s5 = np.stack([s5re[:, sl], s5im[:, sl]], axis=1)
        s5 = s5.reshape(2, 2, NSQ, 64, 2, 64).transpose(0, 1, 4, 5, 3, 2).reshape(2, 2, 128, 64, NSQ)
        ssd_c = state_ssd[:, sl]
        ssdT = f(ssd_c.transpose(0, 1, 4, 2, 3).reshape(2, NSQ, 128, 2048))
        cv = cache_conv[:, sl].reshape(2, NSQ, 3, 32, 128).transpose(0, 4, 3, 1, 2)
        in_maps.append({
            "xpT": xpT, "xsT": xsT, "s5in": f(s5), "ssd_in": f(ssd_c), "ssd_inT": ssdT, "conv_in": f(cv),
            "w_in": w_in, "glu_w": glu_w, "w_out": w_out, "vecs": vecs, "bc32": f(bc32), "s5par": s5par,
            "bw": bw, "cw": cw, "cst": cst, "negrep": neg,
        })
    nc = build({"vecs": list(vecs.shape), "cst": list(cst.shape)})
    res = run_bass_kernel_spmd(nc, in_maps, core_ids=list(range(8))).results
    global LAST_RES
    LAST_RES = res

    B4 = 4
    y_prompt = np.zeros((B4, SEQ, D), np.float32)
    y_sample = np.zeros((128, 4, D), np.float32)
    p_re = np.zeros((2, B4, 128, 64), np.float32); p_im = np.zeros_like(p_re)
    p_ssd = np.zeros((2, B4, 32, 64, 128), np.float32)
    p_conv = np.zeros((2, B4, 3, 4096), np.float32)
    s_re = np.zeros((2, 128, 128, 64), np.float32); s_im = np.zeros_like(s_re)
    s_ssd = np.zeros((2, 128, 32, 64, 128), np.float32)
    s_conv = np.zeros((2, 128, 3, 4096), np.float32)
    for c in range(8):
        r = res[c]
        sl = slice(c * NSQ, (c + 1) * NSQ)
        yT = r["yT_out"]
        ytok = yT.transpose(2, 1, 0).reshape(SEQ + NS, D)
        y_sample[sl] = ytok[SEQ:].reshape(NSQ, 4, D)
        s5s = r["s5s_out"].reshape(2, 2, 2, 64, 64, NSQ).transpose(0, 1, 5, 4, 2, 3).reshape(2, 2, NSQ, 128, 64)
        s_re[:, sl], s_im[:, sl] = s5s[:, 0], s5s[:, 1]
        s_ssd[:, sl] = r["ssds_out"]
        s_conv[:, sl] = r["convs_out"].transpose(0, 3, 4, 2, 1).reshape(2, NSQ, 3, 4096)
        if c < 4:
            y_prompt[c] = ytok[:SEQ]
            s5p = r["s5p_out"].reshape(2, 2, 2, 64, 64).transpose(0, 1, 4, 2, 3).reshape(2, 2, 128, 64)
            p_re[:, c], p_im[:, c] = s5p[:, 0], s5p[:, 1]
            p_ssd[:, c] = r["ssdp_out"].reshape(2, 128, 32, 64).transpose(0, 2, 3, 1)
            p_conv[:, c] = r["convp_out"].transpose(0, 3, 2, 1).reshape(2, 3, 4096)
    return (y_prompt, y_sample, p_re, p_im, p_ssd, p_conv, s_re, s_im, s_ssd, s_conv)
```

```python
import bisect
import numpy as np
import ml_dtypes
import concourse.bass as bass
import concourse.mybir as mybir
from concourse.bass_utils import run_bass_kernel_spmd
from contextlib import ExitStack

F32 = mybir.dt.float32
BF16 = mybir.dt.bfloat16
DEBUG = False
LAST_RES = None
AF = mybir.ActivationFunctionType
ALU = mybir.AluOpType

D = 2048
KD = 16
SEQ = 2048
NSQ = 16
NCH = 8
NTP = SEQ // NCH
NS = 64
NTMAX = NTP + NS
SB = 64
NWB = 6
W_A = 2048
IN_COLS = 10272
EPS = 1e-5
TWO_PI_HI = 6.28125
TWO_PI_LO = 0.0019353071795864769


class Eng:
    def __init__(self, name, e, sem, compute):
        self.name, self.e, self.sem, self.compute = name, e, sem, compute
        self.idx = 0
        self.inc_idx = []
        self.last = None
        self.waited = {}


class Builder:
    def __init__(self, nc, es):
        self.nc, self.es = nc, es
        self.engs = {}
        for name, e, comp in (("pe", nc.tensor, True), ("act", nc.scalar, True), ("dve", nc.vector, True),
                              ("sp", nc.sync, False), ("pool", nc.gpsimd, False)):
            self.engs[name] = Eng(name, e, es.enter_context(nc.semaphore("sem_" + name)), comp)
        self.res = {}
        self.chans = {}

    def chan(self, name):
        if name not in self.chans:
            self.chans[name] = [self.es.enter_context(self.nc.semaphore("ch_" + name)), 0]
        return self.chans[name]

    def _need(self, waiter, dep):
        if dep is None:
            return
        if dep[0] == "dma":
            _, cname, n = dep
            n = self.chans[cname][1]
            key = "dma:" + cname
            if waiter.waited.get(key, 0) >= n:
                return
            waiter.waited[key] = n
            waiter.e.wait_ge(self.chans[cname][0], 16 * n)
            return
        pname, i = dep
        p = self.engs[pname]
        if waiter.name == "pe" and pname == "pe":
            return
        k = bisect.bisect_left(p.inc_idx, i)
        if k == len(p.inc_idx):
            p.last.then_inc(p.sem, 1)
            p.inc_idx.append(p.idx - 1)
        v = k + 1
        if waiter.waited.get(pname, 0) >= v:
            return
        waiter.waited[pname] = v
        waiter.e.wait_ge(p.sem, v)

    def _deps(self, waiter, r, w):
        for key in r:
            st = self.res.get(key)
            if st:
                self._need(waiter, st["w"])
        for key in w:
            st = self.res.get(key)
            if st:
                self._need(waiter, st["w"])
                for dep in st["r"].values():
                    self._need(waiter, dep)

    def _record(self, me, r, w):
        for key in r:
            st = self.res.setdefault(key, {"w": None, "r": {}})
            st["r"][me[0] if me[0] != "dma" else "dma:" + me[1]] = me
        for key in w:
            self.res[key] = {"w": me, "r": {}}

    def op(self, eng, fn, r=(), w=()):
        E = self.engs[eng]
        self._deps(E, r, w)
        ins = fn()
        E.last = ins
        me = (eng, E.idx)
        E.idx += 1
        self._record(me, r, w)
        return ins

    def dma(self, eng, out, in_, r=(), w=(), chan="d"):
        E = self.engs[eng]
        self._deps(E, r, w)
        c = self.chan(chan)
        E.e.dma_start(out=out, in_=in_).then_inc(c[0], 16)
        c[1] += 1
        self._record(("dma", chan, c[1]), r, w)

    def barrier(self):
        for W in self.engs.values():
            for p in ("pe", "act", "dve"):
                P = self.engs[p]
                if P.last is not None and not (W.name == "pe" and p == "pe"):
                    self._need(W, (p, P.idx - 1))
            for cname, c in self.chans.items():
                if c[1]:
                    self._need(W, ("dma", cname, c[1]))

    def finish(self):
        E = self.engs["sp"]
        for p in ("pe", "act", "dve"):
            P = self.engs[p]
            if P.last is not None:
                self._need(E, (p, P.idx - 1))
        for cname, c in self.chans.items():
            if c[1]:
                self._need(E, ("dma", cname, c[1]))


def build(const_shapes):
    nc = bass.Bass("TRN2", target_bir_lowering=False)
    es = ExitStack()
    with es:
        _build(nc, es, const_shapes)
    return nc


def _build(nc, es, const_shapes):
    def din(name, shape, dt=F32):
        return nc.dram_tensor(name, list(shape), dt, kind="ExternalInput").ap()

    def dout(name, shape):
        return nc.dram_tensor(name, list(shape), F32, kind="ExternalOutput").ap()

    xpT = din("xpT", [128, KD, SEQ])
    xsT = din("xsT", [128, KD, NS])
    s5in = din("s5in", [2, 2, 128, 64, NSQ])
    ssd_in = din("ssd_in", [2, NSQ, 32, 64, 128])
    ssd_inT = din("ssd_inT", [2, NSQ, 128, 2048])
    conv_in = din("conv_in", [2, 128, 32, NSQ, 3])
    w_in = din("w_in", [2, D, IN_COLS])
    glu_w = din("glu_w", [2, W_A, W_A])
    w_out = din("w_out", [2, 4096, D])
    vecs = din("vecs", const_shapes["vecs"])
    bc32 = din("bc32", [128, 2, 2, 32])
    s5par = din("s5par", [2, 3, 128, 64])
    bw = din("bw", [2, 128, 16, 2, 128])
    cw = din("cw", [2, 128, 64, 2, 32])
    cst = din("cst", const_shapes["cst"])
    negrep = din("negrep", [128, 2, 512], BF16)

    yT_out = dout("yT_out", [128, KD, SEQ + NS])
    s5p_out = dout("s5p_out", [2, 2, 128, 64])
    s5s_out = dout("s5s_out", [2, 2, 128, 64, NSQ])
    ssdp_out = dout("ssdp_out", [2, 128, 2048])
    ssds_out = dout("ssds_out", [2, NSQ, 32, 64, 128])
    convp_out = dout("convp_out", [2, 128, 32, 3])
    convs_out = dout("convs_out", [2, 128, 32, NSQ, 3])

    B = Builder(nc, es)
    op, dma = B.op, B.dma
    dbg_n = [0]

    def dbg(name, ap, key, dt=F32):
        if not DEBUG:
            return
        shp = list(ap.shape)
        t = nc.dram_tensor("dbg_" + name, shp, dt, kind="ExternalOutput").ap()
        dbg_n[0] += 1
        dma("sp", t, ap, r=[key], chan="dbg")

    def sb(name, shape, dt=F32):
        return es.enter_context(nc.sbuf_tensor("sb_" + name, list(shape), dt))

    ps = es.enter_context(nc.psum_tensor("ps", [128, 8, 512], F32))

    NVEC = const_shapes["vecs"][1]
    vec = sb("vec", [128, NVEC])
    dma("sp", vec[:], vecs[:, :], w=["vec"], chan="c_vec")
    NCST = const_shapes["cst"][1]
    cs_t = sb("cst", [128, NCST])
    dma("sp", cs_t[:], cst[:, :], w=["cst"], chan="c_cst")
    negr_b = sb("negr_b", [128, 2, 512], BF16)
    dma("sp", negr_b[:], negrep[:, :, :], w=["negr_b"], chan="c_neg")
    bc = sb("bc", [128, 2, 2, 32])
    dma("sp", bc[:], bc32[:, :, :, :], w=["bc"], chan="c_bc")
    ident_f = cs_t[:, 0:128]
    ones_f = cs_t[:, 128:256]
    maskT_p = cs_t[:, 256:384]
    maskT_s = cs_t[:, 384:448]
    EL_p = cs_t[:, 448:576]
    EL_s = cs_t[:, 576:640]
    selq = cs_t[:, 640:656]
    sellast = cs_t[:, 656:672]
    ident_b = sb("ident_b", [128, 128], BF16)
    op("act", lambda: nc.scalar.activation(out=ident_b[:], in_=ident_f, func=AF.Copy), r=["cst"], w=["ident_b"])

    VO = {}
    off = 0
    for nm, n in (("norm_w", 2 * 16), ("final_w", 16), ("s5_d", 2 * 16), ("glu_b", 2 * 16), ("s5_nw", 2 * 16),
                  ("conv_w", 2 * 4 * 32), ("conv_b", 2 * 32), ("ssd_nw", 2 * 16), ("ssd_d", 2 * 16)):
        VO[nm] = off
        off += n

    def vcol(nm, i):
        c = VO[nm] + i
        return vec[:, c:c + 1]

    a_bc = sb("a_bc", [128, 2, 32])
    op("act", lambda: nc.scalar.activation(out=a_bc[:], in_=bc[:, :, 1, :], func=AF.Exp), r=["bc"], w=["a_bc"])
    op("dve", lambda: nc.vector.tensor_scalar(out=a_bc[:], in0=a_bc[:], scalar1=-1.0, scalar2=None, op0=ALU.mult),
       r=["a_bc"], w=["a_bc"])

    lam = sb("lam", [128, 2, 5, 64])
    lis = sb("lis", [128, 2, 64, 2])
    setup_ph = ExitStack()
    sp_raw = setup_ph.enter_context(nc.sbuf_tensor("su_sp_raw", [128, 2, 3, 64], F32))
    for l in range(2):
        dma("sp", sp_raw[:, l, :, :], s5par[l].rearrange("k p q -> p k q"), w=["sp_raw"], chan="c_sp")
    tmpp = setup_ph.enter_context(nc.sbuf_tensor("su_tmpp", [128, 10, 64], F32))

    for l in range(2):
        T = lambda i: tmpp[:, i, :]
        lre, lim, lst = sp_raw[:, l, 0, :], sp_raw[:, l, 1, :], sp_raw[:, l, 2, :]
        k0 = ["tmpp"]

        def V(fn, r=k0, w=k0):
            op("dve", fn, r=list(r) + ["sp_raw"], w=w)

        def A(fn, r=k0, w=k0):
            op("act", fn, r=list(r) + ["sp_raw"], w=w)
        V(lambda: nc.vector.tensor_scalar(out=T(0), in0=lre, scalar1=-1e-4, scalar2=None, op0=ALU.min))
        A(lambda: nc.scalar.activation(out=T(1), in_=lst, func=AF.Exp))
        V(lambda: nc.vector.tensor_tensor(out=T(2), in0=T(0), in1=T(1), op=ALU.mult))
        A(lambda: nc.scalar.activation(out=T(2), in_=T(2), func=AF.Exp))
        V(lambda: nc.vector.tensor_tensor(out=T(3), in0=lim, in1=T(1), op=ALU.mult))
        V(lambda: nc.vector.tensor_scalar(out=T(4), in0=T(3), scalar1=float(1.0 / (2 * np.pi)), scalar2=12582912.0,
                                          op0=ALU.mult, op1=ALU.add))
        V(lambda: nc.vector.tensor_scalar(out=T(4), in0=T(4), scalar1=12582912.0, scalar2=None, op0=ALU.subtract))
        V(lambda: nc.vector.scalar_tensor_tensor(out=T(3), in0=T(4), scalar=-TWO_PI_HI, in1=T(3), op0=ALU.mult, op1=ALU.add))
        V(lambda: nc.vector.scalar_tensor_tensor(out=T(3), in0=T(4), scalar=-TWO_PI_LO, in1=T(3), op0=ALU.mult, op1=ALU.add))
        V(lambda: nc.vector.tensor_scalar(out=T(3), in0=T(3), scalar1=3.14159, scalar2=-3.14159, op0=ALU.min, op1=ALU.max))
        A(lambda: nc.scalar.activation(out=T(5), in_=T(3), func=AF.Sin))
        V(lambda: nc.vector.tensor_scalar(out=T(6), in0=T(3), scalar1=-1.0, scalar2=None, op0=ALU.mult))
        V(lambda: nc.vector.tensor_tensor(out=T(6), in0=T(6), in1=T(3), op=ALU.max))
        V(lambda: nc.vector.tensor_scalar(out=T(6), in0=T(6), scalar1=-1.0, scalar2=float(np.pi / 2), op0=ALU.mult, op1=ALU.add))
        A(lambda: nc.scalar.activation(out=T(6), in_=T(6), func=AF.Sin))
        V(lambda: nc.vector.tensor_tensor(out=lam[:, l, 0, :], in0=T(2), in1=T(6), op=ALU.mult), w=["lam", "tmpp"])
        V(lambda: nc.vector.tensor_tensor(out=lam[:, l, 1, :], in0=T(2), in1=T(5), op=ALU.mult), w=["lam", "tmpp"])
        V(lambda: nc.vector.tensor_tensor(out=T(7), in0=T(0), in1=T(0), op=ALU.mult))
        V(lambda: nc.vector.tensor_tensor(out=T(8), in0=lim, in1=lim, op=ALU.mult))
        V(lambda: nc.vector.tensor_tensor(out=T(7), in0=T(7), in1=T(8), op=ALU.add))
        V(lambda: nc.vector.reciprocal(out=T(7), in_=T(7)))
        V(lambda: nc.vector.tensor_scalar(out=T(8), in0=lam[:, l, 0, :], scalar1=-1.0, scalar2=None, op0=ALU.add),
          r=["tmpp", "lam"])
        V(lambda: nc.vector.tensor_tensor(out=T(1), in0=T(8), in1=T(0), op=ALU.mult))
        V(lambda: nc.vector.tensor_tensor(out=T(2), in0=lam[:, l, 1, :], in1=lim, op=ALU.mult), r=["tmpp", "lam"])
        V(lambda: nc.vector.tensor_tensor(out=T(1), in0=T(1), in1=T(2), op=ALU.add))
        V(lambda: nc.vector.tensor_tensor(out=lam[:, l, 2, :], in0=T(1), in1=T(7), op=ALU.mult), w=["lam", "tmpp"])
        V(lambda: nc.vector.tensor_tensor(out=T(1), in0=lam[:, l, 1, :], in1=T(0), op=ALU.mult), r=["tmpp", "lam"])
        V(lambda: nc.vector.tensor_tensor(out=T(2), in0=T(8), in1=lim, op=ALU.mult))
        V(lambda: nc.vector.tensor_tensor(out=T(1), in0=T(1), in1=T(2), op=ALU.subtract))
        V(lambda: nc.vector.tensor_tensor(out=lam[:, l, 3, :], in0=T(1), in1=T(7), op=ALU.mult), w=["lam", "tmpp"])
        V(lambda: nc.vector.tensor_scalar(out=lam[:, l, 4, :], in0=lam[:, l, 3, :], scalar1=-1.0, scalar2=None, op0=ALU.mult),
          r=["tmpp", "lam"], w=["lam", "tmpp"])
        V(lambda: nc.vector.tensor_scalar(out=lis[:, l, :, 0], in0=lam[:, l, 1, :], scalar1=-1.0, scalar2=None, op0=ALU.mult),
          r=["tmpp", "lam"], w=["lam", "tmpp"])
        V(lambda: nc.vector.tensor_copy(lis[:, l, :, 1], lam[:, l, 1, :]), r=["tmpp", "lam"], w=["lam", "tmpp"])

    B.barrier()
    setup_ph.close()
    wbf = [sb(f"wbf{i}", [128, 16, 128], BF16) for i in range(NWB)]
    bw_bf = sb("bw_bf", [128, 16, 2, 128], BF16)
    cw_bf = sb("cw_bf", [128, 64, 2, 32], BF16)
    wscr = nc.dram_tensor("wscr", [256, 128, 2048], BF16, kind="Internal").ap()
    s5scr = nc.dram_tensor("s5scr", [2, 2, 128, 4096], BF16, kind="Internal").ap()
    cur_ch = [0]
    tile_idx = [0]
    tmp_n = [0]

    def load_s5_weights(l):
        bwf = bw_bf[:].rearrange("p a b c -> p (a b c)")
        cwf = cw_bf[:].rearrange("p a b c -> p (a b c)")
        if cur_ch[0] > 0:
            dma("pool", bwf, s5scr[l, 0], r=[("s5scr", l)], w=["bw_bf"], chan="s5w")
            dma("pool", cwf, s5scr[l, 1], r=[("s5scr", l)], w=["cw_bf"], chan="s5w")
            return
        for ri in range(2):
            dma("pool", cw_bf[:, :, ri, :], cw[l, :, :, ri, :], w=["cw_bf"], chan="s5cw")
        op("act", lambda: nc.scalar.activation(out=cw_bf[:, :, 1, :], in_=cw_bf[:, :, 1, :], func=AF.Copy, scale=-1.0),
           r=["cw_bf"], w=["cw_bf"])
        tp = ExitStack()
        tmp_n[0] += 1
        mk = lambda nm: tp.enter_context(nc.sbuf_tensor(f"tp_{nm}_{tmp_n[0]}", [128, 16, 128], F32))
        gl = [mk("glre"), mk("glim")]
        bwt = [mk("bwr"), mk("bwi")]
        Xd = mk("Xd")
        tm = mk("tm")
        for ri in range(2):
            dma("pool", bwt[ri][:], bw[l, :, :, ri, :], w=[("bwt", ri)], chan=f"s5bw{ri}")
        for j in range(4):
            for gi in range(2):
                gcol = lam[:, l, 2 + gi, j:64:4]
                op("dve", lambda: nc.vector.tensor_tensor(out=Xd[:], in0=ident_f.unsqueeze(1).broadcast_to([128, 16, 128]),
                                                          in1=gcol.unsqueeze(2).broadcast_to([128, 16, 128]), op=ALU.mult),
                   r=["cst", "lam"], w=["Xd"])
                for q4 in range(4):
                    b_ = 4 * gi + q4
                    mm(ps[:, b_, :], ones_f, Xd[:, 4 * q4:4 * q4 + 4, :].rearrange("p a b -> p (a b)"), True, True,
                       r=["cst", "Xd"], w=[("ps", b_)])
                op("act", lambda: nc.scalar.activation(out=gl[gi][32 * j:32 * j + 32, :, :].rearrange("p a b -> p (a b)"),
                                                       in_=ps[32 * j:32 * j + 32, 4 * gi:4 * gi + 4, :].rearrange("p a b -> p (a b)"),
                                                       func=AF.Copy),
                   r=[("ps", 4 * gi + q4) for q4 in range(4)], w=[("gl", gi)])
        op("dve", lambda: nc.vector.tensor_tensor(out=Xd[:], in0=gl[0][:], in1=bwt[0][:], op=ALU.mult), r=[("gl", 0), ("bwt", 0)], w=["Xd"])
        op("dve", lambda: nc.vector.tensor_tensor(out=tm[:], in0=gl[1][:], in1=bwt[1][:], op=ALU.mult), r=[("gl", 1), ("bwt", 1)], w=["tm"])
        op("dve", lambda: nc.vector.tensor_tensor(out=bw_bf[:, :, 0, :], in0=Xd[:], in1=tm[:], op=ALU.subtract), r=["Xd", "tm"], w=["bw_bf"])
        op("dve", lambda: nc.vector.tensor_tensor(out=Xd[:], in0=gl[0][:], in1=bwt[1][:], op=ALU.mult), r=[("gl", 0), ("bwt", 1)], w=["Xd"])
        op("dve", lambda: nc.vector.tensor_tensor(out=tm[:], in0=gl[1][:], in1=bwt[0][:], op=ALU.mult), r=[("gl", 1), ("bwt", 0)], w=["tm"])
        op("dve", lambda: nc.vector.tensor_tensor(out=bw_bf[:, :, 1, :], in0=Xd[:], in1=tm[:], op=ALU.add), r=["Xd", "tm"], w=["bw_bf"])
        B.barrier()
        tp.close()
        dma("sp", s5scr[l, 0], bwf, r=["bw_bf"], w=[("s5scr", l)], chan="wsto")
        dma("sp", s5scr[l, 1], cwf, r=["cw_bf"], w=[("s5scr", l)], chan="wsto")

    xT = sb("xT", [128, KD, NTMAX])
    hT = sb("hT", [128, KD, NTMAX], BF16)
    yT = sb("yT", [128, KD, NTMAX], BF16)
    uT = sb("uT", [128, KD, NTMAX], BF16)
    gT = sb("gT", [128, KD, NTMAX], BF16)
    rstd = sb("rstd", [128, NTMAX])
    sq = [sb(f"sq{i}", [128, NTMAX]) for i in range(2)]
    t1 = sb("t1", [128, NTMAX])
    t2 = sb("t2", [128, NTMAX])
    s5c = sb("s5c", [128, 2, 64, 2])
    s5o = sb("s5o", [128, 2, 64])
    ST = sb("ST", [128, 2, 2048])
    ST_bf = sb("ST_bf", [128, 2048], BF16)
    convc = sb("convc", [128, 2, 32, 3])
    wdt2 = sb("wdt", [128, 2, 16, 32], BF16)
    phase_n = [0]

    def psb_(ph, name, shape, dt=F32):
        phase_n[0] += 1
        return ph.enter_context(nc.sbuf_tensor(f"ph_{name}_{phase_n[0]}", list(shape), dt))

    op("dve", lambda: nc.vector.memset(s5c[:], 0.0), w=["s5c"])
    op("dve", lambda: nc.vector.memset(ST[:], 0.0), w=["ST"])
    op("dve", lambda: nc.vector.memset(convc[:], 0.0), w=["convc"])

    wcnt = [0]

    def wtile(src, nk, ncols, scale=None):
        s = wcnt[0] % NWB
        wcnt[0] += 1
        idx = tile_idx[0]
        tile_idx[0] += 1
        out = wbf[s][:, 0:nk, 0:ncols]
        flat = wbf[s][:].rearrange("p a b -> p (a b)")
        if cur_ch[0] == 0:
            dma("pool", wbf[s][:], src.rearrange("(k p) c -> p k c", p=128), w=[("wbf", s)], chan=f"wl{s}")
            dma("sp", wscr[idx], flat, r=[("wbf", s)], w=[("wscr", idx)], chan="wsto")
        else:
            dma("pool", flat, wscr[idx], r=[("wscr", idx)], w=[("wbf", s)], chan=f"wl{s}")
        return out, ("wbf", s)

    def mm(out, lhsT, rhs, start, stop, r, w, **kw):
        op("pe", lambda: nc.tensor.matmul(out, lhsT, rhs, start=start, stop=stop, **kw), r=r, w=w)

    def proj(src, nk, rhs_t, rkey, NT, bank):
        wt, wk = wtile(src, nk, 128)
        outs = []
        for (c0, n, b) in ((0, NTP, bank), (NTP, NT - NTP, bank + 1)):
            if n <= 0:
                continue
            for k in range(nk):
                mm(ps[:, b, 0:n], wt[:, k, :], rhs_t[:, k, c0:c0 + n], k == 0, k == nk - 1, r=[wk, rkey], w=[("ps", b)])
            outs.append((ps[:, b, 0:n], c0, n, ("ps", b)))
        return outs

    def ssq_rstd(tag, blocks_fn, NT, width):
        for i in range(16):
            src, skey = blocks_fn(i)
            q = sq[i % 2]
            op("act", lambda: nc.scalar.activation(out=q[:, 0:NT], in_=src, func=AF.Square), r=[skey], w=[("sq", i % 2)])
            for (c0, n, b) in ((0, NTP, 6), (NTP, NT - NTP, 7)):
                if n > 0:
                    mm(ps[:, b, 0:n], ones_f, q[:, c0:c0 + n], i == 0, i == 15, r=[("sq", i % 2), "cst"], w=[("ps", b)])
        finish_rstd(NT, width)

    def finish_rstd(NT, width, src=None):
        for (c0, n, b) in ((0, NTP, 6), (NTP, NT - NTP, 7)):
            if n > 0 and src is not None:
                op("act", lambda: nc.scalar.activation(out=rstd[:, c0:c0 + n], in_=src[:, c0:c0 + n], func=AF.Sqrt,
                                                       scale=1.0 / width, bias=vec[:, VO["eps"]:VO["eps"] + 1]),
                   r=["ssq_acc", "vec"], w=["rstd"])
            elif n > 0:
                op("act", lambda: nc.scalar.activation(out=rstd[:, c0:c0 + n], in_=ps[:, b, 0:n], func=AF.Sqrt,
                                                       scale=1.0 / width, bias=vec[:, VO["eps"]:VO["eps"] + 1]),
                   r=[("ps", b), "vec"], w=["rstd"])
        op("dve", lambda: nc.vector.reciprocal(out=rstd[:, 0:NT], in_=rstd[:, 0:NT]), r=["rstd"], w=["rstd"])

    VO["eps"] = off

    for ch in range(NCH):
        NT = NTP + (NS if ch == 0 else 0)
        cur_ch[0] = ch
        tile_idx[0] = 0
        p0 = ch * NTP
        last = ch == NCH - 1
        units = [("p", 0, 128), ("p", 128, 128)] + ([("s", NTP, 64)] if ch == 0 else [])
        blocks = [("p", c, 0) for c in range(0, NTP, 32)] + ([("s", NTP, 0), ("s", NTP + 32, 8)] if ch == 0 else [])
        dma("sp", xT[:, :, 0:NTP], xpT[:, :, p0:p0 + NTP], w=["xT"], chan="xin")
        if ch == 0:
            dma("sp", xT[:, :, NTP:NT], xsT[:, :, :], w=["xT"], chan="xin")

        for l in range(2):
            ssq_rstd("n", lambda i: (xT[:, i, 0:NT], "xT"), NT, float(D))
            for k in range(KD):
                op("dve", lambda: nc.vector.scalar_tensor_tensor(out=hT[:, k, 0:NT], in0=xT[:, k, 0:NT],
                                                                 scalar=vcol("norm_w", l * 16 + k), in1=rstd[:, 0:NT],
                                                                 op0=ALU.mult, op1=ALU.mult),
                   r=["xT", "rstd", "vec"], w=["hT"])
            for blk in range(16):
                for (pa, c0, n, pk) in proj(w_in[l][:, blk * 128:(blk + 1) * 128], 16, hT, "hT", NT, (blk % 2) * 2):
                    op("act", lambda: nc.scalar.activation(out=uT[:, blk, c0:c0 + n], in_=pa, func=AF.Copy),
                       r=[pk], w=["uT"])
            if ch == 0 and l == 0:
                dbg("hT", hT[:, :, 0:NT], "hT", BF16)
                dbg("uT", uT[:, :, 0:NT], "uT", BF16)
            load_s5_weights(l)
            ph = ExitStack()
            bu2 = [psb_(ph, f"bu{i}", [128, 64, 2, 32]) for i in range(2)]
            hbf2 = [psb_(ph, f"hbf{i}", [128, 64, 2, 32], BF16) for i in range(2)]
            s5s = psb_(ph, "s5s", [128, 64, 2, NSQ]) if ch == 0 else None
            s5t = [psb_(ph, f"s5t{i}", [128, 64, 2, 8]) for i in range(2)]
            hc = [psb_(ph, f"hc{i}", [128, 64, 2]) for i in range(3 if ch == 0 else 6)]
            if ch == 0:
                for ri in range(2):
                    dma("sp", s5s[:, :, ri, :], s5in[l, ri], w=["s5s"], chan="s5s")
            lr_, li_ = lam[:, l, 0, :], lam[:, l, 1, :]
            za_todo = list(range(16))
            nb = 32
            NHC = 3 if ch == 0 else 6

            def stageA(k):
                kind, c0, q0 = blocks[k]
                bu = bu2[k % 2]
                for qd in range(4):
                    for bl in range(4):
                        blk = qd * 4 + bl
                        for j in range(4):
                            for ri in range(2):
                                o = ps[:, j, bl * 2 * nb + ri * nb: bl * 2 * nb + ri * nb + nb]
                                mm(o, bw_bf[32 * j:32 * j + 32, blk, ri, :], uT[32 * j:32 * j + 32, blk, c0:c0 + nb],
                                   True, True, r=["bw_bf", "uT"], w=[("ps", j)], tile_position=(32 * j, 0))
                    pv = ps[:, 0:4, 0:8 * nb].rearrange("p j (bl r t) -> p j bl r t", bl=4, r=2)
                    bv = bu[:, qd * 16:(qd + 1) * 16, :, :].rearrange("p (bl j) r t -> p j bl r t", j=4)
                    pk = [("ps", j) for j in range(4)]
                    for ri in range(2):
                        op("act", lambda: nc.scalar.activation(out=bv[:, :, :, ri, :], in_=pv[:, :, :, ri, :], func=AF.Copy),
                           r=pk, w=[("bu", k % 2)])

            def stageB(k):
                kind, c0, q0 = blocks[k]
                bu, hbf = bu2[k % 2], hbf2[k % 2]
                kb, kh = ("bu", k % 2), ("hbf", k % 2)
                if kind == "p":
                    shp = [128, 64, 2]
                    lr2 = lam[:, l, 0, :].unsqueeze(2).broadcast_to(shp)
                    li2 = lis[:, l, :, :]
                    tA = s5t[0][:].rearrange("p a b c -> p (a b c)")[:, 0:128].rearrange("p (a b) -> p a b", b=2)
                    tB = s5t[1][:].rearrange("p a b c -> p (a b c)")[:, 0:128].rearrange("p (a b) -> p a b", b=2)
                    for t in range(nb):
                        src_t = s5c[:, l, :, :] if t == 0 else hc[(t - 1) % NHC][:]
                        src_s = s5c[:, l, :, ::-1] if t == 0 else hc[(t - 1) % NHC][:, :, ::-1]
                        skey = "s5c" if t == 0 else ("hc", (t - 1) % NHC)
                        dst_t = s5c[:, l, :, :] if t == nb - 1 else hc[t % NHC][:]
                        dkey = "s5c" if t == nb - 1 else ("hc", t % NHC)
                        op("dve", lambda: nc.vector.tensor_tensor(out=tA, in0=src_t, in1=lr2, op=ALU.mult), r=[skey, "lam"], w=["s5t0"])
                        op("dve", lambda: nc.vector.tensor_tensor(out=tB, in0=src_s, in1=li2, op=ALU.mult), r=[skey, "lam"], w=["s5t1"])
                        op("dve", lambda: nc.vector.tensor_tensor(out=tA, in0=tA, in1=tB, op=ALU.add), r=["s5t0", "s5t1"], w=["s5t0"])
                        op("dve", lambda: nc.vector.tensor_tensor(out=dst_t, in0=tA, in1=bu[:, :, :, t], op=ALU.add), r=["s5t0", kb], w=[dkey])
                        op("act", lambda: nc.scalar.activation(out=hbf[:, :, :, t], in_=dst_t, func=AF.Copy), r=[dkey], w=[kh])
                    if last and c0 == NTP - nb:
                        op("act", lambda: nc.scalar.activation(out=s5o[:], in_=s5c[:, l, :, :].rearrange("p q r -> p r q"), func=AF.Copy),
                           r=["s5c"], w=["s5o"])
                        dma("sp", s5p_out[l].rearrange("r p q -> p r q"), s5o[:], r=["s5o"], chan="so")
                else:
                    shp = [128, 64, 2, 8]
                    lr2 = lam[:, l, 0, :].unsqueeze(2).unsqueeze(3).broadcast_to(shp)
                    li2 = lis[:, l, :, :].unsqueeze(3).broadcast_to(shp)
                    bu5 = bu[:].rearrange("p q r (s t) -> p q r s t", t=4)
                    col = lambda t: bu5[:, :, :, :, t]
                    cols_ = lambda t: bu5[:, :, ::-1, :, t]
                    prev0, prev0s = s5s[:, :, :, q0:q0 + 8], s5s[:, :, ::-1, q0:q0 + 8]
                    tA, tB = s5t[0][:], s5t[1][:]
                    for t in range(4):
                        pr = prev0 if t == 0 else col(t - 1)
                        prs = prev0s if t == 0 else cols_(t - 1)
                        rk = [kb, "s5s", "lam"]
                        op("dve", lambda: nc.vector.tensor_tensor(out=tA, in0=pr, in1=lr2, op=ALU.mult), r=rk, w=["s5t0"])
                        op("dve", lambda: nc.vector.tensor_tensor(out=tB, in0=prs, in1=li2, op=ALU.mult), r=rk, w=["s5t1"])
                        op("dve", lambda: nc.vector.tensor_tensor(out=tA, in0=tA, in1=tB, op=ALU.add), r=["s5t0", "s5t1"], w=["s5t0"])
                        op("dve", lambda: nc.vector.tensor_tensor(out=col(t), in0=col(t), in1=tA, op=ALU.add), r=["s5t0", kb], w=[kb])
                    op("act", lambda: nc.scalar.activation(out=s5s[:, :, :, q0:q0 + 8], in_=col(3), func=AF.Copy), r=[kb], w=["s5s"])
                    if q0 == 8:
                        for ri in range(2):
                            dma("sp", s5s_out[l, ri], s5s[:, :, ri, :], r=["s5s"], chan="so")
                    op("act", lambda: nc.scalar.activation(out=hbf[:], in_=bu[:], func=AF.Copy), r=[kb], w=[kh])

            def stageCmm(k):
                hbf = hbf2[k % 2]
                b = 4 + (k % 2)
                for blk in range(16):
                    for j in range(4):
                        pair = blk * 4 + j
                        for ri in range(2):
                            mm(ps[32 * j:32 * j + 32, b, blk * nb:(blk + 1) * nb], cw_bf[:, pair, ri, :], hbf[:, pair, ri, :], ri == 0, ri == 1,
                               r=["cw_bf", ("hbf", k % 2)], w=[("ps", b)], tile_position=(0, 32 * j), skip_group_check=True)

            def stageCdve(k):
                kind, c0, q0 = blocks[k]
                b = 4 + (k % 2)
                tmpv = s5t[1][:].rearrange("p a b c -> p (a b c)")[:, 0:16 * nb].rearrange("p (a t) -> p a t", t=nb)
                dcols = vec[:, VO["s5_d"] + l * 16: VO["s5_d"] + l * 16 + 16]
                op("dve", lambda: nc.vector.tensor_tensor(out=tmpv, in0=uT[:, :, c0:c0 + nb],
                                                          in1=dcols.unsqueeze(2).broadcast_to([128, 16, nb]), op=ALU.mult),
                   r=["uT", "vec"], w=["s5t1"])
                op("dve", lambda: nc.vector.tensor_tensor(out=gT[:, :, c0:c0 + nb], in0=tmpv,
                                                          in1=ps[:, b, :].rearrange("p (a t) -> p a t", t=nb), op=ALU.add),
                   r=["s5t1", ("ps", b)], w=["gT"])

            nblk = len(blocks)
            stageA(0)
            for k in range(nblk):
                if k + 1 < nblk:
                    stageA(k + 1)
                stageB(k)
                stageCmm(k)
                for _ in range(2):
                    if za_todo:
                        ob = za_todo.pop(0)
                        for (pa, zc0, zn, pk) in proj(w_in[l][:, 2048 + ob * 128: 2048 + (ob + 1) * 128], 16, hT, "hT", NT, 6):
                            op("act", lambda: nc.scalar.activation(out=yT[:, ob, zc0:zc0 + zn], in_=pa, func=AF.Silu), r=[pk], w=["yT"])
                if k >= 1:
                    stageCdve(k - 1)
            stageCdve(nblk - 1)
            for blk in range(16):
                xg = gT[:, blk, 0:NT]
                w2 = t2[:, 0:NT] if blk % 2 == 0 else t1[:, 0:NT]
                wk2 = "t2" if blk % 2 == 0 else "t1"
                op("act", lambda: nc.scalar.activation(out=w2, in_=xg, func=AF.Square), r=["gT"], w=[wk2])
                op("dve", lambda: nc.vector.tensor_scalar(out=w2, in0=w2, scalar1=0.044715, scalar2=1.0, op0=ALU.mult, op1=ALU.add),
                   r=[wk2], w=[wk2])
                op("dve", lambda: nc.vector.tensor_tensor(out=w2, in0=w2, in1=xg, op=ALU.mult), r=[wk2, "gT"], w=[wk2])
                op("act", lambda: nc.scalar.activation(out=w2, in_=w2, func=AF.Sigmoid, scale=1.5957691216057308), r=[wk2], w=[wk2])
                op("dve", lambda: nc.vector.tensor_tensor(out=xg, in0=xg, in1=w2, op=ALU.mult), r=[wk2, "gT"], w=["gT"])
            if ch == 0 and l == 0:
                dbg("gT", gT[:, :, 0:NT], "gT", BF16)
            B.barrier()
            ph.close()
            assert not za_todo
            pend = None

            def flush_ssq():
                ob_, q_ = pend
                for (c0_, n_, b_) in ((0, NTP, 6), (NTP, NT - NTP, 7)):
                    if n_ > 0:
                        mm(ps[:, b_, 0:n_], ones_f, q_[:, c0_:c0_ + n_], ob_ == 0, ob_ == 15, r=[("sq", ob_ % 2), "cst"], w=[("ps", b_)])
            for ob in range(16):
                pouts = proj(glu_w[l][:, ob * 128:(ob + 1) * 128], 16, gT, "gT", NT, (ob % 2) * 2)
                if pend is not None:
                    flush_ssq()
                for (pa, c0, n, pk) in pouts:
                    op("act", lambda: nc.scalar.activation(out=t1[:, c0:c0 + n], in_=pa, func=AF.Sigmoid,
                                                           bias=vcol("glu_b", l * 16 + ob)), r=[pk, "vec"], w=["t1"])
                op("dve", lambda: nc.vector.tensor_tensor(out=t1[:, 0:NT], in0=t1[:, 0:NT], in1=gT[:, ob, 0:NT], op=ALU.mult),
                   r=["t1", "gT"], w=["t1"])
                op("dve", lambda: nc.vector.tensor_tensor(out=t1[:, 0:NT], in0=t1[:, 0:NT], in1=yT[:, ob, 0:NT], op=ALU.mult),
                   r=["t1", "yT"], w=["t1"])
                q = sq[ob % 2]
                op("act", lambda: nc.scalar.activation(out=q[:, 0:NT], in_=t1[:, 0:NT], func=AF.Square), r=["t1"], w=[("sq", ob % 2)])
                pend = (ob, q)
                op("dve", lambda: nc.vector.tensor_scalar(out=yT[:, ob, 0:NT], in0=t1[:, 0:NT], scalar1=vcol("s5_nw", l * 16 + ob),
                                                          scalar2=None, op0=ALU.mult), r=["t1", "vec"], w=["yT"])
            flush_ssq()
            finish_rstd(NT, float(W_A))
            if ch == 0 and l == 0:
                dbg("yaT", yT[:, :, 0:NT], "yT", BF16)
                dbg("rstd_a", rstd[:, 0:NT], "rstd")

            def out_proj(row0):
                for db in range(16):
                    for (pa, c0, n, pk) in proj(w_out[l][row0:row0 + 2048, db * 128:(db + 1) * 128], 16, yT, "yT", NT, (db % 2) * 2):
                        op("dve", lambda: nc.vector.tensor_tensor(out=t2[:, c0:c0 + n], in0=pa, in1=rstd[:, c0:c0 + n], op=ALU.mult),
                           r=[pk, "rstd"], w=["t2"])
                        op("dve", lambda: nc.vector.tensor_tensor(out=xT[:, db, c0:c0 + n], in0=xT[:, db, c0:c0 + n], in1=t2[:, c0:c0 + n],
                                                                  op=ALU.add), r=["t2", "xT"], w=["xT"])
            out_proj(0)
            if ch == 0 and l == 0:
                dbg("x1a", xT[:, :, 0:NT], "xT")

            ph = ExitStack()
            ssq_acc = psb_(ph, "ssq_acc", [128, NTMAX])
            cvs = psb_(ph, "cvs", [128, 32, NSQ, 3])
            raw = psb_(ph, "raw", [128, 3 + NTP])
            raws = psb_(ph, "raws", [128, NSQ, 7])
            dt_sb = psb_(ph, "dt_sb", [128, 32]); da_sb = psb_(ph, "da_sb", [128, 32]); cs_sb = psb_(ph, "cs_sb", [128, 32])
            ncs_sb = psb_(ph, "ncs_sb", [128, 32]); dte_sb = psb_(ph, "dte_sb", [128, 32]); dec_sb = psb_(ph, "dec_sb", [128, 32])
            dtd_sb = psb_(ph, "dtd_sb", [128, 32])
            Ecs_f = psb_(ph, "Ecs", [128, 2048], BF16)
            Lt2 = [psb_(ph, f"Lt{i}", [128, 4, 128], BF16) for i in range(2)]
            MT_f = psb_(ph, "MT", [128, 2048], BF16)
            CpT_f = psb_(ph, "CpT", [128, 2048], BF16)
            xd = psb_(ph, "xd", [128, 2048], BF16); xdp = psb_(ph, "xdp", [128, 2048], BF16)
            Btok = psb_(ph, "Btok", [128, 1024], BF16); Bm = psb_(ph, "Bm", [128, 1024], BF16)
            h0_2 = [psb_(ph, f"h0{i}", [128, 16, 128]) for i in range(2)]
            h0T_2 = [psb_(ph, "h0T", [128, 2048])] * 2
            h0T_bf = psb_(ph, "h0T_bf", [128, 2048], BF16)
            decT = psb_(ph, "decT", [128, NSQ, 16])
            Rq = h0T_2[0][:, 0:512].rearrange("p (a b c) -> p a b c", a=2, b=NSQ)
            for ob in range(16):
                for (pa, c0, n, pk) in proj(w_in[l][:, 8192 + ob * 128: 8192 + (ob + 1) * 128], 16, hT, "hT", NT, (ob % 2) * 2):
                    op("act", lambda: nc.scalar.activation(out=yT[:, ob, c0:c0 + n], in_=pa, func=AF.Silu), r=[pk], w=["yT"])
            wdt = wdt2[:, l, :, :]
            if ch == 0:
                dma("pool", wdt, w_in[l][:, 10240:10272].rearrange("(k p) c -> p k c", p=128), w=["wdt"], chan="wdt")
            if ch == 0:
                dma("sp", cvs[:], conv_in[l], w=["cvs"], chan="cvin")
            for blk in range(32):
                dst_t, dblk = (uT, blk) if blk < 16 else (gT, blk - 16)
                wk_ = lambda k: vcol("conv_w", (l * 4 + k) * 32 + blk)
                bcol = vcol("conv_b", l * 32 + blk)
                for (pa, c0, n, pk) in proj(w_in[l][:, 4096 + blk * 128: 4096 + (blk + 1) * 128], 16, hT, "hT", NT, (blk % 2) * 2):
                    if c0 == 0:
                        op("act", lambda: nc.scalar.activation(out=raw[:, 0:3], in_=convc[:, l, blk, :], func=AF.Copy),
                           r=["convc"], w=["raw"])
                        op("act", lambda: nc.scalar.activation(out=raw[:, 3:3 + NTP], in_=pa, func=AF.Copy), r=[pk], w=["raw"])
                        op("act", lambda: nc.scalar.activation(out=convc[:, l, blk, :], in_=raw[:, NTP:NTP + 3], func=AF.Copy),
                           r=["raw"], w=["convc"])
                        acc = t1[:, 0:NTP]
                        op("dve", lambda: nc.vector.tensor_scalar(out=acc, in0=raw[:, 3:3 + NTP], scalar1=wk_(3), scalar2=bcol,
                                                                  op0=ALU.mult, op1=ALU.add), r=["raw", "vec"], w=["t1"])
                        for k in range(3):
                            op("dve", lambda: nc.vector.scalar_tensor_tensor(out=acc, in0=raw[:, k:k + NTP], scalar=wk_(k), in1=acc,
                                                                             op0=ALU.mult, op1=ALU.add), r=["raw", "vec", "t1"], w=["t1"])
                        op("act", lambda: nc.scalar.activation(out=dst_t[:, dblk, 0:NTP], in_=acc, func=AF.Silu), r=["t1"],
                           w=["uT" if blk < 16 else "gT"])
                    else:
                        op("act", lambda: nc.scalar.activation(out=raws[:, :, 0:3], in_=cvs[:, blk, :, :], func=AF.Copy),
                           r=["cvs"], w=["raws"])
                        op("act", lambda: nc.scalar.activation(out=raws[:, :, 3:7], in_=pa.rearrange("p (s t) -> p s t", t=4), func=AF.Copy),
                           r=[pk], w=["raws"])
                        op("act", lambda: nc.scalar.activation(out=cvs[:, blk, :, :], in_=raws[:, :, 4:7], func=AF.Copy),
                           r=["raws"], w=["cvs"])
                        acc = t2[:, 0:NS].rearrange("p (s t) -> p s t", t=4)
                        op("dve", lambda: nc.vector.tensor_scalar(out=acc, in0=raws[:, :, 3:7], scalar1=wk_(3), scalar2=bcol,
                                                                  op0=ALU.mult, op1=ALU.add), r=["raws", "vec"], w=["t2"])
                        for k in range(3):
                            op("dve", lambda: nc.vector.scalar_tensor_tensor(out=acc, in0=raws[:, :, k:k + 4], scalar=wk_(k), in1=acc,
                                                                             op0=ALU.mult, op1=ALU.add), r=["raws", "vec", "t2"], w=["t2"])
                        op("act", lambda: nc.scalar.activation(out=dst_t[:, dblk, NTP:NT], in_=t2[:, 0:NS], func=AF.Silu), r=["t2"],
                           w=["uT" if blk < 16 else "gT"])
            if ch == 0:
                dma("sp", convs_out[l], cvs[:], r=["cvs"], chan="cvout")
            if last:
                dma("sp", convp_out[l], convc[:, l, :, :], r=["convc"], chan="cvout")
            if ch == 0 and l == 0:
                dbg("xsT", uT[:, :, 0:NT], "uT", BF16)
                dbg("bcT", gT[:, :, 0:NT], "gT", BF16)
                dbg("zsT", yT[:, :, 0:NT], "yT", BF16)
            BTt = lambda g: gT[:, g, :]
            CTt = lambda g: gT[:, 8 + g, :]

            for (kind, c0, TT) in units:
                samp = kind == "s"
                maskT = maskT_s if samp else maskT_p
                EL = EL_s if samp else EL_p
                ngr = negr_b[0:TT, 1 if samp else 0, 0:4 * TT]
                for k in range(16):
                    mm(ps[0:TT, 4, 0:32], hT[:, k, c0:c0 + TT], wdt[:, k, :], k == 0, k == 15, r=["hT", "wdt"], w=[("ps", 4)])
                U = lambda t: t[0:TT, :]
                op("dve", lambda: nc.vector.tensor_tensor(out=U(dt_sb), in0=ps[0:TT, 4, 0:32], in1=bc[0:TT, l, 0, :], op=ALU.add),
                   r=[("ps", 4), "bc"], w=["dt_sb"])
                op("act", lambda: nc.scalar.activation(out=U(dt_sb), in_=U(dt_sb), func=AF.Exp), r=["dt_sb"], w=["dt_sb"])
                op("act", lambda: nc.scalar.activation(out=U(dt_sb), in_=U(dt_sb), func=AF.Ln, bias=vec[0:TT, VO["eps"] + 1:VO["eps"] + 2]),
                   r=["dt_sb", "vec"], w=["dt_sb"])
                op("dve", lambda: nc.vector.tensor_tensor(out=U(da_sb), in0=U(dt_sb), in1=a_bc[0:TT, l, :], op=ALU.mult),
                   r=["dt_sb", "a_bc"], w=["da_sb"])
                mm(ps[0:TT, 4, 32:64], maskT[0:TT, 0:TT], U(da_sb), True, True, r=["cst", "da_sb"], w=[("ps", 4)])
                op("act", lambda: nc.scalar.activation(out=U(cs_sb), in_=ps[0:TT, 4, 32:64], func=AF.Copy), r=[("ps", 4)], w=["cs_sb"])
                op("dve", lambda: nc.vector.tensor_scalar(out=U(ncs_sb), in0=U(cs_sb), scalar1=-1.0, scalar2=None, op0=ALU.mult),
                   r=["cs_sb"], w=["ncs_sb"])
                mm(ps[0:TT, 4, 64:96], EL[0:TT, 0:TT], U(cs_sb), True, True, r=["cst", "cs_sb"], w=[("ps", 4)])
                op("dve", lambda: nc.vector.tensor_tensor(out=U(dte_sb), in0=ps[0:TT, 4, 64:96], in1=U(cs_sb), op=ALU.subtract),
                   r=[("ps", 4), "cs_sb"], w=["dte_sb"])
                op("act", lambda: nc.scalar.activation(out=U(dte_sb), in_=U(dte_sb), func=AF.Exp), r=["dte_sb"], w=["dte_sb"])
                op("dve", lambda: nc.vector.tensor_tensor(out=U(dtd_sb), in0=U(dte_sb), in1=U(dt_sb), op=ALU.mult),
                   r=["dte_sb", "dt_sb"], w=["dtd_sb"])
                if not samp:
                    mm(ps[:, 4, 96:128], EL_p, cs_sb[:, :], True, True, r=["cst", "cs_sb"], w=[("ps", 4)])
                    op("act", lambda: nc.scalar.activation(out=dec_sb[:], in_=ps[:, 4, 96:128], func=AF.Exp), r=[("ps", 4)], w=["dec_sb"])
                psb = ps[:, 5:7, :].rearrange("p a b -> p (a b)").bitcast(BF16)
                for blk in range(16):
                    op("pe", lambda: nc.tensor.transpose(psb[0:TT, blk * 128:(blk + 1) * 128], uT[:, blk, c0:c0 + TT], ident_b[:, :]),
                       r=["uT", "ident_b"], w=[("ps", 5), ("ps", 6)])
                pv3 = psb[0:TT, :].rearrange("p (h d) -> p h d", d=64)
                op("dve", lambda: nc.vector.tensor_tensor(out=xd[0:TT, :].rearrange("p (h d) -> p h d", d=64), in0=pv3,
                                                          in1=U(dt_sb).unsqueeze(2).broadcast_to([TT, 32, 64]), op=ALU.mult),
                   r=[("ps", 5), ("ps", 6), "dt_sb"], w=["xd"])
                op("dve", lambda: nc.vector.tensor_tensor(out=xdp[0:TT, :].rearrange("p (h d) -> p h d", d=64), in0=pv3,
                                                          in1=U(dtd_sb).unsqueeze(2).broadcast_to([TT, 32, 64]), op=ALU.mult),
                   r=[("ps", 5), ("ps", 6), "dtd_sb"], w=["xdp"])
                psb7 = ps[:, 7, :].bitcast(BF16)
                for g in range(8):
                    op("pe", lambda: nc.tensor.transpose(psb7[0:TT, g * 128:(g + 1) * 128], BTt(g)[:, c0:c0 + TT], ident_b[:, :]),
                       r=["gT", "ident_b"], w=[("ps", 7)])
                op("act", lambda: nc.scalar.activation(out=Btok[0:TT, :], in_=psb7[0:TT, :], func=AF.Copy), r=[("ps", 7)], w=["Btok"])
                if not samp:
                    op("act", lambda: nc.scalar.activation(out=ST_bf[:], in_=ST[:, l, :], func=AF.Copy), r=["ST"], w=["ST_bf"])
                ystarted = set()

                def views(g):
                    hs = slice(4 * g, 4 * g + 4)
                    if samp:
                        vw = lambda t_, np_: t_[0:np_, :].rearrange("p (h t) -> p h t", t=64)[:, hs, :]
                    else:
                        o_ = (g % 4) * 512
                        vw = lambda t_, np_: t_[0:np_, o_:o_ + 512].rearrange("p (h t) -> p h t", t=128)
                    return hs, vw(Ecs_f, 128), vw(MT_f, TT), vw(CpT_f, 128)

                def stage1(g):
                    par = g % 2
                    hs, Ec, Mg, Cg = views(g)
                    bA = 2 if par == 0 else 5
                    Ltp = Lt2[par]
                    kl, ke = ("Lt", par), ("Ecs", g % 4)
                    for h4 in range(4):
                        mm(ps[:, bA, h4 * TT:(h4 + 1) * TT], cs_sb[0:TT, 4 * g + h4:4 * g + h4 + 1].broadcast_to([TT, 128]),
                           ident_f[0:TT, 0:TT], h4 == 0, False, r=["cst", "cs_sb"], w=[("ps", bA)], skip_group_check=True)
                    op("act", lambda: nc.scalar.activation(out=Ec, in_=ps[:, bA, 0:4 * TT].rearrange("p (h t) -> p h t", h=4), func=AF.Exp),
                       r=[("ps", bA)], w=[ke])
                    mm(ps[0:TT, bA, 0:4 * TT], ident_b[0:TT, 0:TT], ngr, False, True, r=["ident_b", "negr_b"], w=[("ps", bA)],
                       skip_group_check=True)
                    for h4 in range(4):
                        op("act", lambda: nc.scalar.activation(out=Ltp[0:TT, h4, 0:TT], in_=ps[0:TT, bA, h4 * TT:(h4 + 1) * TT], func=AF.Exp,
                                                               bias=ncs_sb[0:TT, 4 * g + h4:4 * g + h4 + 1]),
                           r=[("ps", bA), "ncs_sb"], w=[kl])
                    bC = 3 if par == 0 else 6
                    mm(ps[0:TT, bC, 0:TT], BTt(g)[:, c0:c0 + TT], CTt(g)[:, c0:c0 + TT], True, True, r=["gT"], w=[("ps", bC)],
                       skip_group_check=True)
                    if not samp:
                        mm(ps[:, bC, 128:384], Btok[0:TT, g * 128:(g + 1) * 128], xdp[0:TT, g * 256:(g + 1) * 256], False, True,
                           r=["Btok", "xdp"], w=[("ps", bC)], skip_group_check=True)

                def stage2(g):
                    par = g % 2
                    hs, Ec, Mg, Cg = views(g)
                    Ltp = Lt2[par]
                    kl, ke, km, kc = ("Lt", par), ("Ecs", g % 4), ("MT", g % 4), ("CpT", g % 4)
                    bC = 3 if par == 0 else 6
                    op("dve", lambda: nc.vector.tensor_tensor(out=Mg, in0=Ltp[0:TT, :, 0:TT],
                                                              in1=ps[0:TT, bC, 0:TT].unsqueeze(1).broadcast_to([TT, 4, TT]), op=ALU.mult),
                       r=[kl, ("ps", bC)], w=[km])
                    op("dve", lambda: nc.vector.tensor_tensor(out=Cg, in0=Ec,
                                                              in1=CTt(g)[:, c0:c0 + TT].unsqueeze(1).broadcast_to([128, 4, TT]), op=ALU.mult),
                       r=[ke, "gT"], w=[kc])
                    for h4 in range(4):
                        h = 4 * g + h4
                        hp, half = h // 2, h % 2
                        if samp:
                            o = ps[:, 0:2, :].rearrange("p a b -> p (a b)")[64 * half:64 * half + 64, hp * 64:(hp + 1) * 64]
                            okey = [("ps", 0), ("ps", 1)]
                            fk = (hp // 8, half)
                            st_flag = fk not in ystarted
                            ystarted.add(fk)
                        else:
                            o = ps[64 * half:64 * half + 64, 0, (hp % 2) * 128:(hp % 2) * 128 + 128]
                            okey = [("ps", 0)]
                            st_flag = True
                        mm(o, xd[0:TT, h * 64:(h + 1) * 64], Mg[:, h4, :], st_flag, False, r=["xd", km], w=okey,
                           tile_position=(0, 64 * half), skip_group_check=True)
                        if not samp:
                            mm(o, ST_bf[:, h * 64:(h + 1) * 64], Cg[:, h4, :], False, True, r=["ST_bf", kc], w=okey,
                               tile_position=(0, 64 * half))
                    if not samp:
                        sv = ST[:, l, g * 256:(g + 1) * 256].rearrange("p (h d) -> p h d", d=64)
                        op("dve", lambda: nc.vector.tensor_tensor(out=sv, in0=sv, in1=dec_sb[:, hs].unsqueeze(2).broadcast_to([128, 4, 64]),
                                                                  op=ALU.mult), r=["ST", "dec_sb", "ST_bf"], w=["ST"])
                        op("dve", lambda: nc.vector.tensor_tensor(out=ST[:, l, g * 256:(g + 1) * 256], in0=ST[:, l, g * 256:(g + 1) * 256],
                                                                  in1=ps[:, bC, 128:384], op=ALU.add), r=["ST", ("ps", bC)], w=["ST"])
                        ssd_tail(nc, op, mm, ps, g, c0, TT, l, uT, yT, t1, t2, sq, vcol, ones_f, ps[:, 0, 0:256], [("ps", 0)], ssq_acc)

                stage1(0)
                for g in range(8):
                    if g + 1 < 8:
                        stage1(g + 1)
                    stage2(g)
                if samp:
                    for h2 in range(2):
                        csv = cs_sb[0:64, :].rearrange("p (hp two) -> p two hp", two=2)[:, h2, :]
                        op("dve", lambda: nc.vector.tensor_tensor(out=Rq[0:64, h2, :, :], in0=csv.unsqueeze(1).broadcast_to([64, NSQ, 16]),
                                                                  in1=sellast[0:64, :].unsqueeze(2).broadcast_to([64, NSQ, 16]), op=ALU.mult),
                           r=["cs_sb", "cst"], w=["h0T"])
                        mm(ps[:, 5, 256 * h2:256 * h2 + 256], ones_f[0:64, :], Rq[0:64, h2, :, :].rearrange("p a b -> p (a b)"),
                           True, True, r=["cst", "h0T"], w=[("ps", 5)])
                    for h2 in range(2):
                        op("act", lambda: nc.scalar.activation(out=decT[64 * h2:64 * h2 + 64, :, :].rearrange("p a b -> p (a b)"),
                                                               in_=ps[64 * h2:64 * h2 + 64, 5, 256 * h2:256 * h2 + 256], func=AF.Exp),
                           r=[("ps", 5)], w=["decT"])
                    yall = ps[:, 0:2, :].rearrange("p a b -> p (a b)")
                    for q in range(NSQ):
                        h0, h0T = h0_2[q % 2], h0T_2[q % 2]
                        kh0, kh0T = ("h0", q % 2), "h0T"
                        dma("sp", h0T[:], ssd_inT[l, q], w=[kh0T], chan="h0T")
                        dma("sp", h0[:], ssd_in[l, q].rearrange("(hp two) d n -> (two d) hp n", two=2), w=[kh0], chan=f"h0{q % 2}")
                        op("act", lambda: nc.scalar.activation(out=h0T_bf[:], in_=h0T[:], func=AF.Copy), r=[kh0T], w=["h0T_bf"])
                        for h in range(32):
                            hp, half = h // 2, h % 2
                            o = yall[64 * half:64 * half + 64, hp * 64 + 4 * q: hp * 64 + 4 * q + 4]
                            mm(o, h0T_bf[:, h * 64:(h + 1) * 64], CpT_f[:, h * 64 + 4 * q: h * 64 + 4 * q + 4], False, q == NSQ - 1,
                               r=["h0T_bf"] + [("CpT", i) for i in range(4)], w=[("ps", 0), ("ps", 1)], tile_position=(0, 64 * half),
                               skip_group_check=True)
                        op("dve", lambda: nc.vector.tensor_scalar(out=Bm[0:64, :], in0=Btok[0:64, :], scalar1=selq[0:64, q:q + 1],
                                                                  scalar2=None, op0=ALU.mult), r=["Btok", "cst"], w=["Bm"])
                        for hp in range(16):
                            g = hp // 2
                            mm(ps[:, 4 + hp // 4, (hp % 4) * 128:(hp % 4) * 128 + 128], xdp[0:64, hp * 128:(hp + 1) * 128],
                               Bm[0:64, g * 128:(g + 1) * 128], True, True, r=["xdp", "Bm"], w=[("ps", 4 + hp // 4)])
                        upk = [("ps", 4), ("ps", 5), ("ps", 6), ("ps", 7)]
                        op("dve", lambda: nc.vector.tensor_tensor(out=h0[:], in0=h0[:],
                                                                  in1=decT[:, q, :].unsqueeze(2).broadcast_to([128, 16, 128]), op=ALU.mult),
                           r=[kh0, "decT"], w=[kh0])
                        op("dve", lambda: nc.vector.tensor_tensor(out=h0[:], in0=h0[:],
                                                                  in1=ps[:, 4:8, :].rearrange("p a (b n) -> p (a b) n", n=128), op=ALU.add),
                           r=[kh0] + upk, w=[kh0])
                        dma("sp", ssds_out[l, q].rearrange("(hp two) d n -> (two d) hp n", two=2), h0[:], r=[kh0], chan=f"h0o{q % 2}")
                    for g in range(8):
                        ssd_tail(nc, op, mm, ps, g, c0, TT, l, uT, yT, t1, t2, sq, vcol, ones_f,
                                 yall[:, g * 128:(g + 1) * 128], [("ps", 0), ("ps", 1)], ssq_acc)
            if last:
                dma("sp", ssdp_out[l], ST[:, l, :], r=["ST"], chan="stout")
            finish_rstd(NT, float(W_A), ssq_acc)
            B.barrier()
            ph.close()
            if ch == 0 and l == 0:
                dbg("ysT", yT[:, :, 0:NT], "yT", BF16)
                dbg("rstd_s", rstd[:, 0:NT], "rstd")
            out_proj(2048)
            if ch == 0 and l == 0:
                dbg("x1", xT[:, :, 0:NT], "xT")

        ssq_rstd("f", lambda i: (xT[:, i, 0:NT], "xT"), NT, float(D))
        for k in range(KD):
            op("dve", lambda: nc.vector.scalar_tensor_tensor(out=xT[:, k, 0:NT], in0=xT[:, k, 0:NT], scalar=vcol("final_w", k),
                                                             in1=rstd[:, 0:NT], op0=ALU.mult, op1=ALU.mult),
               r=["xT", "rstd", "vec"], w=["xT"])
        dma("sp", yT_out[:, :, p0:p0 + NTP], xT[:, :, 0:NTP], r=["xT"], chan="yout")
        if ch == 0:
            dma("sp", yT_out[:, :, SEQ:SEQ + NS], xT[:, :, NTP:NT], r=["xT"], chan="yout")
    B.finish()


def ssd_tail(nc, op, mm, ps, g, c0, TT, l, uT, yT, t1, t2, sq, vcol, ones_f, ypsum, ykeys, ssq_acc):
    for j in range(2):
        hp = 2 * g + j
        yp = ypsum[:, j * TT:(j + 1) * TT]
        ya = t1[:, 0:TT]
        op("dve", lambda: nc.vector.scalar_tensor_tensor(out=ya, in0=uT[:, hp, c0:c0 + TT], scalar=vcol("ssd_d", l * 16 + hp), in1=yp,
                                                         op0=ALU.mult, op1=ALU.add), r=["uT", "vec"] + ykeys, w=["t1"])
        op("dve", lambda: nc.vector.tensor_tensor(out=ya, in0=ya, in1=yT[:, hp, c0:c0 + TT], op=ALU.mult), r=["t1", "yT"], w=["t1"])
        q = sq[hp % 2]
        op("act", lambda: nc.scalar.activation(out=q[:, 0:TT], in_=ya, func=AF.Square), r=["t1"], w=[("sq", hp % 2)])
        mm(ps[:, 7, 0:TT], ones_f, q[:, 0:TT], hp == 0, hp == 15, r=[("sq", hp % 2), "cst"], w=[("ps", 7)])
        if hp == 15:
            op("act", lambda: nc.scalar.activation(out=ssq_acc[:, c0:c0 + TT], in_=ps[:, 7, 0:TT], func=AF.Copy),
               r=[("ps", 7)], w=["ssq_acc"])
        op("dve", lambda: nc.vector.tensor_scalar(out=yT[:, hp, c0:c0 + TT], in0=ya, scalar1=vcol("ssd_nw", l * 16 + hp),
                                                  scalar2=None, op0=ALU.mult), r=["t1", "vec"], w=["yT"])


def _consts():
    cst = np.zeros((128, 672), np.float32)
    cst[:, 0:128] = np.eye(128)
    cst[:, 128:256] = 1.0
    s = np.arange(128)
    cst[:, 256:384] = (s[:, None] <= s[None, :])
    s6 = np.arange(64)
    cst[:64, 384:448] = (s6[:, None] <= s6[None, :]) & (s6[:, None] // 4 == s6[None, :] // 4)
    cst[127, 448:576] = 1.0
    cst[:64, 576:640] = (s6[:, None] == 4 * (s6[None, :] // 4) + 3)
    cst[:64, 640:656] = (s6[:, None] // 4 == np.arange(16)[None, :])
    cst[:64, 656:672] = (s6[:, None] == 4 * np.arange(16)[None, :] + 3)
    neg = np.zeros((128, 2, 512), np.float32)
    mp = np.where(s[:, None] <= s[None, :], 0.0, -30000.0)
    neg[:, 0, :] = np.tile(mp, (1, 4))
    ms = np.where((s6[:, None] <= s6[None, :]) & (s6[:, None] // 4 == s6[None, :] // 4), 0.0, -30000.0)
    neg[:64, 1, 0:256] = np.tile(ms, (1, 4))
    return cst, neg.astype(ml_dtypes.bfloat16)


def kernel(x_prompt, x_sample, state_s5_re, state_s5_im, state_ssd, cache_conv,
           norm_w, w_in, s5_lambda_re, s5_lambda_im, s5_log_step, s5_b_re, s5_b_im,
           s5_c_re, s5_c_im, s5_d, s5_glu_w, s5_glu_b, s5_norm_w,
           conv_w, conv_b, dt_bias, a_log, ssd_d, ssd_norm_w, w_out, final_norm_w):
    f = lambda a: np.ascontiguousarray(np.asarray(a, dtype=np.float32))
    x_prompt, x_sample = f(x_prompt), f(x_sample)
    state_ssd = f(state_ssd)
    cst, neg = _consts()

    def chan(v):
        v = f(v)
        return v.reshape(v.shape[:-1] + (v.shape[-1] // 128, 128))

    cols = []
    cols.append(chan(norm_w).transpose(2, 0, 1).reshape(128, -1))
    cols.append(chan(final_norm_w).T)
    cols.append(chan(s5_d).transpose(2, 0, 1).reshape(128, -1))
    cols.append(chan(s5_glu_b).transpose(2, 0, 1).reshape(128, -1))
    cols.append(chan(s5_norm_w).transpose(2, 0, 1).reshape(128, -1))
    cols.append(chan(conv_w).transpose(3, 0, 1, 2).reshape(128, -1))
    cols.append(chan(conv_b).transpose(2, 0, 1).reshape(128, -1))
    cols.append(chan(ssd_norm_w).transpose(2, 0, 1).reshape(128, -1))
    sd = f(ssd_d)
    sdl = np.repeat(sd.reshape(2, 16, 2).transpose(2, 0, 1)[:, None, :, :], 64, axis=1).reshape(128, 32)
    cols.append(sdl)
    cols.append(np.full((128, 1), EPS, np.float32))
    cols.append(np.ones((128, 1), np.float32))
    vecs = f(np.concatenate(cols, axis=1))
    bc32 = np.broadcast_to(np.stack([f(dt_bias), f(a_log)], axis=1)[None], (128, 2, 2, 32)).copy()
    def gp(v):
        return f(v).reshape(2, 64, 2, 64).transpose(0, 2, 3, 1).reshape(2, 128, 64)
    lst = np.broadcast_to(f(s5_log_step)[:, :, None], (2, 128, 64))
    s5par = f(np.stack([gp(s5_lambda_re), gp(s5_lambda_im), gp(lst)], axis=1))
    bw = np.zeros((2, 4, 2, 16, 16, 2, 2, 64), np.float32)
    for ri, bsrc in enumerate((f(s5_b_re), f(s5_b_im))):
        bb = bsrc.reshape(2, 16, 4, 2, 64, 16)
        for g2 in range(2):
            bw[:, :, g2, :, :, ri, g2, :] = bb[:, :, :, g2, :, :].transpose(0, 2, 4, 1, 3)
    bw = f(bw.reshape(2, 128, 16, 2, 128))
    cw = np.zeros((2, 2, 64, 64, 2, 2, 16), np.float32)
    for ri, csrc in enumerate((f(s5_c_re), f(s5_c_im))):
        cc = csrc.reshape(2, 64, 2, 16, 64)
        for g2 in range(2):
            cw[:, g2, :, :, ri, g2, :] = cc[:, :, g2, :, :].transpose(0, 3, 1, 2)
    cw = f(cw.reshape(2, 128, 64, 2, 32))

    w_in, glu_w, w_out = f(w_in), f(s5_glu_w), f(w_out)
    s5re, s5im = f(state_s5_re), f(state_s5_im)
    cache_conv = f(cache_conv)
    in_maps = []
    for c in range(8):
        b = c % 4
        sl = slice(c * NSQ, (c + 1) * NSQ)
        xpT = f(x_prompt[b].reshape(SEQ, 16, 128).transpose(2, 1, 0))
        xsT = f(x_sample[sl].reshape(NS, 16, 128).transpose(2, 1, 0))
        s5 = np.stack([s5re[:, sl], s5im[:, sl]], axis=1)
        s5 = s5.reshape(2, 2, NSQ, 64, 2, 64).transpose(0, 1, 4, 5, 3, 2).reshape(2, 2, 128, 64, NSQ)
        ssd_c = state_ssd[:, sl]
        ssdT = f(ssd_c.transpose(0, 1, 4, 2, 3).reshape(2, NSQ, 128, 2048))
        cv = cache_conv[:, sl].reshape(2, NSQ, 3, 32, 128).transpose(0, 4, 3, 1, 2)
        in_maps.append({
            "xpT": xpT, "xsT": xsT, "s5in": f(s5), "ssd_in": f(ssd_c), "ssd_inT": ssdT, "conv_in": f(cv),
            "w_in": w_in, "glu_w": glu_w, "w_out": w_out, "vecs": vecs, "bc32": f(bc32), "s5par": s5par,
            "bw": bw, "cw": cw, "cst": cst, "negrep": neg,
        })
    nc = build({"vecs": list(vecs.shape), "cst": list(cst.shape)})
    res = run_bass_kernel_spmd(nc, in_maps, core_ids=list(range(8))).results
    global LAST_RES
    LAST_RES = res

    B4 = 4
    y_prompt = np.zeros((B4, SEQ, D), np.float32)
    y_sample = np.zeros((128, 4, D), np.float32)
    p_re = np.zeros((2, B4, 128, 64), np.float32); p_im = np.zeros_like(p_re)
    p_ssd = np.zeros((2, B4, 32, 64, 128), np.float32)
    p_conv = np.zeros((2, B4, 3, 4096), np.float32)
    s_re = np.zeros((2, 128, 128, 64), np.float32); s_im = np.zeros_like(s_re)
    s_ssd = np.zeros((2, 128, 32, 64, 128), np.float32)
    s_conv = np.zeros((2, 128, 3, 4096), np.float32)
    for c in range(8):
        r = res[c]
        sl = slice(c * NSQ, (c + 1) * NSQ)
        yT = r["yT_out"]
        ytok = yT.transpose(2, 1, 0).reshape(SEQ + NS, D)
        y_sample[sl] = ytok[SEQ:].reshape(NSQ, 4, D)
        s5s = r["s5s_out"].reshape(2, 2, 2, 64, 64, NSQ).transpose(0, 1, 5, 4, 2, 3).reshape(2, 2, NSQ, 128, 64)
        s_re[:, sl], s_im[:, sl] = s5s[:, 0], s5s[:, 1]
        s_ssd[:, sl] = r["ssds_out"]
        s_conv[:, sl] = r["convs_out"].transpose(0, 3, 4, 2, 1).reshape(2, NSQ, 3, 4096)
        if c < 4:
            y_prompt[c] = ytok[:SEQ]
            s5p = r["s5p_out"].reshape(2, 2, 2, 64, 64).transpose(0, 1, 4, 2, 3).reshape(2, 2, 128, 64)
            p_re[:, c], p_im[:, c] = s5p[:, 0], s5p[:, 1]
            p_ssd[:, c] = r["ssdp_out"].reshape(2, 128, 32, 64).transpose(0, 2, 3, 1)
            p_conv[:, c] = r["convp_out"].transpose(0, 3, 2, 1).reshape(2, 3, 4096)
    return (y_prompt, y_sample, p_re, p_im, p_ssd, p_conv, s_re, s_im, s_ssd, s_conv)
```
